# Optimizing a Trainium2 kernel written in Bass

```python
import math
import jax, jax.numpy as jnp
from jax import lax
import numpy as np

D_MODEL = 1024
BATCH = 8
SEQ = 2048
DEPTH = 1

MIX_WIDTH = D_MODEL
M_HEADS = 4
M_HEAD_DIM = (MIX_WIDTH // 2) // M_HEADS
M_WIDTH = M_HEADS * M_HEAD_DIM
CONV_WIDTH = 4
CHUNK = 64
D_HEADS = 4
D_HEAD_DIM = (MIX_WIDTH // 2) // (2 * D_HEADS)
D_WIDTH = D_HEADS * 2 * D_HEAD_DIM
Q_BLOCK = 128
P_HEADS = 8
N_KEYS = 128
N_EXPERTS = N_KEYS * N_KEYS
P_TOPK = 16
P_QUERY_DIM = 256
P_HALF = P_QUERY_DIM // 2
TOKEN_BLOCK = 128

EPS = 1e-6
NEG = -1e30

kernel_name = "hybrid_mlstm_diffattn_peer_block"


def rmsnorm(x, g):
    xf = x.astype(jnp.float32)
    y = xf * lax.rsqrt(jnp.mean(xf * xf, axis=-1, keepdims=True) + EPS)
    return (y * g.astype(jnp.float32)).astype(x.dtype)


def causal_dwconv(u, w, b):
    C = u.shape[-1]
    y = lax.conv_general_dilated(
        u, w[:, None, :].astype(u.dtype), window_strides=(1,),
        padding=[(CONV_WIDTH - 1, 0)],
        dimension_numbers=("NWC", "WIO", "NWC"),
        feature_group_count=C)
    return y + b.astype(u.dtype)


def mlstm_chunkwise(q, k, v, ig, lf):
    Bb, H, S, d = q.shape
    nc = S // CHUNK

    def to_chunks(a):
        a = a.reshape(a.shape[:2] + (nc, CHUNK) + a.shape[3:])
        return jnp.moveaxis(a, 2, 0)

    qc, kc, vc, ic, fc = (to_chunks(a) for a in (q, k, v, ig, lf))
    causal = jnp.tril(jnp.ones((CHUNK, CHUNK), dtype=bool))

    def step(carry, inp):
        C, n, m = carry
        qb, kb, vb, ib, fb = inp
        b = jnp.cumsum(fb, axis=-1)
        logD = jnp.where(causal, b[..., :, None] - b[..., None, :] + ib[..., None, :], -jnp.inf)
        m_t = jnp.maximum(b + m[..., None], jnp.max(logD, axis=-1))
        Dw = jnp.exp(logD - m_t[..., None])
        inter = jnp.exp(b + m[..., None] - m_t)
        sqk = jnp.einsum('bhtd,bhsd->bhts', qb, kb) * Dw
        num = (jnp.einsum('bhts,bhsv->bhtv', sqk, vb)
               + inter[..., None] * jnp.einsum('bhtd,bhdv->bhtv', qb, C))
        den = jnp.sum(sqk, axis=-1) + inter * jnp.einsum('bhtd,bhd->bht', qb, n)
        h = num / jnp.maximum(jnp.abs(den), jnp.exp(-m_t))[..., None]
        bL = b[..., -1]
        logw = bL[..., None] - b + ib
        m_new = jnp.maximum(bL + m, jnp.max(logw, axis=-1))
        w = jnp.exp(logw - m_new[..., None])
        decay = jnp.exp(bL + m - m_new)
        C_new = decay[..., None, None] * C + jnp.einsum('bhs,bhsd,bhsv->bhdv', w, kb, vb)
        n_new = decay[..., None] * n + jnp.einsum('bhs,bhsd->bhd', w, kb)
        return (C_new, n_new, m_new), h

    init = (jnp.zeros((Bb, H, d, d), jnp.float32),
            jnp.zeros((Bb, H, d), jnp.float32),
            jnp.zeros((Bb, H), jnp.float32))
    _, hs = lax.scan(step, init, (qc, kc, vc, ic, fc))
    return jnp.moveaxis(hs, 0, 2).reshape(Bb, H, S, d)


def diff_attention(q, k, v, lam):
    Bb, H, _, S, dh = q.shape
    nb = S // Q_BLOCK
    kpos = jnp.arange(S)
    scale = D_HEAD_DIM ** -0.5

    def block(i):
        qb = lax.dynamic_slice_in_dim(q, i * Q_BLOCK, Q_BLOCK, axis=3)
        s = jnp.einsum('bhptd,bhpsd->bhpts', qb, k) * scale
        qpos = i * Q_BLOCK + jnp.arange(Q_BLOCK)
        mask = kpos[None, :] <= qpos[:, None]
        a = jax.nn.softmax(jnp.where(mask, s, NEG), axis=-1)
        att = a[:, :, 0] - lam * a[:, :, 1]
        return jnp.einsum('bhts,bhsv->bhtv', att, v)

    o = lax.map(block, jnp.arange(nb))
    return jnp.moveaxis(o, 0, 2).reshape(Bb, H, S, v.shape[-1])


def peer(h, w_query, sub_keys, u_tab, v_tab):
    Bb, S, D = h.shape
    xt = h.reshape(-1, TOKEN_BLOCK, D)

    def block(xb):
        q = (xb @ w_query).reshape(TOKEN_BLOCK, P_HEADS, 2, P_HALF)
        s = jnp.einsum('thpk,hpnk->thpn', q, sub_keys).astype(jnp.float32)
        sv, si = lax.top_k(s, P_TOPK)
        cand = (sv[:, :, 0, :, None] + sv[:, :, 1, None, :]).reshape(TOKEN_BLOCK, P_HEADS, P_TOPK * P_TOPK)
        cidx = (si[:, :, 0, :, None] * N_KEYS + si[:, :, 1, None, :]).reshape(TOKEN_BLOCK, P_HEADS, P_TOPK * P_TOPK)
        fv, fpos = lax.top_k(cand, P_TOPK)
        eidx = jnp.take_along_axis(cidx, fpos, axis=-1)
        g = jax.nn.softmax(fv, axis=-1)
        u = u_tab[eidx]
        act = jax.nn.gelu(jnp.einsum('td,thkd->thk', xb, u).astype(jnp.float32), approximate=False)
        return jnp.einsum('thk,thkd->td', (g * act).astype(xb.dtype), v_tab[eidx])

    return lax.map(block, xt).reshape(Bb, S, D)


def setup_inputs(seed: int = 0) -> dict:
    key = jax.random.key(seed)
    ks = jax.random.split(key, 24)
    L, D = DEPTH, D_MODEL
    n_in = 4 * M_WIDTH + 2 * M_HEADS + 3 * D_WIDTH
    nrm = lambda k, shape, s: jax.random.normal(k, shape, jnp.float32) * s
    f_bias = jnp.linspace(3.0, 6.0, M_HEADS, dtype=jnp.float32)
    gate_b = jnp.concatenate([
        nrm(ks[7], (L, M_HEADS), 0.1),
        f_bias[None, :] + nrm(ks[8], (L, M_HEADS), 0.1)], axis=-1)
    return {
        "x": nrm(ks[0], (BATCH, SEQ, D), 1.0),
        "c": nrm(ks[1], (BATCH, D), 1.0),
        "ada_w": nrm(ks[2], (L, D, 6 * D), 0.5 * D ** -0.5),
        "ada_b": nrm(ks[3], (L, 6 * D), 0.02),
        "norm1_g": 1.0 + nrm(ks[4], (L, D), 0.02),
        "w_in": nrm(ks[5], (L, D, n_in), D ** -0.5),
        "conv_w": nrm(ks[6], (L, CONV_WIDTH, 2 * M_WIDTH), CONV_WIDTH ** -0.5),
        "conv_b": nrm(ks[9], (L, 2 * M_WIDTH), 0.02),
        "mlstm_gate_b": gate_b,
        "mlstm_norm_g": 1.0 + nrm(ks[10], (L, M_WIDTH), 0.02),
        "lambda_q1": nrm(ks[11], (L, D_HEAD_DIM), 0.1),
        "lambda_k1": nrm(ks[12], (L, D_HEAD_DIM), 0.1),
        "lambda_q2": nrm(ks[13], (L, D_HEAD_DIM), 0.1),
        "lambda_k2": nrm(ks[14], (L, D_HEAD_DIM), 0.1),
        "diff_norm_g": 1.0 + nrm(ks[15], (L, 2 * D_HEAD_DIM), 0.02),
        "w_out": nrm(ks[16], (L, MIX_WIDTH, D), MIX_WIDTH ** -0.5),
        "norm2_g": 1.0 + nrm(ks[17], (L, D), 0.02),
        "peer_w_query": nrm(ks[18], (L, D, P_HEADS * P_QUERY_DIM), D ** -0.5),
        "peer_sub_keys": nrm(ks[19], (L, P_HEADS, 2, N_KEYS, P_HALF), P_HALF ** -0.5),
        "peer_u": nrm(ks[20], (L, N_EXPERTS, D), D ** -0.5),
        "peer_v": nrm(ks[21], (L, N_EXPERTS, D), 0.1),
        "final_g": 1.0 + nrm(ks[22], (D,), 0.02),
    }


def reference(x, c, ada_w, ada_b, norm1_g, w_in, conv_w, conv_b, mlstm_gate_b,
              mlstm_norm_g, lambda_q1, lambda_k1, lambda_q2, lambda_k2, diff_norm_g,
              w_out, norm2_g, peer_w_query, peer_sub_keys, peer_u, peer_v, final_g):
    Bb, S, D = x.shape
    split_pts = [M_WIDTH, 2 * M_WIDTH, 3 * M_WIDTH, 4 * M_WIDTH,
                 4 * M_WIDTH + M_HEADS, 4 * M_WIDTH + 2 * M_HEADS,
                 4 * M_WIDTH + 2 * M_HEADS + D_WIDTH,
                 4 * M_WIDTH + 2 * M_HEADS + 2 * D_WIDTH]
    for l in range(DEPTH):
        mod = jax.nn.silu(c) @ ada_w[l] + ada_b[l]
        sh1, sc1, g1, sh2, sc2, g2 = jnp.split(mod[:, None, :], 6, axis=-1)

        h = rmsnorm(x, norm1_g[l]) * (1.0 + sc1) + sh1
        p = h @ w_in[l]
        mq, mk, mv, mo, mi, mf, dq, dk, dv = jnp.split(p, split_pts, axis=-1)

        qk = jax.nn.silu(causal_dwconv(jnp.concatenate([mq, mk], axis=-1), conv_w[l], conv_b[l]))
        mq, mk = jnp.split(qk, 2, axis=-1)
        heads = lambda a: a.reshape(Bb, S, M_HEADS, M_HEAD_DIM).transpose(0, 2, 1, 3).astype(jnp.float32)
        q_m = heads(mq)
        k_m = heads(mk) * (M_HEAD_DIM ** -0.5)
        v_m = heads(mv)
        gb = mlstm_gate_b[l].astype(jnp.float32)
        ig = (mi.astype(jnp.float32) + gb[:M_HEADS]).transpose(0, 2, 1)
        lf = jax.nn.log_sigmoid(mf.astype(jnp.float32) + gb[M_HEADS:]).transpose(0, 2, 1)
        hm = mlstm_chunkwise(q_m, k_m, v_m, ig, lf).transpose(0, 2, 1, 3)
        hm = rmsnorm(hm, mlstm_norm_g[l].reshape(M_HEADS, M_HEAD_DIM)).reshape(Bb, S, M_WIDTH)
        hm = (hm * jax.nn.sigmoid(mo.astype(jnp.float32))).astype(x.dtype)

        lam_init = 0.8 - 0.6 * math.exp(-0.3 * l)
        lam = (jnp.exp(jnp.sum(lambda_q1[l].astype(jnp.float32) * lambda_k1[l].astype(jnp.float32)))
               - jnp.exp(jnp.sum(lambda_q2[l].astype(jnp.float32) * lambda_k2[l].astype(jnp.float32)))
               + lam_init)
        qk_heads = lambda a: a.reshape(Bb, S, D_HEADS, 2, D_HEAD_DIM).transpose(0, 2, 3, 1, 4).astype(jnp.float32)
        q_d, k_d = qk_heads(dq), qk_heads(dk)
        v_d = dv.reshape(Bb, S, D_HEADS, 2 * D_HEAD_DIM).transpose(0, 2, 1, 3).astype(jnp.float32)
        od = diff_attention(q_d, k_d, v_d, lam)
        od = rmsnorm(od, diff_norm_g[l]) * (1.0 - lam_init)
        od = od.transpose(0, 2, 1, 3).reshape(Bb, S, D_WIDTH).astype(x.dtype)

        y = jnp.concatenate([hm, od], axis=-1) @ w_out[l]
        x = x + g1 * y

        h2 = rmsnorm(x, norm2_g[l]) * (1.0 + sc2) + sh2
        x = x + g2 * peer(h2, peer_w_query[l], peer_sub_keys[l], peer_u[l], peer_v[l])

    return rmsnorm(x, final_g)
```

```python
import math
from contextlib import ExitStack

import numpy as np
import concourse.bass as bass
import concourse.mybir as mybir
from concourse.bass_utils import run_bass_kernel_spmd

F32 = mybir.dt.float32
BF16 = mybir.dt.bfloat16
U32 = mybir.dt.uint32
I32 = mybir.dt.int32
AF = mybir.ActivationFunctionType
ALU = mybir.AluOpType
AX = mybir.AxisListType

S = 2048
D = 1024
NT = 16
KC = 8
N_IN = 3592
EPS = 1e-6
LAM_INIT = 0.8 - 0.6 * math.exp(0.0)
TAGGED = ("w_in", "w_out", "w_query", "keysT", "peer_uv")


class Fw:
    ENGS = ("sync", "act", "dve", "pool", "pe")

    def __init__(self, nc, stack, n_dma_sems=12, same_engine_sync=True):
        self.nc = nc
        self.e = {"sync": nc.sync, "act": nc.scalar, "dve": nc.vector,
                  "pool": nc.gpsimd, "pe": nc.tensor}
        self.sem = {k: stack.enter_context(nc.semaphore("s_" + k)) for k in self.ENGS}
        self.cnt = {k: 0 for k in self.ENGS}
        self.pending = {k: False for k in self.ENGS}
        self.known = {k: {} for k in self.ENGS}
        self.last_w = {}
        self.readers = {}
        self.same = same_engine_sync
        self.dma_pool = {}
        for q in ("sync", "pool"):
            sems = [stack.enter_context(nc.semaphore("d_%s_%d" % (q, i))) for i in range(n_dma_sems)]
            self.dma_pool[q] = {"sems": sems, "val": [0] * n_dma_sems, "next": 0}
        self.semobj = {}
        for k in self.ENGS:
            self.semobj[("e", k)] = self.sem[k]
        for q, d in self.dma_pool.items():
            for i, s in enumerate(d["sems"]):
                self.semobj[("d", q, i)] = s
        self.n_ins = {k: 0 for k in self.ENGS}

    def _deps(self, reads, writes):
        need = {}

        def add(ev):
            if ev is None:
                return
            k, v = ev
            if need.get(k, 0) < v:
                need[k] = v

        for r in reads:
            add(self.last_w.get(r))
        for w in writes:
            add(self.last_w.get(w))
            for ev in self.readers.get(w, ()):
                add(ev)
        return need

    def _wait(self, eng, need):
        for k, v in need.items():
            if k == ("e", eng) and (eng == "pe" or not self.same):
                continue
            if self.known[eng].get(k, 0) >= v:
                continue
            self.e[eng].wait_ge(self.semobj[k], v)
            self.known[eng][k] = v

    def _record(self, ev, reads, writes):
        for r in reads:
            lst = self.readers.setdefault(r, [])
            lst.append(ev)
            if len(lst) > 64:
                best = {}
                for k, v in lst:
                    if best.get(k, 0) < v:
                        best[k] = v
                self.readers[r] = list(best.items())
        for w in writes:
            self.last_w[w] = ev
            self.readers[w] = []

    def op(self, eng, fn, reads=(), writes=(), signal=True):
        ps_r = [k for k in reads if isinstance(k, str) and k.startswith("ps")]
        if ps_r:
            writes = list(writes) + ps_r
        need = self._deps(reads, writes)
        self._wait(eng, need)
        ins = fn(self.e[eng])
        self.n_ins[eng] += 1
        if signal:
            self.cnt[eng] += 1
            ins.then_inc(self.sem[eng], 1)
            ev = (("e", eng), self.cnt[eng])
            self.pending[eng] = False
        else:
            ev = (("e", eng), self.cnt[eng] + 1)
            self.pending[eng] = True
        self._record(ev, reads, writes)
        return ev

    def dma(self, q, fn, reads=(), writes=()):
        need = self._deps(reads, writes)
        pool = self.dma_pool[q]
        i = pool["next"]
        pool["next"] = (i + 1) % len(pool["sems"])
        key = ("d", q, i)
        if pool["val"][i] > 0 and need.get(key, 0) < pool["val"][i]:
            need[key] = pool["val"][i]
        self._wait(q, need)
        ins = fn(self.e[q])
        pool["val"][i] += 16
        ins.then_inc(pool["sems"][i], 16)
        ev = (key, pool["val"][i])
        self.n_ins[q] += 1
        self._record(ev, reads, writes)
        return ev

    def _all_events(self):
        need = {}
        for k in self.ENGS:
            assert not self.pending[k], "engine %s has an unsignalled tail" % k
            if self.cnt[k] > 0:
                need[("e", k)] = self.cnt[k]
        for q, d in self.dma_pool.items():
            for i, v in enumerate(d["val"]):
                if v > 0:
                    need[("d", q, i)] = v
        return need

    def wait_all(self, eng):
        need = self._all_events()
        for k, v in need.items():
            if k == ("e", eng) and (eng in ("pe", "sync") or not self.same):
                continue
            if self.known[eng].get(k, 0) >= v:
                continue
            self.e[eng].wait_ge(self.semobj[k], v)
            self.known[eng][k] = v

    def barrier(self):
        for eng in self.ENGS:
            self.wait_all(eng)
        self.last_w = {}
        self.readers = {}


class Arena:
    def __init__(self, nc, base=16640, limit=229376 - 256):
        self.nc = nc
        self.top = base
        self.limit = limit
        self.n = 0
        self.peak = 0

    def alloc(self, name, shape, dt):
        sz = {F32: 4, BF16: 2, U32: 4, I32: 4}[dt]
        nb = sz
        for s in shape[1:]:
            nb *= s
        nb = (nb + 63) // 64 * 64
        off = self.top
        self.top += nb
        self.peak = max(self.peak, self.top)
        assert self.top <= self.limit, "SBUF arena overflow at %s: %d" % (name, self.top)
        self.n += 1
        return self.nc.alloc_sbuf_tensor_at("%s_%d" % (name, self.n), list(shape), dt, offset=off)

    def mark(self):
        return self.top

    def release(self, m):
        self.top = m


def build(stage=99, dbg=(), peer_tiles=NT):
    nc = bass.Bass("TRN2", target_bir_lowering=False)
    dram = {}

    def din(name, shape, dt=F32):
        dram[name] = nc.dram_tensor(name, list(shape), dt, kind="ExternalInput").ap()
        return dram[name]

    x_d = din("x", [S, D])
    cT_d = din("cT", [128, KC])
    adaw_d = din("ada_w", [D, 6 * D])
    adab_d = din("ada_bT", [128, 48])
    n1g_d = din("n1gT", [128, KC])
    n2g_d = din("n2gT", [128, KC])
    fg_d = din("final_g", [1, D])
    win_d = din("w_in", [D + 1, N_IN])[0:D, :]
    cw_d = din("conv_wT", [128, 8, 4])
    cb_d = din("conv_bT", [128, 8])
    gb_d = din("gate_b", [4, 2])
    mng_d = din("mnorm_g", [1, 512])
    lamv_d = din("lamv", [1, 256])
    dng_d = din("dnorm_g", [1, 128])
    wout_d = din("w_out", [D + 1, D])[0:D, :]
    wq_d = din("w_query", [D + 1, 2048])[0:D, :]
    keys_d = din("keysT", [129, 16, 128])[0:128]
    uv_d = din("peer_uv", [16385, 2 * D])[0:16384, :]
    ident_d = din("ident", [128, 128])
    tri_d = din("tri", [128, 128])
    sel4_d = din("sel4", [4, 512])
    iota16_d = din("iota16", [1, 16])
    out_d = nc.dram_tensor("out", [S, D], F32, kind="ExternalOutput").ap()
    uvb_d = nc.dram_tensor("uvb", [16384, 2 * D], BF16, kind="Internal").ap()
    dbg_d = {}

    with ExitStack() as st:
        fw = Fw(nc, st)
        ar = Arena(nc)
        op, dma = fw.op, fw.dma

        psF = [st.enter_context(nc.psum_tensor("psF%d" % i, [128, 512], F32)) for i in range(6)]
        psT = [st.enter_context(nc.psum_tensor("psT%d" % i, [128, 1024], BF16)) for i in range(2)]

        def dbg_out(name, tensor_ap, shape, key):
            if name not in dbg:
                return
            d = nc.dram_tensor("dbg_" + name, list(shape), F32, kind="ExternalOutput").ap()
            dbg_d[name] = d
            dma("sync", lambda e: e.dma_start(out=d, in_=tensor_ap), reads=[key])

        ident_f = ar.alloc("ident_f", [128, 128], F32)
        ident_b = ar.alloc("ident_b", [128, 128], BF16)
        tri_f = ar.alloc("tri_f", [128, 128], F32)
        tri_b = ar.alloc("tri_b", [128, 128], BF16)
        ones_f = ar.alloc("ones_f", [128, 128], F32)
        sel4 = ar.alloc("sel4", [4, 512], F32)
        mod_fm = ar.alloc("mod_fm", [128, 48], F32)
        g1_fm = ar.alloc("g1_fm", [128, KC], F32)
        g2_fm = ar.alloc("g2_fm", [128, KC], F32)
        small = ar.alloc("small", [128, 64], F32)
        dma("sync", lambda e: e.dma_start(out=ident_f[:], in_=ident_d), writes=["ident_f"])
        dma("sync", lambda e: e.dma_start(out=tri_f[:], in_=tri_d), writes=["tri_f"])
        dma("sync", lambda e: e.dma_start(out=sel4[:], in_=sel4_d), writes=["sel4"])
        op("dve", lambda e: e.tensor_copy(out=ident_b[:], in_=ident_f[:]), reads=["ident_f"], writes=["ident_b"])
        op("dve", lambda e: e.tensor_copy(out=tri_b[:], in_=tri_f[:]), reads=["tri_f"], writes=["tri_b"])
        op("dve", lambda e: e.memset(ones_f[:], 1.0), writes=["ones_f"])

        m0 = ar.mark()
        cT = ar.alloc("cT", [128, KC], F32)
        sc2 = ar.alloc("silu_c2", [128, KC, 2], F32)
        adab = ar.alloc("adab", [128, 48], F32)
        n1g = ar.alloc("n1g", [128, KC], F32)
        n2g = ar.alloc("n2g", [128, KC], F32)
        awb = [ar.alloc("aw%d" % i, [128, KC, 512], F32) for i in range(2)]
        dma("sync", lambda e: e.dma_start(out=cT[:], in_=cT_d), writes=["cT"])
        dma("sync", lambda e: e.dma_start(out=adab[:], in_=adab_d), writes=["adab"])
        dma("sync", lambda e: e.dma_start(out=n1g[:], in_=n1g_d), writes=["n1g"])
        dma("sync", lambda e: e.dma_start(out=n2g[:], in_=n2g_d), writes=["n2g"])
        for j in range(2):
            op("act", lambda e, j=j: e.activation(out=sc2[:, :, j], in_=cT[:], func=AF.Silu),
               reads=["cT"], writes=["sc2"])
        ps_mod = psF[0]
        for blk in range(12):
            b = blk % 2
            for kc in range(KC):
                dma("sync", lambda e, blk=blk, b=b, kc=kc: e.dma_start(
                    out=awb[b][:, kc, :], in_=adaw_d[kc * 128:(kc + 1) * 128, blk * 512:(blk + 1) * 512]),
                    writes=["aw%d" % b])
            for jj in range(4):
                j = blk * 4 + jj
                for kc in range(KC):
                    op("pe", lambda e, b=b, jj=jj, kc=kc, j=j: e.matmul(
                        ps_mod[:, 2 * j:2 * j + 2], lhsT=awb[b][:, kc, jj * 128:(jj + 1) * 128],
                        rhs=sc2[:, kc, :], start=(kc == 0), stop=(kc == KC - 1)),
                        reads=["aw%d" % b, "sc2"], writes=["ps0"], signal=(kc == KC - 1))
        op("dve", lambda e: e.tensor_tensor(
            out=mod_fm[:], in0=ps_mod[:, 0:96].rearrange("p (j two) -> p j two", two=2)[:, :, 0], in1=adab[:], op=ALU.add),
            reads=["ps0", "adab"], writes=["mod_fm"])
        op("dve", lambda e: e.scalar_tensor_tensor(out=g1_fm[:], in0=mod_fm[:, 8:16], scalar=1.0, in1=n1g[:],
                                                   op0=ALU.add, op1=ALU.mult),
           reads=["mod_fm", "n1g"], writes=["g1_fm"])
        op("dve", lambda e: e.scalar_tensor_tensor(out=g2_fm[:], in0=mod_fm[:, 32:40], scalar=1.0, in1=n2g[:],
                                                   op0=ALU.add, op1=ALU.mult),
           reads=["mod_fm", "n2g"], writes=["g2_fm"])
        dbg_out("mod_fm", mod_fm[:], [128, 48], "mod_fm")
        fw.barrier()
        ar.release(m0)

        bc_n = [0]
        dg = [ar.alloc("dg%d" % i, [128, 128], F32) for i in range(2)]

        def bcast(dst, dkey, src_ap, skey):
            for kc in range(KC):
                k = bc_n[0] % 2
                bc_n[0] += 1
                op("dve", lambda e, k=k, kc=kc: e.tensor_scalar(out=dg[k][:], in0=ident_f[:], scalar1=src_ap[:, kc:kc + 1],
                                                                scalar2=None, op0=ALU.mult),
                   reads=["ident_f", skey], writes=["dg%d" % k])
                bank = 4 + kc // 4
                op("pe", lambda e, k=k, kc=kc, bank=bank: e.matmul(
                    psF[bank][:, (kc % 4) * 128:(kc % 4 + 1) * 128], lhsT=ones_f[:], rhs=dg[k][:], start=True, stop=True),
                    reads=["ones_f", "dg%d" % k], writes=["ps%d" % bank])
            for hb in range(2):
                op("act", lambda e, hb=hb: e.copy(out=dst[:, hb * 512:(hb + 1) * 512], in_=psF[4 + hb][:]),
                   reads=["ps%d" % (4 + hb)], writes=[dkey])

        def rstd_of(ssq_ap, out_ap, n, key_in, key_out):
            op("act", lambda e: e.activation(out=out_ap, in_=ssq_ap, func=AF.Ln, scale=1.0 / n, bias=eps_col(out_ap)),
               reads=[key_in, "epsc"], writes=[key_out])
            op("act", lambda e: e.activation(out=out_ap, in_=out_ap, func=AF.Exp, scale=-0.5),
               reads=[key_out], writes=[key_out])

        epsc = ar.alloc("epsc", [128, 1], F32)
        op("dve", lambda e: e.memset(epsc[:], EPS), writes=["epsc"])

        def eps_col(out_ap):
            return epsc[0:out_ap.shape[0], 0:1]

        regA0 = ar.mark()
        actT = ar.alloc("actT", [128, KC, S], BF16)
        cvb = [ar.alloc("cvb%d" % i, [128, D], BF16) for i in range(2)]
        cv_n = [0]

        def convert_chunks(n):
            for _ in range(2 * n):
                c = cv_n[0]
                if c >= 256:
                    return
                cv_n[0] += 1
                k = c % 2
                r0, hf = (c // 2) * 128, (c % 2) * D
                dma("pool", lambda e, r0=r0, hf=hf, k=k: e.dma_start(out=cvb[k][:], in_=uv_d[r0:r0 + 128, hf:hf + D]), writes=["cvb%d" % k])
                dma("sync", lambda e, r0=r0, hf=hf, k=k: e.dma_start(out=uvb_d[r0:r0 + 128, hf:hf + D], in_=cvb[k][:]), reads=["cvb%d" % k])

        m1 = ar.mark()
        G1b = ar.alloc("G1b", [128, D], F32)
        SH1b = ar.alloc("SH1b", [128, D], F32)
        bcast(G1b, "G1b", g1_fm, "g1_fm")
        bcast(SH1b, "SH1b", mod_fm[:, 0:8], "mod_fm")
        dbg_out("G1b", G1b[:], [128, D], "G1b")
        dbg_out("SH1b", SH1b[:], [128, D], "SH1b")
        xb = [ar.alloc("xb%d" % i, [128, D], F32) for i in range(2)]
        junk = ar.alloc("junk", [128, D], BF16)
        t1 = ar.alloc("t1", [128, D], F32)
        hb16 = [ar.alloc("hb16_%d" % i, [128, D], BF16) for i in range(2)]
        ssq = ar.alloc("ssq", [128, NT], F32)
        rstd = ar.alloc("rstd", [128, NT], F32)
        for i in range(NT):
            b = i % 2
            dma("sync", lambda e, i=i, b=b: e.dma_start(out=xb[b][:], in_=x_d[i * 128:(i + 1) * 128, :]), writes=["xb%d" % b])
            op("act", lambda e, i=i, b=b: e.activation(out=junk[:], in_=xb[b][:], func=AF.Square, accum_out=ssq[:, i:i + 1]),
               reads=["xb%d" % b], writes=["junk", "ssq%d" % i])
            rstd_of(ssq[:, i:i + 1], rstd[:, i:i + 1], D, "ssq%d" % i, "rstd%d" % i)
            op("dve", lambda e, i=i, b=b: e.scalar_tensor_tensor(out=t1[:], in0=xb[b][:], scalar=rstd[:, i:i + 1], in1=G1b[:],
                                                                 op0=ALU.mult, op1=ALU.mult),
               reads=["xb%d" % b, "rstd%d" % i, "G1b"], writes=["t1"])
            op("pool", lambda e, b=b: e.tensor_tensor(out=hb16[b][:], in0=t1[:], in1=SH1b[:], op=ALU.add),
               reads=["t1", "SH1b"], writes=["hb16_%d" % b])
            pt = psT[i % 2]
            for kc in range(KC):
                op("pe", lambda e, kc=kc, b=b, pt=pt: e.transpose(pt[:, kc * 128:(kc + 1) * 128], hb16[b][:, kc * 128:(kc + 1) * 128], ident_b[:]),
                   reads=["hb16_%d" % b, "ident_b"], writes=["psT%d" % (i % 2)])
            op("act", lambda e, i=i, pt=pt: e.copy(out=actT[:, :, i * 128:(i + 1) * 128],
                                                   in_=pt[:].rearrange("p (kc t) -> p kc t", kc=KC)),
               reads=["psT%d" % (i % 2)], writes=["actT"])
        dbg_out("ssq", ssq[:], [128, NT], "ssq15")
        dbg_out("rstd", rstd[:], [128, NT], "rstd15")
        dbg_out("t1", t1[:], [128, D], "t1")
        if "hT" in dbg:
            hdbg = ar.alloc("hdbg", [128, KC, 128], F32)
            op("dve", lambda e: e.tensor_copy(out=hdbg[:], in_=actT[:, :, 128:256]), reads=["actT"], writes=["hdbg"])
            dbg_out("hT", hdbg[:], [128, KC, 128], "hdbg")
        fw.barrier()
        ar.release(m1)
        if stage <= 1:
            fw.wait_all("sync")
            return nc, dbg_d

        o_tok = ar.alloc("o_tok", [128, NT, D], BF16)
        gmn_b = ar.alloc("gmn_b", [128, 512], F32)
        dma("sync", lambda e: e.dma_start(out=gmn_b[:], in_=mng_d[0:1, :].partition_broadcast(128)), writes=["gmn_b"])
        m2 = ar.mark()
        qkT = ar.alloc("qkT", [128, 8, S], BF16)
        vm = ar.alloc("vm", [128, NT, 4, 129], BF16)
        og = ar.alloc("og", [128, NT, 512], BF16)
        gneg = ar.alloc("gneg", [4, S], F32)
        gi = ar.alloc("gi", [4, S], F32)
        gsp = ar.alloc("gsp", [4, S], F32)
        cw = ar.alloc("cw", [128, 8, 4], F32)
        cb = ar.alloc("cb", [128, 8], F32)
        gb = ar.alloc("gb", [4, 2], F32)
        ngbf = ar.alloc("ngbf", [4, 1], F32)
        dma("sync", lambda e: e.dma_start(out=cw[:], in_=cw_d), writes=["cw"])
        dma("sync", lambda e: e.dma_start(out=cb[:], in_=cb_d), writes=["cb"])
        dma("sync", lambda e: e.dma_start(out=gb[:], in_=gb_d), writes=["gb"])
        op("dve", lambda e: e.tensor_scalar(out=ngbf[:], in0=gb[:, 1:2], scalar1=-1.0, scalar2=None, op0=ALU.mult),
           reads=["gb"], writes=["ngbf"])
        op("pool", lambda e: e.memset(vm[:, :, :, 128:129], 1.0), writes=["vm_ones"])
        m2s = ar.mark()
        wbf = [ar.alloc("wbf%d" % i, [128, KC, 520], BF16) for i in range(2)]
        ub = [ar.alloc("ub%d" % i, [128, S + 3], F32) for i in range(2)]
        cacc = ar.alloc("cacc", [128, S], F32)
        for i in range(2):
            op("pool", lambda e, i=i: e.memset(ub[i][:, 0:3], 0.0), writes=["ub%d" % i])
        wn = [0]
        pcn = [0]

        def load_w(src_d, c0, n):
            b = wn[0] % 2
            wn[0] += 1
            for kc in range(KC):
                dma("pool", lambda e, kc=kc: e.dma_start(out=wbf[b][:, kc, 0:n], in_=src_d[kc * 128:(kc + 1) * 128, c0:c0 + n]),
                    writes=["wbf%d" % b])
            return b

        def proj_fm(b, cc, m, evac, k0=0):
            convert_chunks(1)
            for tb in range(4):
                pi = pcn[0] % 4
                pcn[0] += 1
                for kc in range(KC):
                    op("pe", lambda e, kc=kc, tb=tb, pi=pi: e.matmul(
                        psF[pi][0:m, :], lhsT=wbf[b][:, kc, cc:cc + m], rhs=actT[:, kc, tb * 512:(tb + 1) * 512],
                        start=(kc == 0), stop=(kc == KC - 1)),
                        reads=["wbf%d" % b, "actT"], writes=["ps%d" % pi], signal=(kc == KC - 1))
                evac(tb, psF[pi], "ps%d" % pi)

        def proj_tm(b, evac):
            convert_chunks(2)
            for i in range(NT):
                pi = pcn[0] % 4
                pcn[0] += 1
                for kc in range(KC):
                    op("pe", lambda e, kc=kc, i=i, pi=pi: e.matmul(
                        psF[pi][:, :], lhsT=actT[:, kc, i * 128:(i + 1) * 128], rhs=wbf[b][:, kc, 0:512],
                        start=(kc == 0), stop=(kc == KC - 1)),
                        reads=["wbf%d" % b, "actT"], writes=["ps%d" % pi], signal=(kc == KC - 1))
                evac(i, psF[pi], "ps%d" % pi)

        for blk in range(2):
            b = load_w(win_d, blk * 512, 512)
            for cc in range(4):
                ch = blk * 4 + cc
                u = ub[ch % 2]
                ukey = "ub%d" % (ch % 2)
                proj_fm(b, cc * 128, 128, lambda tb, ps, pk, u=u, ukey=ukey: op(
                    "act", lambda e: e.copy(out=u[:, 3 + tb * 512:3 + (tb + 1) * 512], in_=ps[:, :]),
                    reads=[pk], writes=[ukey]))
                op("dve", lambda e, u=u, ch=ch: e.tensor_scalar(out=cacc[:], in0=u[:, 3:3 + S], scalar1=cw[:, ch, 3:4],
                                                               scalar2=None, op0=ALU.mult),
                   reads=[ukey, "cw"], writes=["cacc"])
                for j in (2, 1, 0):
                    op("dve", lambda e, u=u, ch=ch, j=j: e.scalar_tensor_tensor(
                        out=cacc[:], in0=u[:, j:j + S], scalar=cw[:, ch, j:j + 1], in1=cacc[:], op0=ALU.mult, op1=ALU.add),
                        reads=[ukey, "cw", "cacc"], writes=["cacc"])
                op("act", lambda e, ch=ch: e.activation(out=qkT[:, ch, :], in_=cacc[:], func=AF.Silu, bias=cb[:, ch:ch + 1]),
                   reads=["cacc", "cb"], writes=["qkT%d" % ch])
        b = load_w(win_d, 1024, 512)
        proj_tm(b, lambda i, ps, pk: op(
            "act", lambda e: e.copy(out=vm[:, i, :, 0:128], in_=ps[:, :].rearrange("p (h d) -> p h d", h=4)),
            reads=[pk], writes=["vm%d" % i]))
        b = load_w(win_d, 1536, 520)
        proj_tm(b, lambda i, ps, pk: op(
            "act", lambda e: e.activation(out=og[:, i, :], in_=ps[:, :], func=AF.Sigmoid),
            reads=[pk], writes=["og%d" % i]))
        proj_fm(b, 512, 4, lambda tb, ps, pk: op(
            "act", lambda e: e.activation(out=gi[:, tb * 512:(tb + 1) * 512], in_=ps[0:4, :], func=AF.Identity, bias=gb[:, 0:1]),
            reads=[pk, "gb"], writes=["gi"]))
        proj_fm(b, 516, 4, lambda tb, ps, pk: op(
            "act", lambda e: e.activation(out=gsp[:, tb * 512:(tb + 1) * 512], in_=ps[0:4, :], func=AF.Exp, scale=-1.0, bias=ngbf[:, 0:1]),
            reads=[pk, "ngbf"], writes=["gsp"]))
        op("act", lambda e: e.activation(out=gsp[:], in_=gsp[:], func=AF.Ln, scale=1.0, bias=1.0),
           reads=["gsp"], writes=["gsp"])
        if "gi" in dbg:
            dbg_out("gi", gi[:], [4, S], "gi")
            dbg_out("gsp", gsp[:], [4, S], "gsp")
        if "qkT" in dbg:
            qdbg = ar.alloc("qdbg", [128, 2, 512], F32)
            op("dve", lambda e: e.tensor_copy(out=qdbg[:, 0, :], in_=qkT[:, 1, 0:512]), reads=["qkT1"], writes=["qdbg"])
            op("dve", lambda e: e.tensor_copy(out=qdbg[:, 1, :], in_=qkT[:, 6, 1536:2048]), reads=["qkT6"], writes=["qdbg"])
            dbg_out("qkT", qdbg[:], [128, 2, 512], "qdbg")
        fw.barrier()
        ar.release(m2s)
        if stage <= 2:
            fw.wait_all("sync")
            return nc, dbg_d

        ones4 = ar.alloc("ones4", [4, S], F32)
        Bn = ar.alloc("Bn", [4, S], F32)
        a_colT = ar.alloc("a_colT", [128, NT, 4], F32)
        emtT = ar.alloc("emtT", [128, NT, 4], F32)
        negA_b = [ar.alloc("negA_b0", [128, S], F32)] * 2
        Dt = [ar.alloc("Dt%d" % i, [128, 512], F32) for i in range(2)]
        Pt = [ar.alloc("Pt%d" % i, [128, 512], BF16) for i in range(3)]
        fsm = [ar.alloc("fsm%d" % i, [128, 8], F32) for i in range(2)]
        hh = [ar.alloc("hh%d" % i, [128, 128], F32) for i in range(2)]
        o1 = [ar.alloc("o1_%d" % i, [128, 128], F32) for i in range(2)]
        junk128 = ar.alloc("junk128", [128, 128], BF16)
        op("dve", lambda e: e.memset(ones4[:], 1.0), writes=["ones4"])
        op("dve", lambda e: e.tensor_tensor_scan(out=Bn[:], data0=ones4[:], data1=gsp[:], initial=0.0, op0=ALU.mult, op1=ALU.add),
           reads=["ones4", "gsp"], writes=["Bn"])
        op("dve", lambda e: e.tensor_tensor(out=gi[:], in0=gi[:], in1=Bn[:], op=ALU.add), reads=["gi", "Bn"], writes=["gi"])
        op("dve", lambda e: e.tensor_tensor_scan(out=gsp[:], data0=ones4[:], data1=gi[:], initial=0.0, op0=ALU.mult, op1=ALU.max),
           reads=["ones4", "gi"], writes=["gsp"])
        op("dve", lambda e: e.tensor_scalar(out=gneg[:], in0=gsp[:], scalar1=-1.0, scalar2=None, op0=ALU.mult),
           reads=["gsp"], writes=["gneg"])
        op("dve", lambda e: e.tensor_tensor(out=Bn[:], in0=Bn[:], in1=gneg[:], op=ALU.add), reads=["Bn", "gneg"], writes=["Bn"])
        op("act", lambda e: e.activation(out=Bn[:], in_=Bn[:], func=AF.Exp), reads=["Bn"], writes=["Bn"])
        for j in range(NT):
            op("pe", lambda e, j=j: e.matmul(psF[4][:, j * 4:(j + 1) * 4], lhsT=gi[0:4, j * 128:(j + 1) * 128], rhs=ident_f[0:4, 0:4],
                                             start=True, stop=True), reads=["gi", "ident_f"], writes=["ps4"])
            op("pe", lambda e, j=j: e.matmul(psF[5][:, j * 4:(j + 1) * 4], lhsT=Bn[0:4, j * 128:(j + 1) * 128], rhs=ident_f[0:4, 0:4],
                                             start=True, stop=True), reads=["Bn", "ident_f"], writes=["ps5"])
        op("dve", lambda e: e.tensor_scalar(out=a_colT[:], in0=psF[4][:, 0:64].rearrange("p (j h) -> p j h", h=4),
                                            scalar1=float(math.log(128.0 ** -0.5)), scalar2=None, op0=ALU.add),
           reads=["ps4"], writes=["a_colT"])
        op("dve", lambda e: e.tensor_copy(out=emtT[:], in_=psF[5][:, 0:64].rearrange("p (j h) -> p j h", h=4)),
           reads=["ps5"], writes=["emtT"])
        dbg_out("a_colT", a_colT[:], [128, NT, 4], "a_colT")
        dbg_out("emtT", emtT[:], [128, NT, 4], "emtT")

        fin_n = [0]
        if "acc3" in dbg:
            acc3d = ar.alloc("acc3d", [128, 8, 129], F32)
        for h in range(4):
            nb = negA_b[h % 2]
            nbk = "negA_b0"
            for tb in range(4):
                op("pe", lambda e, h=h, tb=tb: e.matmul(psF[4 + tb % 2][:, :], lhsT=sel4[0:4, h * 128:(h + 1) * 128],
                                                        rhs=gneg[0:4, tb * 512:(tb + 1) * 512], start=True, stop=True),
                   reads=["sel4", "gneg"], writes=["ps%d" % (4 + tb % 2)])
                op("act", lambda e, tb=tb, nb=nb: e.copy(out=nb[:, tb * 512:(tb + 1) * 512], in_=psF[4 + tb % 2][:, :]),
                   reads=["ps%d" % (4 + tb % 2)], writes=[nbk])
            qh = qkT[:, h, :]
            kh = qkT[:, 4 + h, :]
            for tb in range(4):
                jmax = 4 * tb + 3
                convert_chunks(3)

                def s_mm(j, h=h, tb=tb, qh=qh, kh=kh):
                    op("pe", lambda e: e.matmul(psF[j % 2][:, :], lhsT=kh[:, j * 128:(j + 1) * 128], rhs=qh[:, tb * 512:(tb + 1) * 512],
                                                start=True, stop=True),
                       reads=["qkT%d" % h, "qkT%d" % (4 + h)], writes=["ps%d" % (j % 2)])

                s_mm(0)
                for j in range(jmax + 1):
                    if j + 1 <= jmax:
                        s_mm(j + 1)
                    d = Dt[j % 2]
                    p = Pt[j % 3]
                    pk = "Pt%d" % (j % 3)
                    op("act", lambda e, j=j, d=d, nb=nb: e.activation(out=d[:], in_=nb[:, tb * 512:(tb + 1) * 512], func=AF.Exp,
                                                                     bias=a_colT[:, j, h:h + 1]),
                       reads=[nbk, "a_colT"], writes=["Dt%d" % (j % 2)])
                    op("dve", lambda e, j=j, d=d, p=p: e.tensor_tensor(out=p[:], in0=psF[j % 2][:, :], in1=d[:], op=ALU.mult),
                       reads=["ps%d" % (j % 2), "Dt%d" % (j % 2)], writes=[pk])
                    if j >= 4 * tb:
                        li = j - 4 * tb
                        op("pool", lambda e, p=p, li=li: e.tensor_tensor(out=p[:, li * 128:(li + 1) * 128], in0=p[:, li * 128:(li + 1) * 128],
                                                                       in1=tri_b[:], op=ALU.mult),
                           reads=[pk, "tri_b"], writes=[pk])
                    for li in range(max(j - 4 * tb, 0), 4):
                        i = 4 * tb + li
                        acc = psF[2 + li // 2][:, (li % 2) * 256:(li % 2) * 256 + 129]
                        op("pe", lambda e, p=p, li=li, j=j, i=i, acc=acc: e.matmul(acc, lhsT=p[:, li * 128:(li + 1) * 128], rhs=vm[:, j, h, :],
                                                                                  start=(j == 0 and li % 2 == 0), stop=(j == i), skip_group_check=True),
                           reads=[pk, "vm%d" % j, "vm_ones"], writes=["ps%d" % (2 + li // 2)])
                for li in range(4):
                    i = 4 * tb + li
                    acc = psF[2 + li // 2][:, (li % 2) * 256:(li % 2) * 256 + 129]
                    k = fin_n[0] % 2
                    fin_n[0] += 1
                    sm, hk, ok_ = fsm[k], hh[k], o1[k]
                    smk, hkk, okk = "fsm%d" % k, "hh%d" % k, "o1_%d" % k
                    if "acc3" in dbg and h == 3 and i < 8:
                        op("dve", lambda e, acc=acc, i=i: e.tensor_copy(out=acc3d[:, i, :], in_=acc), reads=["ps%d" % (2 + li // 2)], writes=["acc3d"])
                    op("act", lambda e, acc=acc, sm=sm: e.activation(out=sm[:, 0:1], in_=acc[:, 128:129], func=AF.Abs),
                       reads=["ps%d" % (2 + li // 2)], writes=[smk])
                    op("dve", lambda e, sm=sm, i=i, h=h: e.tensor_tensor(out=sm[:, 1:2], in0=sm[:, 0:1], in1=emtT[:, i, h:h + 1], op=ALU.max),
                       reads=[smk, "emtT"], writes=[smk])
                    op("dve", lambda e, sm=sm: e.reciprocal(out=sm[:, 2:3], in_=sm[:, 1:2]), reads=[smk], writes=[smk])
                    op("dve", lambda e, sm=sm, acc=acc, hk=hk: e.tensor_scalar(out=hk[:], in0=acc[:, 0:128], scalar1=sm[:, 2:3], scalar2=None,
                                                                               op0=ALU.mult),
                       reads=["ps%d" % (2 + li // 2), smk], writes=[hkk])
                    op("act", lambda e, hk=hk, sm=sm: e.activation(out=junk128[:], in_=hk[:], func=AF.Square, accum_out=sm[:, 3:4]),
                       reads=[hkk], writes=["junk128", smk])
                    rstd_of(sm[:, 3:4], sm[:, 4:5], 128, smk, smk)
                    op("dve", lambda e, hk=hk, sm=sm, ok_=ok_, h=h: e.scalar_tensor_tensor(
                        out=ok_[:], in0=hk[:], scalar=sm[:, 4:5], in1=gmn_b[:, h * 128:(h + 1) * 128], op0=ALU.mult, op1=ALU.mult),
                        reads=[hkk, smk, "gmn_b"], writes=[okk])
                    op("pool", lambda e, ok_=ok_, i=i, h=h: e.tensor_tensor(out=o_tok[:, i, h * 128:(h + 1) * 128], in0=ok_[:],
                                                                          in1=og[:, i, h * 128:(h + 1) * 128], op=ALU.mult),
                       reads=[okk, "og%d" % i], writes=["o_tok%d" % i])
        if "acc3" in dbg:
            dbg_out("acc3", acc3d[:], [128, 8, 129], "acc3d")
        fw.barrier()
        ar.release(m2)
        if "hm" in dbg:
            hmd = ar.alloc("hmd", [128, NT, 512], F32)
            op("dve", lambda e: e.tensor_copy(out=hmd[:], in_=o_tok[:, :, 0:512]), reads=["o_tok%d" % i for i in range(NT)], writes=["hmd"])
            dbg_out("hm", hmd[:], [128, NT, 512], "hmd")
        if stage <= 3:
            fw.wait_all("sync")
            return nc, dbg_d

        m3 = ar.mark()
        dqk = ar.alloc("dqk", [128, 8, S], BF16)
        vd = ar.alloc("vd", [128, NT, 4, 129], BF16)
        gdn_b = ar.alloc("gdn_b", [128, 128], F32)
        lamv = ar.alloc("lamv", [128, 256], F32)
        lsm = ar.alloc("lsm", [128, 8], F32)
        ljunk = ar.alloc("ljunk", [128, 64], F32)
        dma("sync", lambda e: e.dma_start(out=gdn_b[:], in_=dng_d[0:1, :].partition_broadcast(128)), writes=["gdn_b"])
        dma("sync", lambda e: e.dma_start(out=lamv[:], in_=lamv_d[0:1, :].partition_broadcast(128)), writes=["lamv"])
        op("dve", lambda e: e.tensor_scalar(out=gdn_b[:], in0=gdn_b[:], scalar1=float(1.0 - LAM_INIT), scalar2=None, op0=ALU.mult),
           reads=["gdn_b"], writes=["gdn_b"])
        for t in range(2):
            op("dve", lambda e, t=t: e.scalar_tensor_tensor(out=ljunk[:], in0=lamv[:, t * 128:t * 128 + 64], scalar=1.0,
                                                          in1=lamv[:, t * 128 + 64:t * 128 + 128], op0=ALU.mult, op1=ALU.mult,
                                                          accum_out=lsm[:, t:t + 1]),
               reads=["lamv"], writes=["ljunk", "lsm"])
        op("dve", lambda e: e.tensor_copy(out=lsm[:, 6:8], in_=lsm[:, 0:2]), reads=["lsm"], writes=["lsm"])
        op("act", lambda e: e.activation(out=lsm[:, 2:4], in_=lsm[:, 6:8], func=AF.Exp), reads=["lsm"], writes=["lsm"])
        op("dve", lambda e: e.tensor_tensor(out=lsm[:, 4:5], in0=lsm[:, 3:4], in1=lsm[:, 2:3], op=ALU.subtract), reads=["lsm"], writes=["lsm"])
        op("dve", lambda e: e.tensor_scalar(out=lsm[:, 5:6], in0=lsm[:, 4:5], scalar1=float(-LAM_INIT), scalar2=None, op0=ALU.add),
           reads=["lsm"], writes=["lsm"])
        nlam = lsm[:, 5:6]
        op("pool", lambda e: e.memset(vd[:, :, :, 128:129], 1.0), writes=["vd_ones"])
        m3s = ar.mark()
        wbf = [ar.alloc("wbfd%d" % i, [128, KC, 512], BF16) for i in range(2)]
        for blk in range(2):
            b = load_w(win_d, 2056 + blk * 512, 512)
            for cc in range(4):
                ch = blk * 4 + cc
                proj_fm(b, cc * 128, 128, lambda tb, ps, pk, ch=ch: op(
                    "act", lambda e: e.copy(out=dqk[:, ch, tb * 512:(tb + 1) * 512], in_=ps[:, :]),
                    reads=[pk], writes=["dqk%d" % ch]))
        b = load_w(win_d, 3080, 512)
        proj_tm(b, lambda i, ps, pk: op(
            "act", lambda e: e.copy(out=vd[:, i, :, 0:128], in_=ps[:, :].rearrange("p (h d) -> p h d", h=4)),
            reads=[pk], writes=["vd%d" % i]))
        fw.barrier()
        ar.release(m3s)

        Et = [ar.alloc("Et%d" % i, [128, 512], BF16) for i in range(4)]
        fsm = [ar.alloc("dfsm%d" % i, [128, 8], F32) for i in range(2)]
        o0 = [ar.alloc("o0_%d" % i, [128, 128], F32) for i in range(2)]
        odf = [ar.alloc("odf%d" % i, [128, 128], F32) for i in range(2)]
        junk128 = ar.alloc("junk128d", [128, 128], BF16)
        fin_n = [0]
        en = [0]
        for h in range(4):
            for tb in range(4):
                jmax = 4 * tb + 3
                convert_chunks(3)
                steps = [(j, p) for j in range(jmax + 1) for p in range(2)]

                def s_mm(idx, h=h, tb=tb):
                    j, p = steps[idx]
                    op("pe", lambda e: e.matmul(psF[idx % 2][:, :], lhsT=dqk[p * 64:(p + 1) * 64, 4 + h, j * 128:(j + 1) * 128],
                                                rhs=dqk[p * 64:(p + 1) * 64, h, tb * 512:(tb + 1) * 512], start=True, stop=True),
                       reads=["dqk%d" % h, "dqk%d" % (4 + h)], writes=["ps%d" % (idx % 2)])

                s_mm(0)
                for idx, (j, p) in enumerate(steps):
                    if idx + 1 < len(steps):
                        s_mm(idx + 1)
                    ek = en[0] % 4
                    en[0] += 1
                    E = Et[ek]
                    ekey = "Et%d" % ek
                    op("act", lambda e, E=E, idx=idx: e.activation(out=E[:], in_=psF[idx % 2][:, :], func=AF.Exp, scale=0.125),
                       reads=["ps%d" % (idx % 2)], writes=[ekey])
                    if j >= 4 * tb:
                        li = j - 4 * tb
                        op("pool", lambda e, E=E, li=li: e.tensor_tensor(out=E[:, li * 128:(li + 1) * 128], in0=E[:, li * 128:(li + 1) * 128],
                                                                       in1=tri_b[:], op=ALU.mult),
                           reads=[ekey, "tri_b"], writes=[ekey])
                    for li in range(max(j - 4 * tb, 0), 4):
                        i = 4 * tb + li
                        bank = 2 + 2 * p + li // 2
                        acc = psF[bank][:, (li % 2) * 256:(li % 2) * 256 + 129]
                        op("pe", lambda e, E=E, li=li, j=j, i=i, acc=acc: e.matmul(acc, lhsT=E[:, li * 128:(li + 1) * 128], rhs=vd[:, j, h, :],
                                                                                  start=(j == 0 and li % 2 == 0), stop=(j == i), skip_group_check=True),
                           reads=[ekey, "vd%d" % j, "vd_ones"], writes=["ps%d" % bank])
                for li in range(4):
                    i = 4 * tb + li
                    a0 = psF[2 + li // 2][:, (li % 2) * 256:(li % 2) * 256 + 129]
                    a1 = psF[4 + li // 2][:, (li % 2) * 256:(li % 2) * 256 + 129]
                    k = fin_n[0] % 2
                    fin_n[0] += 1
                    sm, o0k, odk = fsm[k], o0[k], odf[k]
                    smk, o0kk, odkk = "dfsm%d" % k, "o0_%d" % k, "odf%d" % k
                    op("dve", lambda e, sm=sm, a0=a0: e.reciprocal(out=sm[:, 0:1], in_=a0[:, 128:129]), reads=["ps%d" % (2 + li // 2)], writes=[smk])
                    op("dve", lambda e, sm=sm, a1=a1: e.reciprocal(out=sm[:, 1:2], in_=a1[:, 128:129]), reads=["ps%d" % (4 + li // 2)], writes=[smk])
                    op("dve", lambda e, sm=sm: e.tensor_tensor(out=sm[:, 2:3], in0=sm[:, 1:2], in1=nlam, op=ALU.mult), reads=[smk, "lsm"], writes=[smk])
                    op("dve", lambda e, sm=sm, a0=a0, o0k=o0k: e.tensor_scalar(out=o0k[:], in0=a0[:, 0:128], scalar1=sm[:, 0:1], scalar2=None, op0=ALU.mult),
                       reads=["ps%d" % (2 + li // 2), smk], writes=[o0kk])
                    op("dve", lambda e, sm=sm, a1=a1, o0k=o0k, odk=odk: e.scalar_tensor_tensor(
                        out=odk[:], in0=a1[:, 0:128], scalar=sm[:, 2:3], in1=o0k[:], op0=ALU.mult, op1=ALU.add),
                        reads=["ps%d" % (4 + li // 2), smk, o0kk], writes=[odkk])
                    op("act", lambda e, odk=odk, sm=sm: e.activation(out=junk128[:], in_=odk[:], func=AF.Square, accum_out=sm[:, 3:4]),
                       reads=[odkk], writes=["junk128d", smk])
                    rstd_of(sm[:, 3:4], sm[:, 4:5], 128, smk, smk)
                    op("dve", lambda e, odk=odk, sm=sm, i=i, h=h: e.scalar_tensor_tensor(
                        out=o_tok[:, i, 512 + h * 128:512 + (h + 1) * 128], in0=odk[:], scalar=sm[:, 4:5], in1=gdn_b[:], op0=ALU.mult, op1=ALU.mult),
                        reads=[odkk, smk, "gdn_b"], writes=["o_tok%d" % i])
        fw.barrier()
        ar.release(m3)
        if "od" in dbg:
            odd = ar.alloc("odd", [128, NT, 512], F32)
            op("dve", lambda e: e.tensor_copy(out=odd[:], in_=o_tok[:, :, 512:1024]), reads=["o_tok%d" % i for i in range(NT)], writes=["odd"])
            dbg_out("od", odd[:], [128, NT, 512], "odd")
        if stage <= 4:
            fw.wait_all("sync")
            return nc, dbg_d

        regA1 = ar.mark()
        x1 = ar.alloc("x1", [128, NT, D], F32)
        peer_top = ar.mark()
        m4 = ar.mark()
        GT1b = ar.alloc("GT1b", [128, D], F32)
        bcast(GT1b, "GT1b", mod_fm[:, 16:24], "mod_fm")
        wo = ar.alloc("wo", [128, KC, D], BF16)
        for hb in range(2):
            for kc in range(KC):
                dma("pool", lambda e, hb=hb, kc=kc: e.dma_start(out=wo[:, kc, hb * 512:(hb + 1) * 512],
                                                               in_=wout_d[kc * 128:(kc + 1) * 128, hb * 512:(hb + 1) * 512]),
                    writes=["wo%d" % hb])
        xb2 = [ar.alloc("xb2_%d" % i, [128, D], F32) for i in range(2)]
        ytmp = [ar.alloc("ytmp%d" % i, [128, 512], F32) for i in range(2)]
        for i in range(NT):
            pt = psT[i % 2]
            for kc in range(KC):
                op("pe", lambda e, kc=kc, i=i, pt=pt: e.transpose(pt[:, kc * 128:(kc + 1) * 128], o_tok[:, i, kc * 128:(kc + 1) * 128], ident_b[:]),
                   reads=["o_tok%d" % i, "ident_b"], writes=["psT%d" % (i % 2)])
            op("act", lambda e, i=i, pt=pt: e.copy(out=actT[:, :, i * 128:(i + 1) * 128], in_=pt[:].rearrange("p (kc t) -> p kc t", kc=KC)),
               reads=["psT%d" % (i % 2)], writes=["actT%d" % i])
        convert_chunks(8)
        for i in range(NT):
            b = i % 2
            convert_chunks(1)
            dma("sync", lambda e, i=i, b=b: e.dma_start(out=xb2[b][:], in_=x_d[i * 128:(i + 1) * 128, :]), writes=["xb2_%d" % b])
            for hb in range(2):
                pi = (2 * i + hb) % 4
                for kc in range(KC):
                    op("pe", lambda e, kc=kc, i=i, hb=hb, pi=pi: e.matmul(psF[pi][:, :], lhsT=actT[:, kc, i * 128:(i + 1) * 128],
                                                                         rhs=wo[:, kc, hb * 512:(hb + 1) * 512], start=(kc == 0), stop=(kc == KC - 1)),
                       reads=["actT%d" % i, "wo%d" % hb], writes=["ps%d" % pi], signal=(kc == KC - 1))
                yt = ytmp[hb]
                op("dve", lambda e, pi=pi, hb=hb, yt=yt: e.tensor_tensor(out=yt[:], in0=psF[pi][:, :], in1=GT1b[:, hb * 512:(hb + 1) * 512], op=ALU.mult),
                   reads=["ps%d" % pi, "GT1b"], writes=["ytmp%d" % hb])
                op("pool", lambda e, i=i, hb=hb, b=b, yt=yt: e.tensor_tensor(out=x1[:, i, hb * 512:(hb + 1) * 512], in0=yt[:],
                                                                           in1=xb2[b][:, hb * 512:(hb + 1) * 512], op=ALU.add),
                   reads=["ytmp%d" % hb, "xb2_%d" % b], writes=["x1_%d" % i])
        convert_chunks(128)
        if "x1" in dbg:
            dbg_out("x1", x1[:], [128, NT, D], "x1_15")
        fw.barrier()
        ar.release(m4)
        if stage <= 5:
            fw.wait_all("sync")
            return nc, dbg_d

        arA = Arena(nc, base=regA0, limit=regA1)
        arA.n = 5000
        wq = arA.alloc("wq", [128, KC, 2048], BF16)
        keysT = arA.alloc("keysT", [128, 16, 128], BF16)
        G2b = arA.alloc("G2b", [128, D], F32)
        SH2b = arA.alloc("SH2b", [128, D], F32)
        GT2b = arA.alloc("GT2b", [128, D], F32)
        FGb = arA.alloc("FGb", [128, D], F32)
        m_ssb = arA.mark()
        s_sb = arA.alloc("s_sb", [128, 16, 128], F32)
        for qb in range(4):
            for kc in range(KC):
                dma("pool", lambda e, qb=qb, kc=kc: e.dma_start(out=wq[:, kc, qb * 512:(qb + 1) * 512],
                                                               in_=wq_d[kc * 128:(kc + 1) * 128, qb * 512:(qb + 1) * 512]),
                    writes=["wq"])
        for c4 in range(4):
            dma("pool", lambda e, c4=c4: e.dma_start(out=keysT[:, c4 * 4:(c4 + 1) * 4, :], in_=keys_d[:, c4 * 4:(c4 + 1) * 4, :]), writes=["keysT"])
        dma("sync", lambda e: e.dma_start(out=FGb[:], in_=fg_d[0:1, :].partition_broadcast(128)), writes=["FGb"])
        bcast(G2b, "G2b", g2_fm, "g2_fm")
        bcast(SH2b, "SH2b", mod_fm[:, 24:32], "mod_fm")
        bcast(GT2b, "GT2b", mod_fm[:, 40:48], "mod_fm")
        iota16 = ar.alloc("iota16", [128, 16], F32)
        thr15 = ar.alloc("thr15", [128, 15], F32)
        dma("sync", lambda e: e.dma_start(out=iota16[:], in_=iota16_d[0:1, :].partition_broadcast(128)), writes=["iota16"])
        op("dve", lambda e: e.tensor_scalar(out=thr15[:], in0=iota16[:, 0:15], scalar1=16.0, scalar2=16.0, op0=ALU.mult, op1=ALU.add),
           reads=["iota16"], writes=["thr15"])
        h2bs = [ar.alloc("h2b0", [128, D], BF16), arA.alloc("h2b1", [128, D], BF16)]
        h2T = ar.alloc("h2T", [128, KC, 128], BF16)
        m_qTb = ar.mark()
        qTb = ar.alloc("qTb", [128, 16, 128], BF16)
        m_wk = ar.mark()
        wk = ar.alloc("wk", [128, 16, 128], F32)
        m_cand = ar.mark()
        cand = ar.alloc("cand", [128, 8, 256], F32)
        ar_pj = Arena(nc, base=m_cand, limit=m_cand + 4096)
        ar_pj.n = 7300
        PJK = ["cand"] + ["cand_%d" % h for h in range(8)]
        pjunk = ar_pj.alloc("pjunk", [128, D], BF16)
        sv = ar.alloc("sv", [128, 16, 16], F32)
        si = ar.alloc("si", [128, 16, 16], U32)
        sif = ar.alloc("sif", [128, 16, 16], F32)
        fv = ar.alloc("fv", [128, 8, 16], F32)
        fp = ar.alloc("fp", [128, 8, 16], U32)
        fpf = ar.alloc("fpf", [128, 128], F32)
        fa = ar.alloc("fa", [128, 128], F32)
        fb = ar.alloc("fb", [128, 128], F32)
        ia = ar.alloc("ia", [128, 128], F32)
        ib = ar.alloc("ib", [128, 128], F32)
        eidx = ar.alloc("eidx", [128, 128], I32)
        gt = ar.alloc("gt", [128, 8, 16], F32)
        zs = ar.alloc("zs", [128, 16], F32)
        psm = ar.alloc("psm", [128, 8], F32)
        pacc = arA.alloc("pacc", [128, D], F32)
        NG = 8
        NS = 3
        op("dve", lambda e: e.memset(eidx[:], 0), writes=["eidx"])
        NG = 10
        gbuf = [ar.alloc("gbuf%d" % i, [128, 2 * D], BF16) for i in range(NG - 1)]
        ar_wk = Arena(nc, base=m_wk, limit=m_wk + 8192)
        ar_wk.n = 7000
        gbuf.append(ar_wk.alloc("gbuf_wk", [128, 2 * D], BF16))
        gkey = ["gbuf%d" % i for i in range(NG - 1)] + ["wk"]
        for nm, mk, cnt in ():
            ar_al = Arena(nc, base=mk, limit=mk + 4096 * cnt)
            ar_al.n = 7100 + len(gbuf)
            for _ in range(cnt):
                gbuf.append(ar_al.alloc("gbuf_" + nm, [128, 2 * D], BF16))
                gkey.append(nm)
        NG = len(gbuf)
        xal = []
        for t in range(NT):
            ar_x = Arena(nc, base=regA1 + t * 4096, limit=regA1 + (t + 1) * 4096)
            ar_x.n = 7400 + t
            xal.append(ar_x.alloc("gx", [128, 2 * D], BF16))
        dgb = [arA.alloc("dgb%d" % i, [128, 128], BF16) for i in range(8)]
        ssm = [arA.alloc("ssm%d" % i, [128, 16], F32) for i in range(NS)]
        gn = [0]
        sv4 = sv[:].rearrange("p (h two) a -> p h two a", two=2)
        sif4 = sif[:].rearrange("p (h two) a -> p h two a", two=2)
        cand4 = cand[:].rearrange("p h (a b) -> p h a b", b=16)
        wk4 = wk[:].rearrange("p c n -> p (c n)").rearrange("p (h a b) -> p h a b", h=8, b=16)
        cmpT = wk[:].rearrange("p c n -> p (c n)")[:, 0:1920].rearrange("p (s m) -> p s m", m=15)
        fa3 = fa[:].rearrange("p (h k) -> p h k", k=16)
        fb3 = fb[:].rearrange("p (h k) -> p h k", k=16)
        ia3 = ia[:].rearrange("p (h k) -> p h k", k=16)
        ib3 = ib[:].rearrange("p (h k) -> p h k", k=16)

        def front(i):
            xi = x1[:, i, :]
            xk = "x1_%d" % i
            h2b = h2bs[i % 2]
            h2bk = "h2b%d" % (i % 2)
            op("act", lambda e, xi=xi: e.activation(out=pjunk[:], in_=xi, func=AF.Square, accum_out=psm[:, 0:1]),
               reads=[xk], writes=PJK + ["psmA"])
            rstd_of(psm[:, 0:1], psm[:, 1:2], D, "psmA", "psmA")
            op("dve", lambda e, xi=xi: e.scalar_tensor_tensor(out=pacc[:], in0=xi, scalar=psm[:, 1:2], in1=G2b[:], op0=ALU.mult, op1=ALU.mult),
               reads=[xk, "psmA", "G2b"], writes=["pacc"])
            op("dve", lambda e: e.tensor_tensor(out=h2b[:], in0=pacc[:], in1=SH2b[:], op=ALU.add), reads=["pacc", "SH2b"], writes=[h2bk])
            pt = psT[0]
            for kc in range(KC):
                op("pe", lambda e, kc=kc, pt=pt: e.transpose(pt[:, kc * 128:(kc + 1) * 128], h2b[:, kc * 128:(kc + 1) * 128], ident_b[:]),
                   reads=[h2bk, "ident_b"], writes=["psT0"])
            op("act", lambda e, pt=pt: e.copy(out=h2T[:], in_=pt[:].rearrange("p (kc t) -> p kc t", kc=KC)),
               reads=["psT0"], writes=["h2T"])
            for cg in range(4):
                bank = 2 + cg % 2
                for cc in range(4):
                    c = cg * 4 + cc
                    for kc in range(KC):
                        op("pe", lambda e, c=c, cc=cc, kc=kc, bank=bank: e.matmul(
                            psF[bank][:, cc * 128:(cc + 1) * 128], lhsT=wq[:, kc, c * 128:(c + 1) * 128], rhs=h2T[:, kc, :],
                            start=(kc == 0), stop=(kc == KC - 1)),
                            reads=["wq", "h2T"], writes=["ps%d" % bank], signal=(kc == KC - 1))
                op("act", lambda e, cg=cg, bank=bank: e.copy(out=qTb[:, cg * 4:(cg + 1) * 4, :],
                                                             in_=psF[bank][:, :].rearrange("p (c t) -> p c t", c=4)),
                   reads=["ps%d" % bank], writes=["qTb%d" % cg])
            for cg in range(4):
                for cc in range(4):
                    c = cg * 4 + cc
                    op("pe", lambda e, c=c, cc=cc, cg=cg: e.matmul(psF[cg % 2][:, cc * 128:(cc + 1) * 128], lhsT=qTb[:, c, :], rhs=keysT[:, c, :],
                                                                 start=True, stop=True),
                       reads=["qTb%d" % cg, "keysT"], writes=["ps%d" % (cg % 2)])
                op("act", lambda e, cg=cg: e.copy(out=s_sb[:, cg * 4:(cg + 1) * 4, :], in_=psF[cg % 2][:, :].rearrange("p (c n) -> p c n", c=4)),
                   reads=["ps%d" % (cg % 2)], writes=["s_sb%d" % cg])
        def topk(i):
            SK = lambda c: "s_sb%d" % (c // 4)
            for c in range(16):
                op("dve", lambda e, c=c: e.max(out=sv[:, c, 0:8], in_=s_sb[:, c, :]), reads=[SK(c)], writes=["sva%d" % c])
            for c in range(16):
                op("dve", lambda e, c=c: e.max_index(out=si[:, c, 0:8], in_max=sv[:, c, 0:8], in_values=s_sb[:, c, :]),
                   reads=[SK(c), "sva%d" % c], writes=["sia%d" % c])
            for c in range(16):
                op("dve", lambda e, c=c: e.match_replace(out=wk[:, c, :], in_to_replace=sv[:, c, 0:8], in_values=s_sb[:, c, :], imm_value=-1e30),
                   reads=[SK(c), "sva%d" % c], writes=["wk"] if c == 0 else ["wk_%d" % c])
            for c in range(16):
                op("dve", lambda e, c=c: e.max(out=sv[:, c, 8:16], in_=wk[:, c, :]), reads=["wk"] if c == 0 else ["wk_%d" % c], writes=["svb%d" % c])
            for c in range(16):
                op("dve", lambda e, c=c: e.max_index(out=si[:, c, 8:16], in_max=sv[:, c, 8:16], in_values=wk[:, c, :]),
                   reads=(["wk"] if c == 0 else ["wk_%d" % c]) + ["svb%d" % c], writes=["sib%d" % c])
            ALLSV = ["sva%d" % c for c in range(16)] + ["svb%d" % c for c in range(16)]
            ALLSI = ["sia%d" % c for c in range(16)] + ["sib%d" % c for c in range(16)]
            ALLWK = ["wk"] + ["wk_%d" % c for c in range(1, 16)]
            op("dve", lambda e: e.tensor_copy(out=sif[:], in_=si[:]), reads=ALLSI, writes=["sif"])
            op("dve", lambda e: e.tensor_tensor(out=cand4, in0=sv4[:, :, 0, :].unsqueeze(3).to_broadcast([128, 8, 16, 16]),
                                                in1=sv4[:, :, 1, :].unsqueeze(2).to_broadcast([128, 8, 16, 16]), op=ALU.add),
               reads=ALLSV, writes=["cand"] + ["cand_%d" % h for h in range(8)])
            for h in range(8):
                op("dve", lambda e, h=h: e.max(out=fv[:, h, 0:8], in_=cand[:, h, :]), reads=["cand"], writes=["fva%d" % h])
            for h in range(8):
                op("dve", lambda e, h=h: e.max_index(out=fp[:, h, 0:8], in_max=fv[:, h, 0:8], in_values=cand[:, h, :]),
                   reads=["cand", "fva%d" % h], writes=["fpa%d" % h])
            for h in range(8):
                op("dve", lambda e, h=h: e.match_replace(out=cand[:, h, :], in_to_replace=fv[:, h, 0:8], in_values=cand[:, h, :], imm_value=-1e30),
                   reads=["cand", "fva%d" % h, "fpa%d" % h], writes=["cand_%d" % h])
            for h in range(8):
                op("dve", lambda e, h=h: e.max(out=fv[:, h, 8:16], in_=cand[:, h, :]), reads=["cand_%d" % h], writes=["fvb%d" % h])
            for h in range(8):
                op("dve", lambda e, h=h: e.max_index(out=fp[:, h, 8:16], in_max=fv[:, h, 8:16], in_values=cand[:, h, :]),
                   reads=["cand_%d" % h, "fvb%d" % h], writes=["fpb%d" % h])
            ALLFV = ["fva%d" % h for h in range(8)] + ["fvb%d" % h for h in range(8)]
            ALLFP = ["fpa%d" % h for h in range(8)] + ["fpb%d" % h for h in range(8)]
            ALLCAND = ["cand"] + ["cand_%d" % h for h in range(8)]
            op("dve", lambda e: e.tensor_copy(out=fpf[:], in_=fp[:].rearrange("p h k -> p (h k)")), reads=ALLFP, writes=["fpf"])
            op("dve", lambda e: e.tensor_tensor(out=cmpT, in0=fpf[:].unsqueeze(2).to_broadcast([128, 128, 15]),
                                                in1=thr15[:].unsqueeze(1).to_broadcast([128, 128, 15]), op=ALU.is_ge),
               reads=["fpf", "thr15"], writes=ALLWK)
            op("dve", lambda e: e.tensor_reduce(out=fa[:], in_=cmpT, axis=AX.X, op=ALU.add), reads=ALLWK, writes=["fa"])
            op("dve", lambda e: e.scalar_tensor_tensor(out=fb[:], in0=fa[:], scalar=-16.0, in1=fpf[:], op0=ALU.mult, op1=ALU.add),
               reads=["fa", "fpf"], writes=["fb"])
            for (fx3, half, dst, dkey) in ((fa3, 0, ia3, "ia"), (fb3, 1, ib3, "ib")):
                op("dve", lambda e, fx3=fx3: e.tensor_tensor(out=wk4, in0=fx3.unsqueeze(3).to_broadcast([128, 8, 16, 16]),
                                                           in1=iota16[:].unsqueeze(1).unsqueeze(1).to_broadcast([128, 8, 16, 16]), op=ALU.is_equal),
                   reads=["fa", "fb", "iota16"], writes=ALLWK)
                op("dve", lambda e, half=half: e.tensor_tensor(out=cand4, in0=wk4,
                                                             in1=sif4[:, :, half, :].unsqueeze(2).to_broadcast([128, 8, 16, 16]), op=ALU.mult),
                   reads=ALLWK + ["sif"], writes=ALLCAND)
                op("dve", lambda e, dst=dst: e.tensor_reduce(out=dst, in_=cand4, axis=AX.X, op=ALU.add), reads=ALLCAND, writes=[dkey])
            op("dve", lambda e: e.scalar_tensor_tensor(out=ia[:], in0=ia[:], scalar=128.0, in1=ib[:], op0=ALU.mult, op1=ALU.add),
               reads=["ia", "ib"], writes=["ia"])
            if "ia_all" in dbg:
                if i == 0:
                    ia_d = nc.dram_tensor("dbg_ia_all", [NT, 128, 128], F32, kind="ExternalOutput").ap()
                    fp_d = nc.dram_tensor("dbg_fp_all", [NT, 128, 128], F32, kind="ExternalOutput").ap()
                    sif_d = nc.dram_tensor("dbg_sif_all", [NT, 128, 256], F32, kind="ExternalOutput").ap()
                dma("sync", lambda e, i=i: e.dma_start(out=ia_d[i], in_=ia[:]), reads=["ia"])
                dma("sync", lambda e, i=i: e.dma_start(out=fp_d[i], in_=fpf[:]), reads=["fpf"])
                dma("sync", lambda e, i=i: e.dma_start(out=sif_d[i], in_=sif[:].rearrange("p c a -> p (c a)")), reads=["sif"])
            op("dve", lambda e: e.tensor_scalar(out=ia[:], in0=ia[:], scalar1=0.0, scalar2=16383.0, op0=ALU.max, op1=ALU.min),
               reads=["ia"], writes=["ia"])
            op("dve", lambda e: e.tensor_copy(out=eidx[:], in_=ia[:]), reads=["ia"], writes=["eidx"])
            op("dve", lambda e: e.tensor_tensor(out=gt[:], in0=fv[:], in1=fv[:, :, 0:1].to_broadcast([128, 8, 16]), op=ALU.subtract),
               reads=ALLFV, writes=["gt"])
            op("act", lambda e: e.activation(out=gt[:], in_=gt[:], func=AF.Exp), reads=["gt"], writes=["gt"])
            op("dve", lambda e: e.tensor_reduce(out=zs[:, 0:8], in_=gt[:], axis=AX.X, op=ALU.add), reads=["gt"], writes=["zs"])
            op("dve", lambda e: e.reciprocal(out=zs[:, 8:16], in_=zs[:, 0:8]), reads=["zs"], writes=["zs"])
            op("dve", lambda e: e.tensor_tensor(out=gt[:], in0=gt[:], in1=zs[:, 8:16].unsqueeze(2).to_broadcast([128, 8, 16]), op=ALU.mult),
               reads=["gt", "zs"], writes=["gt"])
            if i == 0 and "eidx" in dbg:
                dbg_out("eidxf", ia[:], [128, 128], "ia")
                dbg_out("gates", gt[:].rearrange("p h k -> p (h k)"), [128, 128], "gt")
        def loop(i):
            h2b = h2bs[i % 2]
            h2bk = "h2b%d" % (i % 2)
            gflat = gt[:].rearrange("p h k -> p (h k)")

            def alias_keys(kname):
                if kname in ("s_sb", "qTb"):
                    return [kname] + ["%s%d" % (kname, c) for c in range(4)]
                if kname == "wk":
                    return ["wk"] + ["wk_%d" % c for c in range(1, 16)]
                return [kname]

            def stage_b(grp, gl):
                k = grp % NS
                sk2 = "ssm%d" % k
                op("dve", lambda e: e.tensor_tensor(out=ssm[k][:, 12:16], in0=ssm[k][:, 8:12], in1=gflat[:, grp * 4:grp * 4 + 4], op=ALU.mult),
                   reads=[sk2, "gt"], writes=[sk2])
                for j in range(4):
                    slot = grp * 4 + j
                    g = gl[j]
                    d = (grp % 2) * 4 + j
                    dk, gk = "dgb%d" % d, ring_k[g]
                    op("act", lambda e, d=d, j=j: e.activation(out=dgb[d][:], in_=ident_b[:], func=AF.Copy, scale=ssm[k][:, 12 + j:13 + j]),
                       reads=[sk2, "ident_b"], writes=[dk])
                    for hb in range(2):
                        op("pe", lambda e, g=g, d=d, hb=hb, slot=slot: e.matmul(psF[4 + hb][:, :], lhsT=dgb[d][:], rhs=ring_b[g][:, D + hb * 512:D + (hb + 1) * 512],
                                                                              start=(slot == 0), stop=(slot == 127)),
                           reads=[dk] + alias_keys(gk), writes=["ps%d" % (4 + hb)])

            ring_b = gbuf + xal[0:i]
            ring_k = gkey + ["x1_%d" % t for t in range(i)]
            nring = len(ring_b)
            gn[0] = 0

            def gathers(grp):
                gl = []
                for j in range(4):
                    slot = grp * 4 + j
                    g = gn[0] % nring
                    gn[0] += 1
                    gl.append(g)
                    dma("pool", lambda e, g=g, slot=slot: e.indirect_dma_start(
                        out=ring_b[g][:], out_offset=None, in_=uvb_d, in_offset=bass.IndirectOffsetOnAxis(ap=eidx[:, slot:slot + 1], axis=0)),
                        reads=["eidx", "gt"], writes=alias_keys(ring_k[g]))
                return gl

            gls = {}
            for grp in range(32):
                k = grp % NS
                sk2 = "ssm%d" % k
                gls[grp] = gathers(grp)
                if grp >= 1:
                    stage_b(grp - 1, gls[grp - 1])
                gl = gls[grp]
                for j in range(4):
                    g = gl[j]
                    op("dve", lambda e, g=g, k=k, j=j: e.scalar_tensor_tensor(out=pjunk[:], in0=ring_b[g][:, 0:D], scalar=1.0, in1=h2b[:], op0=ALU.mult, op1=ALU.mult,
                                                                            accum_out=ssm[k][:, j:j + 1]),
                       reads=alias_keys(ring_k[g]) + [h2bk], writes=PJK + [sk2])
                op("dve", lambda e, k=k: e.tensor_copy(out=ssm[k][:, 4:8], in_=ssm[k][:, 0:4]), reads=[sk2], writes=[sk2])
                op("act", lambda e, k=k: e.activation(out=ssm[k][:, 8:12], in_=ssm[k][:, 4:8], func=AF.Gelu), reads=[sk2], writes=[sk2])
            stage_b(31, gls[31])
        def epilogue(i):
            xi = x1[:, i, :]
            xk = "x1_%d" % i
            for hb in range(2):
                op("dve", lambda e, hb=hb: e.tensor_tensor(out=pacc[:, hb * 512:(hb + 1) * 512], in0=psF[4 + hb][:, :], in1=GT2b[:, hb * 512:(hb + 1) * 512], op=ALU.mult),
                   reads=["ps%d" % (4 + hb), "GT2b"], writes=["pacc"])
            op("dve", lambda e, xi=xi: e.tensor_tensor(out=pacc[:], in0=pacc[:], in1=xi, op=ALU.add), reads=["pacc", xk], writes=["pacc"])
            op("act", lambda e: e.activation(out=pjunk[:], in_=pacc[:], func=AF.Square, accum_out=psm[:, 2:3]),
               reads=["pacc"], writes=PJK + ["psmB"])
            rstd_of(psm[:, 2:3], psm[:, 3:4], D, "psmB", "psmB")
            op("dve", lambda e: e.scalar_tensor_tensor(out=pacc[:], in0=pacc[:], scalar=psm[:, 3:4], in1=FGb[:], op0=ALU.mult, op1=ALU.mult),
               reads=["pacc", "psmB", "FGb"], writes=["pacc"])
            dma("sync", lambda e, i=i: e.dma_start(out=out_d[i * 128:(i + 1) * 128, :], in_=pacc[:]), reads=["pacc"])

        ntile = 1 if stage == 6 else min(NT, peer_tiles)
        front(0)
        topk(0)
        for i in range(ntile):
            if i + 1 < ntile:
                front(i + 1)
            loop(i)
            epilogue(i)
            if i + 1 < ntile:
                topk(i + 1)
        fw.wait_all("sync")
    return nc, dbg_d


def _tag(a, b):
    pad = np.full((1,) + a.shape[1:], float(b), dtype=a.dtype)
    return np.ascontiguousarray(np.concatenate([a, pad], axis=0))


def _prep_inputs(inp, b):
    f = lambda a: np.ascontiguousarray(a, dtype=np.float32)
    sel4 = np.zeros((4, 512), np.float32)
    for h in range(4):
        sel4[h, h * 128:(h + 1) * 128] = 1.0
    m = {
        "x": f(inp["x"][b]),
        "cT": f(inp["c"][b].reshape(KC, 128).T),
        "ada_w": f(inp["ada_w"][0]),
        "ada_bT": f(inp["ada_b"][0].reshape(48, 128).T),
        "n1gT": f(inp["norm1_g"][0].reshape(KC, 128).T),
        "n2gT": f(inp["norm2_g"][0].reshape(KC, 128).T),
        "final_g": f(inp["final_g"].reshape(1, D)),
        "w_in": _tag(f(inp["w_in"][0]), b),
        "conv_wT": f(inp["conv_w"][0].reshape(4, 8, 128).transpose(2, 1, 0)),
        "conv_bT": f(inp["conv_b"][0].reshape(8, 128).T),
        "gate_b": f(inp["mlstm_gate_b"][0].reshape(2, 4).T),
        "mnorm_g": f(inp["mlstm_norm_g"][0].reshape(1, 512)),
        "lamv": f(np.concatenate([inp["lambda_q1"][0], inp["lambda_k1"][0], inp["lambda_q2"][0], inp["lambda_k2"][0]]).reshape(1, 256)),
        "dnorm_g": f(inp["diff_norm_g"][0].reshape(1, 128)),
        "w_out": _tag(f(inp["w_out"][0]), b),
        "w_query": _tag(f(inp["peer_w_query"][0]), b),
        "keysT": _tag(f(inp["peer_sub_keys"][0].reshape(16, 128, 128).transpose(2, 0, 1)), b),
        "peer_uv": _tag(np.concatenate([f(inp["peer_u"][0]), f(inp["peer_v"][0])], axis=1), b),
        "ident": np.eye(128, dtype=np.float32),
        "tri": np.triu(np.ones((128, 128), np.float32)),
        "sel4": sel4,
        "iota16": np.arange(16, dtype=np.float32).reshape(1, 16),
    }
    return m


def kernel(**inputs):
    nc, _ = build()
    shared = None
    in_maps = []
    for b in range(8):
        m = _prep_inputs(inputs, b)
        if shared is None:
            shared = m
        else:
            for k in m:
                if k not in ("x", "cT") and k not in TAGGED:
                    m[k] = shared[k]
        in_maps.append(m)
    res = run_bass_kernel_spmd(nc, in_maps, core_ids=list(range(8)))
    out = np.stack([np.asarray(r["out"], dtype=np.float32) for r in res.results], axis=0)
    return out.reshape(8, S, D)
```

```python
import math
from contextlib import ExitStack

import numpy as np
import concourse.bass as bass
import concourse.mybir as mybir
from concourse.bass_utils import run_bass_kernel_spmd

F32 = mybir.dt.float32
BF16 = mybir.dt.bfloat16
U32 = mybir.dt.uint32
I32 = mybir.dt.int32
AF = mybir.ActivationFunctionType
ALU = mybir.AluOpType
AX = mybir.AxisListType

S = 2048
D = 1024
NT = 16
KC = 8
N_IN = 3592
EPS = 1e-6
LAM_INIT = 0.8 - 0.6 * math.exp(0.0)
TAGGED = ("w_in", "w_out", "w_query", "keysT", "peer_uv")


class Fw:
    ENGS = ("sync", "act", "dve", "pool", "pe")

    def __init__(self, nc, stack, n_dma_sems=12, same_engine_sync=True):
        self.nc = nc
        self.e = {"sync": nc.sync, "act": nc.scalar, "dve": nc.vector,
                  "pool": nc.gpsimd, "pe": nc.tensor}
        self.sem = {k: stack.enter_context(nc.semaphore("s_" + k)) for k in self.ENGS}
        self.cnt = {k: 0 for k in self.ENGS}
        self.pending = {k: False for k in self.ENGS}
        self.known = {k: {} for k in self.ENGS}
        self.last_w = {}
        self.readers = {}
        self.same = same_engine_sync
        self.dma_pool = {}
        for q in ("sync", "pool"):
            sems = [stack.enter_context(nc.semaphore("d_%s_%d" % (q, i))) for i in range(n_dma_sems)]
            self.dma_pool[q] = {"sems": sems, "val": [0] * n_dma_sems, "next": 0}
        self.semobj = {}
        for k in self.ENGS:
            self.semobj[("e", k)] = self.sem[k]
        for q, d in self.dma_pool.items():
            for i, s in enumerate(d["sems"]):
                self.semobj[("d", q, i)] = s
        self.n_ins = {k: 0 for k in self.ENGS}

    def _deps(self, reads, writes):
        need = {}

        def add(ev):
            if ev is None:
                return
            k, v = ev
            if need.get(k, 0) < v:
                need[k] = v

        for r in reads:
            add(self.last_w.get(r))
        for w in writes:
            add(self.last_w.get(w))
            for ev in self.readers.get(w, ()):
                add(ev)
        return need

    def _wait(self, eng, need):
        for k, v in need.items():
            if k == ("e", eng) and (eng == "pe" or not self.same):
                continue
            if self.known[eng].get(k, 0) >= v:
                continue
            self.e[eng].wait_ge(self.semobj[k], v)
            self.known[eng][k] = v

    def _record(self, ev, reads, writes):
        for r in reads:
            lst = self.readers.setdefault(r, [])
            lst.append(ev)
            if len(lst) > 64:
                best = {}
                for k, v in lst:
                    if best.get(k, 0) < v:
                        best[k] = v
                self.readers[r] = list(best.items())
        for w in writes:
            self.last_w[w] = ev
            self.readers[w] = []

    def op(self, eng, fn, reads=(), writes=(), signal=True):
        ps_r = [k for k in reads if isinstance(k, str) and k.startswith("ps")]
        if ps_r:
            writes = list(writes) + ps_r
        need = self._deps(reads, writes)
        self._wait(eng, need)
        ins = fn(self.e[eng])
        self.n_ins[eng] += 1
        if signal:
            self.cnt[eng] += 1
            ins.then_inc(self.sem[eng], 1)
            ev = (("e", eng), self.cnt[eng])
            self.pending[eng] = False
        else:
            ev = (("e", eng), self.cnt[eng] + 1)
            self.pending[eng] = True
        self._record(ev, reads, writes)
        return ev

    def dma(self, q, fn, reads=(), writes=()):
        need = self._deps(reads, writes)
        pool = self.dma_pool[q]
        i = pool["next"]
        pool["next"] = (i + 1) % len(pool["sems"])
        key = ("d", q, i)
        if pool["val"][i] > 0 and need.get(key, 0) < pool["val"][i]:
            need[key] = pool["val"][i]
        self._wait(q, need)
        ins = fn(self.e[q])
        pool["val"][i] += 16
        ins.then_inc(pool["sems"][i], 16)
        ev = (key, pool["val"][i])
        self.n_ins[q] += 1
        self._record(ev, reads, writes)
        return ev

    def _all_events(self):
        need = {}
        for k in self.ENGS:
            assert not self.pending[k], "engine %s has an unsignalled tail" % k
            if self.cnt[k] > 0:
                need[("e", k)] = self.cnt[k]
        for q, d in self.dma_pool.items():
            for i, v in enumerate(d["val"]):
                if v > 0:
                    need[("d", q, i)] = v
        return need

    def wait_all(self, eng):
        need = self._all_events()
        for k, v in need.items():
            if k == ("e", eng) and (eng in ("pe", "sync") or not self.same):
                continue
            if self.known[eng].get(k, 0) >= v:
                continue
            self.e[eng].wait_ge(self.semobj[k], v)
            self.known[eng][k] = v

    def barrier(self):
        for eng in self.ENGS:
            self.wait_all(eng)
        self.last_w = {}
        self.readers = {}


class Arena:
    def __init__(self, nc, base=16640, limit=229376 - 256):
        self.nc = nc
        self.top = base
        self.limit = limit
        self.n = 0
        self.peak = 0

    def alloc(self, name, shape, dt):
        sz = {F32: 4, BF16: 2, U32: 4, I32: 4}[dt]
        nb = sz
        for s in shape[1:]:
            nb *= s
        nb = (nb + 63) // 64 * 64
        off = self.top
        self.top += nb
        self.peak = max(self.peak, self.top)
        assert self.top <= self.limit, "SBUF arena overflow at %s: %d" % (name, self.top)
        self.n += 1
        return self.nc.alloc_sbuf_tensor_at("%s_%d" % (name, self.n), list(shape), dt, offset=off)

    def mark(self):
        return self.top

    def release(self, m):
        self.top = m


def build(stage=99, dbg=(), peer_tiles=NT):
    nc = bass.Bass("TRN2", target_bir_lowering=False)
    dram = {}

    def din(name, shape, dt=F32):
        dram[name] = nc.dram_tensor(name, list(shape), dt, kind="ExternalInput").ap()
        return dram[name]

    x_d = din("x", [S, D])
    cT_d = din("cT", [128, KC])
    adaw_d = din("ada_w", [D, 6 * D])
    adab_d = din("ada_bT", [128, 48])
    n1g_d = din("n1gT", [128, KC])
    n2g_d = din("n2gT", [128, KC])
    fg_d = din("final_g", [1, D])
    win_d = din("w_in", [D + 1, N_IN])[0:D, :]
    cw_d = din("conv_wT", [128, 8, 4])
    cb_d = din("conv_bT", [128, 8])
    gb_d = din("gate_b", [4, 2])
    mng_d = din("mnorm_g", [1, 512])
    lamv_d = din("lamv", [1, 256])
    dng_d = din("dnorm_g", [1, 128])
    wout_d = din("w_out", [D + 1, D])[0:D, :]
    wq_d = din("w_query", [D + 1, 2048])[0:D, :]
    keys_d = din("keysT", [129, 16, 128])[0:128]
    uv_d = din("peer_uv", [16385, 2 * D])[0:16384, :]
    ident_d = din("ident", [128, 128])
    tri_d = din("tri", [128, 128])
    sel4_d = din("sel4", [4, 512])
    iota16_d = din("iota16", [1, 16])
    out_d = nc.dram_tensor("out", [S, D], F32, kind="ExternalOutput").ap()
    uvb_d = nc.dram_tensor("uvb", [16384, 2 * D], BF16, kind="Internal").ap()
    dbg_d = {}

    with ExitStack() as st:
        fw = Fw(nc, st)
        ar = Arena(nc)
        op, dma = fw.op, fw.dma

        psF = [st.enter_context(nc.psum_tensor("psF%d" % i, [128, 512], F32)) for i in range(6)]
        psT = [st.enter_context(nc.psum_tensor("psT%d" % i, [128, 1024], BF16)) for i in range(2)]

        def dbg_out(name, tensor_ap, shape, key):
            if name not in dbg:
                return
            d = nc.dram_tensor("dbg_" + name, list(shape), F32, kind="ExternalOutput").ap()
            dbg_d[name] = d
            dma("sync", lambda e: e.dma_start(out=d, in_=tensor_ap), reads=[key])

        ident_f = ar.alloc("ident_f", [128, 128], F32)
        ident_b = ar.alloc("ident_b", [128, 128], BF16)
        tri_f = ar.alloc("tri_f", [128, 128], F32)
        tri_b = ar.alloc("tri_b", [128, 128], BF16)
        ones_f = ar.alloc("ones_f", [128, 128], F32)
        sel4 = ar.alloc("sel4", [4, 512], F32)
        mod_fm = ar.alloc("mod_fm", [128, 48], F32)
        g1_fm = ar.alloc("g1_fm", [128, KC], F32)
        g2_fm = ar.alloc("g2_fm", [128, KC], F32)
        small = ar.alloc("small", [128, 64], F32)
        dma("sync", lambda e: e.dma_start(out=ident_f[:], in_=ident_d), writes=["ident_f"])
        dma("sync", lambda e: e.dma_start(out=tri_f[:], in_=tri_d), writes=["tri_f"])
        dma("sync", lambda e: e.dma_start(out=sel4[:], in_=sel4_d), writes=["sel4"])
        op("dve", lambda e: e.tensor_copy(out=ident_b[:], in_=ident_f[:]), reads=["ident_f"], writes=["ident_b"])
        op("dve", lambda e: e.tensor_copy(out=tri_b[:], in_=tri_f[:]), reads=["tri_f"], writes=["tri_b"])
        op("dve", lambda e: e.memset(ones_f[:], 1.0), writes=["ones_f"])

        m0 = ar.mark()
        cT = ar.alloc("cT", [128, KC], F32)
        sc2 = ar.alloc("silu_c2", [128, KC, 2], F32)
        adab = ar.alloc("adab", [128, 48], F32)
        n1g = ar.alloc("n1g", [128, KC], F32)
        n2g = ar.alloc("n2g", [128, KC], F32)
        awb = [ar.alloc("aw%d" % i, [128, KC, 512], F32) for i in range(2)]
        dma("sync", lambda e: e.dma_start(out=cT[:], in_=cT_d), writes=["cT"])
        dma("sync", lambda e: e.dma_start(out=adab[:], in_=adab_d), writes=["adab"])
        dma("sync", lambda e: e.dma_start(out=n1g[:], in_=n1g_d), writes=["n1g"])
        dma("sync", lambda e: e.dma_start(out=n2g[:], in_=n2g_d), writes=["n2g"])
        for j in range(2):
            op("act", lambda e, j=j: e.activation(out=sc2[:, :, j], in_=cT[:], func=AF.Silu),
               reads=["cT"], writes=["sc2"])
        ps_mod = psF[0]
        for blk in range(12):
            b = blk % 2
            for kc in range(KC):
                dma("sync", lambda e, blk=blk, b=b, kc=kc: e.dma_start(
                    out=awb[b][:, kc, :], in_=adaw_d[kc * 128:(kc + 1) * 128, blk * 512:(blk + 1) * 512]),
                    writes=["aw%d" % b])
            for jj in range(4):
                j = blk * 4 + jj
                for kc in range(KC):
                    op("pe", lambda e, b=b, jj=jj, kc=kc, j=j: e.matmul(
                        ps_mod[:, 2 * j:2 * j + 2], lhsT=awb[b][:, kc, jj * 128:(jj + 1) * 128],
                        rhs=sc2[:, kc, :], start=(kc == 0), stop=(kc == KC - 1)),
                        reads=["aw%d" % b, "sc2"], writes=["ps0"], signal=(kc == KC - 1))
        op("dve", lambda e: e.tensor_tensor(
            out=mod_fm[:], in0=ps_mod[:, 0:96].rearrange("p (j two) -> p j two", two=2)[:, :, 0], in1=adab[:], op=ALU.add),
            reads=["ps0", "adab"], writes=["mod_fm"])
        op("dve", lambda e: e.scalar_tensor_tensor(out=g1_fm[:], in0=mod_fm[:, 8:16], scalar=1.0, in1=n1g[:],
                                                   op0=ALU.add, op1=ALU.mult),
           reads=["mod_fm", "n1g"], writes=["g1_fm"])
        op("dve", lambda e: e.scalar_tensor_tensor(out=g2_fm[:], in0=mod_fm[:, 32:40], scalar=1.0, in1=n2g[:],
                                                   op0=ALU.add, op1=ALU.mult),
           reads=["mod_fm", "n2g"], writes=["g2_fm"])
        dbg_out("mod_fm", mod_fm[:], [128, 48], "mod_fm")
        fw.barrier()
        ar.release(m0)

        bc_n = [0]
        dg = [ar.alloc("dg%d" % i, [128, 128], F32) for i in range(2)]

        def bcast(dst, dkey, src_ap, skey):
            for kc in range(KC):
                k = bc_n[0] % 2
                bc_n[0] += 1
                op("dve", lambda e, k=k, kc=kc: e.tensor_scalar(out=dg[k][:], in0=ident_f[:], scalar1=src_ap[:, kc:kc + 1],
                                                                scalar2=None, op0=ALU.mult),
                   reads=["ident_f", skey], writes=["dg%d" % k])
                bank = 4 + kc // 4
                op("pe", lambda e, k=k, kc=kc, bank=bank: e.matmul(
                    psF[bank][:, (kc % 4) * 128:(kc % 4 + 1) * 128], lhsT=ones_f[:], rhs=dg[k][:], start=True, stop=True),
                    reads=["ones_f", "dg%d" % k], writes=["ps%d" % bank])
            for hb in range(2):
                op("act", lambda e, hb=hb: e.copy(out=dst[:, hb * 512:(hb + 1) * 512], in_=psF[4 + hb][:]),
                   reads=["ps%d" % (4 + hb)], writes=[dkey])

        def rstd_of(ssq_ap, out_ap, n, key_in, key_out):
            op("act", lambda e: e.activation(out=out_ap, in_=ssq_ap, func=AF.Ln, scale=1.0 / n, bias=eps_col(out_ap)),
               reads=[key_in, "epsc"], writes=[key_out])
            op("act", lambda e: e.activation(out=out_ap, in_=out_ap, func=AF.Exp, scale=-0.5),
               reads=[key_out], writes=[key_out])

        epsc = ar.alloc("epsc", [128, 1], F32)
        op("dve", lambda e: e.memset(epsc[:], EPS), writes=["epsc"])

        def eps_col(out_ap):
            return epsc[0:out_ap.shape[0], 0:1]

        regA0 = ar.mark()
        actT = ar.alloc("actT", [128, KC, S], BF16)
        cvb = [ar.alloc("cvb%d" % i, [128, D], BF16) for i in range(2)]
        cv_n = [0]

        def convert_chunks(n):
            for _ in range(2 * n):
                c = cv_n[0]
                if c >= 256:
                    return
                cv_n[0] += 1
                k = c % 2
                r0, hf = (c // 2) * 128, (c % 2) * D
                dma("pool", lambda e, r0=r0, hf=hf, k=k: e.dma_start(out=cvb[k][:], in_=uv_d[r0:r0 + 128, hf:hf + D]), writes=["cvb%d" % k])
                dma("sync", lambda e, r0=r0, hf=hf, k=k: e.dma_start(out=uvb_d[r0:r0 + 128, hf:hf + D], in_=cvb[k][:]), reads=["cvb%d" % k])

        m1 = ar.mark()
        G1b = ar.alloc("G1b", [128, D], F32)
        SH1b = ar.alloc("SH1b", [128, D], F32)
        bcast(G1b, "G1b", g1_fm, "g1_fm")
        bcast(SH1b, "SH1b", mod_fm[:, 0:8], "mod_fm")
        dbg_out("G1b", G1b[:], [128, D], "G1b")
        dbg_out("SH1b", SH1b[:], [128, D], "SH1b")
        xb = [ar.alloc("xb%d" % i, [128, D], F32) for i in range(2)]
        junk = ar.alloc("junk", [128, D], BF16)
        t1 = ar.alloc("t1", [128, D], F32)
        hb16 = [ar.alloc("hb16_%d" % i, [128, D], BF16) for i in range(2)]
        ssq = ar.alloc("ssq", [128, NT], F32)
        rstd = ar.alloc("rstd", [128, NT], F32)
        for i in range(NT):
            b = i % 2
            dma("sync", lambda e, i=i, b=b: e.dma_start(out=xb[b][:], in_=x_d[i * 128:(i + 1) * 128, :]), writes=["xb%d" % b])
            op("act", lambda e, i=i, b=b: e.activation(out=junk[:], in_=xb[b][:], func=AF.Square, accum_out=ssq[:, i:i + 1]),
               reads=["xb%d" % b], writes=["junk", "ssq%d" % i])
            rstd_of(ssq[:, i:i + 1], rstd[:, i:i + 1], D, "ssq%d" % i, "rstd%d" % i)
            op("dve", lambda e, i=i, b=b: e.scalar_tensor_tensor(out=t1[:], in0=xb[b][:], scalar=rstd[:, i:i + 1], in1=G1b[:],
                                                                 op0=ALU.mult, op1=ALU.mult),
               reads=["xb%d" % b, "rstd%d" % i, "G1b"], writes=["t1"])
            op("pool", lambda e, b=b: e.tensor_tensor(out=hb16[b][:], in0=t1[:], in1=SH1b[:], op=ALU.add),
               reads=["t1", "SH1b"], writes=["hb16_%d" % b])
            pt = psT[i % 2]
            for kc in range(KC):
                op("pe", lambda e, kc=kc, b=b, pt=pt: e.transpose(pt[:, kc * 128:(kc + 1) * 128], hb16[b][:, kc * 128:(kc + 1) * 128], ident_b[:]),
                   reads=["hb16_%d" % b, "ident_b"], writes=["psT%d" % (i % 2)])
            op("act", lambda e, i=i, pt=pt: e.copy(out=actT[:, :, i * 128:(i + 1) * 128],
                                                   in_=pt[:].rearrange("p (kc t) -> p kc t", kc=KC)),
               reads=["psT%d" % (i % 2)], writes=["actT"])
        dbg_out("ssq", ssq[:], [128, NT], "ssq15")
        dbg_out("rstd", rstd[:], [128, NT], "rstd15")
        dbg_out("t1", t1[:], [128, D], "t1")
        if "hT" in dbg:
            hdbg = ar.alloc("hdbg", [128, KC, 128], F32)
            op("dve", lambda e: e.tensor_copy(out=hdbg[:], in_=actT[:, :, 128:256]), reads=["actT"], writes=["hdbg"])
            dbg_out("hT", hdbg[:], [128, KC, 128], "hdbg")
        fw.barrier()
        ar.release(m1)
        if stage <= 1:
            fw.wait_all("sync")
            return nc, dbg_d

        o_tok = ar.alloc("o_tok", [128, NT, D], BF16)
        gmn_b = ar.alloc("gmn_b", [128, 512], F32)
        dma("sync", lambda e: e.dma_start(out=gmn_b[:], in_=mng_d[0:1, :].partition_broadcast(128)), writes=["gmn_b"])
        m2 = ar.mark()
        qkT = ar.alloc("qkT", [128, 8, S], BF16)
        vm = ar.alloc("vm", [128, NT, 4, 129], BF16)
        og = ar.alloc("og", [128, NT, 512], BF16)
        gneg = ar.alloc("gneg", [4, S], F32)
        gi = ar.alloc("gi", [4, S], F32)
        gsp = ar.alloc("gsp", [4, S], F32)
        cw = ar.alloc("cw", [128, 8, 4], F32)
        cb = ar.alloc("cb", [128, 8], F32)
        gb = ar.alloc("gb", [4, 2], F32)
        ngbf = ar.alloc("ngbf", [4, 1], F32)
        dma("sync", lambda e: e.dma_start(out=cw[:], in_=cw_d), writes=["cw"])
        dma("sync", lambda e: e.dma_start(out=cb[:], in_=cb_d), writes=["cb"])
        dma("sync", lambda e: e.dma_start(out=gb[:], in_=gb_d), writes=["gb"])
        op("dve", lambda e: e.tensor_scalar(out=ngbf[:], in0=gb[:, 1:2], scalar1=-1.0, scalar2=None, op0=ALU.mult),
           reads=["gb"], writes=["ngbf"])
        op("pool", lambda e: e.memset(vm[:, :, :, 128:129], 1.0), writes=["vm_ones"])
        m2s = ar.mark()
        wbf = [ar.alloc("wbf%d" % i, [128, KC, 520], BF16) for i in range(2)]
        ub = [ar.alloc("ub%d" % i, [128, S + 3], F32) for i in range(2)]
        cacc = ar.alloc("cacc", [128, S], F32)
        for i in range(2):
            op("pool", lambda e, i=i: e.memset(ub[i][:, 0:3], 0.0), writes=["ub%d" % i])
        wn = [0]
        pcn = [0]

        def load_w(src_d, c0, n):
            b = wn[0] % 2
            wn[0] += 1
            for kc in range(KC):
                dma("pool", lambda e, kc=kc: e.dma_start(out=wbf[b][:, kc, 0:n], in_=src_d[kc * 128:(kc + 1) * 128, c0:c0 + n]),
                    writes=["wbf%d" % b])
            return b

        def proj_fm(b, cc, m, evac, k0=0):
            convert_chunks(1)
            for tb in range(4):
                pi = pcn[0] % 4
                pcn[0] += 1
                for kc in range(KC):
                    op("pe", lambda e, kc=kc, tb=tb, pi=pi: e.matmul(
                        psF[pi][0:m, :], lhsT=wbf[b][:, kc, cc:cc + m], rhs=actT[:, kc, tb * 512:(tb + 1) * 512],
                        start=(kc == 0), stop=(kc == KC - 1)),
                        reads=["wbf%d" % b, "actT"], writes=["ps%d" % pi], signal=(kc == KC - 1))
                evac(tb, psF[pi], "ps%d" % pi)

        def proj_tm(b, evac):
            convert_chunks(2)
            for i in range(NT):
                pi = pcn[0] % 4
                pcn[0] += 1
                for kc in range(KC):
                    op("pe", lambda e, kc=kc, i=i, pi=pi: e.matmul(
                        psF[pi][:, :], lhsT=actT[:, kc, i * 128:(i + 1) * 128], rhs=wbf[b][:, kc, 0:512],
                        start=(kc == 0), stop=(kc == KC - 1)),
                        reads=["wbf%d" % b, "actT"], writes=["ps%d" % pi], signal=(kc == KC - 1))
                evac(i, psF[pi], "ps%d" % pi)

        for blk in range(2):
            b = load_w(win_d, blk * 512, 512)
            for cc in range(4):
                ch = blk * 4 + cc
                u = ub[ch % 2]
                ukey = "ub%d" % (ch % 2)
                proj_fm(b, cc * 128, 128, lambda tb, ps, pk, u=u, ukey=ukey: op(
                    "act", lambda e: e.copy(out=u[:, 3 + tb * 512:3 + (tb + 1) * 512], in_=ps[:, :]),
                    reads=[pk], writes=[ukey]))
                op("dve", lambda e, u=u, ch=ch: e.tensor_scalar(out=cacc[:], in0=u[:, 3:3 + S], scalar1=cw[:, ch, 3:4],
                                                               scalar2=None, op0=ALU.mult),
                   reads=[ukey, "cw"], writes=["cacc"])
                for j in (2, 1, 0):
                    op("dve", lambda e, u=u, ch=ch, j=j: e.scalar_tensor_tensor(
                        out=cacc[:], in0=u[:, j:j + S], scalar=cw[:, ch, j:j + 1], in1=cacc[:], op0=ALU.mult, op1=ALU.add),
                        reads=[ukey, "cw", "cacc"], writes=["cacc"])
                op("act", lambda e, ch=ch: e.activation(out=qkT[:, ch, :], in_=cacc[:], func=AF.Silu, bias=cb[:, ch:ch + 1]),
                   reads=["cacc", "cb"], writes=["qkT%d" % ch])
        b = load_w(win_d, 1024, 512)
        proj_tm(b, lambda i, ps, pk: op(
            "act", lambda e: e.copy(out=vm[:, i, :, 0:128], in_=ps[:, :].rearrange("p (h d) -> p h d", h=4)),
            reads=[pk], writes=["vm%d" % i]))
        b = load_w(win_d, 1536, 520)
        proj_tm(b, lambda i, ps, pk: op(
            "act", lambda e: e.activation(out=og[:, i, :], in_=ps[:, :], func=AF.Sigmoid),
            reads=[pk], writes=["og%d" % i]))
        proj_fm(b, 512, 4, lambda tb, ps, pk: op(
            "act", lambda e: e.activation(out=gi[:, tb * 512:(tb + 1) * 512], in_=ps[0:4, :], func=AF.Identity, bias=gb[:, 0:1]),
            reads=[pk, "gb"], writes=["gi"]))
        proj_fm(b, 516, 4, lambda tb, ps, pk: op(
            "act", lambda e: e.activation(out=gsp[:, tb * 512:(tb + 1) * 512], in_=ps[0:4, :], func=AF.Exp, scale=-1.0, bias=ngbf[:, 0:1]),
            reads=[pk, "ngbf"], writes=["gsp"]))
        op("act", lambda e: e.activation(out=gsp[:], in_=gsp[:], func=AF.Ln, scale=1.0, bias=1.0),
           reads=["gsp"], writes=["gsp"])
        if "gi" in dbg:
            dbg_out("gi", gi[:], [4, S], "gi")
            dbg_out("gsp", gsp[:], [4, S], "gsp")
        if "qkT" in dbg:
            qdbg = ar.alloc("qdbg", [128, 2, 512], F32)
            op("dve", lambda e: e.tensor_copy(out=qdbg[:, 0, :], in_=qkT[:, 1, 0:512]), reads=["qkT1"], writes=["qdbg"])
            op("dve", lambda e: e.tensor_copy(out=qdbg[:, 1, :], in_=qkT[:, 6, 1536:2048]), reads=["qkT6"], writes=["qdbg"])
            dbg_out("qkT", qdbg[:], [128, 2, 512], "qdbg")
        fw.barrier()
        ar.release(m2s)
        if stage <= 2:
            fw.wait_all("sync")
            return nc, dbg_d

        ones4 = ar.alloc("ones4", [4, S], F32)
        Bn = ar.alloc("Bn", [4, S], F32)
        a_colT = ar.alloc("a_colT", [128, NT, 4], F32)
        emtT = ar.alloc("emtT", [128, NT, 4], F32)
        negA_b = [ar.alloc("negA_b0", [128, S], F32)] * 2
        Dt = [ar.alloc("Dt%d" % i, [128, 512], F32) for i in range(2)]
        Pt = [ar.alloc("Pt%d" % i, [128, 512], BF16) for i in range(3)]
        fsm = [ar.alloc("fsm%d" % i, [128, 8], F32) for i in range(2)]
        hh = [ar.alloc("hh%d" % i, [128, 128], F32) for i in range(2)]
        o1 = [ar.alloc("o1_%d" % i, [128, 128], F32) for i in range(2)]
        junk128 = ar.alloc("junk128", [128, 128], BF16)
        op("dve", lambda e: e.memset(ones4[:], 1.0), writes=["ones4"])
        op("dve", lambda e: e.tensor_tensor_scan(out=Bn[:], data0=ones4[:], data1=gsp[:], initial=0.0, op0=ALU.mult, op1=ALU.add),
           reads=["ones4", "gsp"], writes=["Bn"])
        op("dve", lambda e: e.tensor_tensor(out=gi[:], in0=gi[:], in1=Bn[:], op=ALU.add), reads=["gi", "Bn"], writes=["gi"])
        op("dve", lambda e: e.tensor_tensor_scan(out=gsp[:], data0=ones4[:], data1=gi[:], initial=0.0, op0=ALU.mult, op1=ALU.max),
           reads=["ones4", "gi"], writes=["gsp"])
        op("dve", lambda e: e.tensor_scalar(out=gneg[:], in0=gsp[:], scalar1=-1.0, scalar2=None, op0=ALU.mult),
           reads=["gsp"], writes=["gneg"])
        op("dve", lambda e: e.tensor_tensor(out=Bn[:], in0=Bn[:], in1=gneg[:], op=ALU.add), reads=["Bn", "gneg"], writes=["Bn"])
        op("act", lambda e: e.activation(out=Bn[:], in_=Bn[:], func=AF.Exp), reads=["Bn"], writes=["Bn"])
        for j in range(NT):
            op("pe", lambda e, j=j: e.matmul(psF[4][:, j * 4:(j + 1) * 4], lhsT=gi[0:4, j * 128:(j + 1) * 128], rhs=ident_f[0:4, 0:4],
                                             start=True, stop=True), reads=["gi", "ident_f"], writes=["ps4"])
            op("pe", lambda e, j=j: e.matmul(psF[5][:, j * 4:(j + 1) * 4], lhsT=Bn[0:4, j * 128:(j + 1) * 128], rhs=ident_f[0:4, 0:4],
                                             start=True, stop=True), reads=["Bn", "ident_f"], writes=["ps5"])
        op("dve", lambda e: e.tensor_scalar(out=a_colT[:], in0=psF[4][:, 0:64].rearrange("p (j h) -> p j h", h=4),
                                            scalar1=float(math.log(128.0 ** -0.5)), scalar2=None, op0=ALU.add),
           reads=["ps4"], writes=["a_colT"])
        op("dve", lambda e: e.tensor_copy(out=emtT[:], in_=psF[5][:, 0:64].rearrange("p (j h) -> p j h", h=4)),
           reads=["ps5"], writes=["emtT"])
        dbg_out("a_colT", a_colT[:], [128, NT, 4], "a_colT")
        dbg_out("emtT", emtT[:], [128, NT, 4], "emtT")

        fin_n = [0]
        if "acc3" in dbg:
            acc3d = ar.alloc("acc3d", [128, 8, 129], F32)
        for h in range(4):
            nb = negA_b[h % 2]
            nbk = "negA_b0"
            for tb in range(4):
                op("pe", lambda e, h=h, tb=tb: e.matmul(psF[4 + tb % 2][:, :], lhsT=sel4[0:4, h * 128:(h + 1) * 128],
                                                        rhs=gneg[0:4, tb * 512:(tb + 1) * 512], start=True, stop=True),
                   reads=["sel4", "gneg"], writes=["ps%d" % (4 + tb % 2)])
                op("act", lambda e, tb=tb, nb=nb: e.copy(out=nb[:, tb * 512:(tb + 1) * 512], in_=psF[4 + tb % 2][:, :]),
                   reads=["ps%d" % (4 + tb % 2)], writes=[nbk])
            qh = qkT[:, h, :]
            kh = qkT[:, 4 + h, :]
            for tb in range(4):
                jmax = 4 * tb + 3
                convert_chunks(3)

                def s_mm(j, h=h, tb=tb, qh=qh, kh=kh):
                    op("pe", lambda e: e.matmul(psF[j % 2][:, :], lhsT=kh[:, j * 128:(j + 1) * 128], rhs=qh[:, tb * 512:(tb + 1) * 512],
                                                start=True, stop=True),
                       reads=["qkT%d" % h, "qkT%d" % (4 + h)], writes=["ps%d" % (j % 2)])

                s_mm(0)
                for j in range(jmax + 1):
                    if j + 1 <= jmax:
                        s_mm(j + 1)
                    d = Dt[j % 2]
                    p = Pt[j % 3]
                    pk = "Pt%d" % (j % 3)
                    op("act", lambda e, j=j, d=d, nb=nb: e.activation(out=d[:], in_=nb[:, tb * 512:(tb + 1) * 512], func=AF.Exp,
                                                                     bias=a_colT[:, j, h:h + 1]),
                       reads=[nbk, "a_colT"], writes=["Dt%d" % (j % 2)])
                    op("dve", lambda e, j=j, d=d, p=p: e.tensor_tensor(out=p[:], in0=psF[j % 2][:, :], in1=d[:], op=ALU.mult),
                       reads=["ps%d" % (j % 2), "Dt%d" % (j % 2)], writes=[pk])
                    if j >= 4 * tb:
                        li = j - 4 * tb
                        op("dve", lambda e, p=p, li=li: e.tensor_tensor(out=p[:, li * 128:(li + 1) * 128], in0=p[:, li * 128:(li + 1) * 128],
                                                                      in1=tri_b[:], op=ALU.mult),
                           reads=[pk, "tri_b"], writes=[pk])
                    for li in range(max(j - 4 * tb, 0), 4):
                        i = 4 * tb + li
                        acc = psF[2 + 2 * (tb % 2) + li // 2][:, (li % 2) * 256:(li % 2) * 256 + 129]
                        op("pe", lambda e, p=p, li=li, j=j, i=i, acc=acc: e.matmul(acc, lhsT=p[:, li * 128:(li + 1) * 128], rhs=vm[:, j, h, :],
                                                                                  start=(j == 0 and li % 2 == 0), stop=(j == i), skip_group_check=True),
                           reads=[pk, "vm%d" % j, "vm_ones"], writes=["ps%d" % (2 + 2 * (tb % 2) + li // 2)])
                for li in range(4):
                    i = 4 * tb + li
                    acc = psF[2 + 2 * (tb % 2) + li // 2][:, (li % 2) * 256:(li % 2) * 256 + 129]
                    k = fin_n[0] % 2
                    fin_n[0] += 1
                    sm, hk, ok_ = fsm[k], hh[k], o1[k]
                    smk, hkk, okk = "fsm%d" % k, "hh%d" % k, "o1_%d" % k
                    if "acc3" in dbg and h == 3 and i < 8:
                        op("dve", lambda e, acc=acc, i=i: e.tensor_copy(out=acc3d[:, i, :], in_=acc), reads=["ps%d" % (2 + 2 * (tb % 2) + li // 2)], writes=["acc3d"])
                    op("act", lambda e, acc=acc, sm=sm: e.activation(out=sm[:, 0:1], in_=acc[:, 128:129], func=AF.Abs),
                       reads=["ps%d" % (2 + 2 * (tb % 2) + li // 2)], writes=[smk])
                    op("dve", lambda e, sm=sm, i=i, h=h: e.tensor_tensor(out=sm[:, 1:2], in0=sm[:, 0:1], in1=emtT[:, i, h:h + 1], op=ALU.max),
                       reads=[smk, "emtT"], writes=[smk])
                    op("dve", lambda e, sm=sm: e.reciprocal(out=sm[:, 2:3], in_=sm[:, 1:2]), reads=[smk], writes=[smk])
                    op("dve", lambda e, sm=sm, acc=acc, hk=hk: e.tensor_scalar(out=hk[:], in0=acc[:, 0:128], scalar1=sm[:, 2:3], scalar2=None,
                                                                               op0=ALU.mult),
                       reads=["ps%d" % (2 + 2 * (tb % 2) + li // 2), smk], writes=[hkk])
                    op("act", lambda e, hk=hk, sm=sm: e.activation(out=junk128[:], in_=hk[:], func=AF.Square, accum_out=sm[:, 3:4]),
                       reads=[hkk], writes=["junk128", smk])
                    rstd_of(sm[:, 3:4], sm[:, 4:5], 128, smk, smk)
                    op("dve", lambda e, hk=hk, sm=sm, ok_=ok_, h=h: e.scalar_tensor_tensor(
                        out=ok_[:], in0=hk[:], scalar=sm[:, 4:5], in1=gmn_b[:, h * 128:(h + 1) * 128], op0=ALU.mult, op1=ALU.mult),
                        reads=[hkk, smk, "gmn_b"], writes=[okk])
                    op("dve", lambda e, ok_=ok_, i=i, h=h: e.tensor_tensor(out=o_tok[:, i, h * 128:(h + 1) * 128], in0=ok_[:],
                                                                          in1=og[:, i, h * 128:(h + 1) * 128], op=ALU.mult),
                       reads=[okk, "og%d" % i], writes=["o_tok%d" % i])
        if "acc3" in dbg:
            dbg_out("acc3", acc3d[:], [128, 8, 129], "acc3d")
        fw.barrier()
        ar.release(m2)
        if "hm" in dbg:
            hmd = ar.alloc("hmd", [128, NT, 512], F32)
            op("dve", lambda e: e.tensor_copy(out=hmd[:], in_=o_tok[:, :, 0:512]), reads=["o_tok%d" % i for i in range(NT)], writes=["hmd"])
            dbg_out("hm", hmd[:], [128, NT, 512], "hmd")
        if stage <= 3:
            fw.wait_all("sync")
            return nc, dbg_d

        m3 = ar.mark()
        dqk = ar.alloc("dqk", [128, 8, S], BF16)
        vd = ar.alloc("vd", [128, NT, 4, 129], BF16)
        gdn_b = ar.alloc("gdn_b", [128, 128], F32)
        lamv = ar.alloc("lamv", [128, 256], F32)
        lsm = ar.alloc("lsm", [128, 8], F32)
        ljunk = ar.alloc("ljunk", [128, 64], F32)
        dma("sync", lambda e: e.dma_start(out=gdn_b[:], in_=dng_d[0:1, :].partition_broadcast(128)), writes=["gdn_b"])
        dma("sync", lambda e: e.dma_start(out=lamv[:], in_=lamv_d[0:1, :].partition_broadcast(128)), writes=["lamv"])
        op("dve", lambda e: e.tensor_scalar(out=gdn_b[:], in0=gdn_b[:], scalar1=float(1.0 - LAM_INIT), scalar2=None, op0=ALU.mult),
           reads=["gdn_b"], writes=["gdn_b"])
        for t in range(2):
            op("dve", lambda e, t=t: e.scalar_tensor_tensor(out=ljunk[:], in0=lamv[:, t * 128:t * 128 + 64], scalar=1.0,
                                                          in1=lamv[:, t * 128 + 64:t * 128 + 128], op0=ALU.mult, op1=ALU.mult,
                                                          accum_out=lsm[:, t:t + 1]),
               reads=["lamv"], writes=["ljunk", "lsm"])
        op("dve", lambda e: e.tensor_copy(out=lsm[:, 6:8], in_=lsm[:, 0:2]), reads=["lsm"], writes=["lsm"])
        op("act", lambda e: e.activation(out=lsm[:, 2:4], in_=lsm[:, 6:8], func=AF.Exp), reads=["lsm"], writes=["lsm"])
        op("dve", lambda e: e.tensor_tensor(out=lsm[:, 4:5], in0=lsm[:, 3:4], in1=lsm[:, 2:3], op=ALU.subtract), reads=["lsm"], writes=["lsm"])
        op("dve", lambda e: e.tensor_scalar(out=lsm[:, 5:6], in0=lsm[:, 4:5], scalar1=float(-LAM_INIT), scalar2=None, op0=ALU.add),
           reads=["lsm"], writes=["lsm"])
        nlam = lsm[:, 5:6]
        op("pool", lambda e: e.memset(vd[:, :, :, 128:129], 1.0), writes=["vd_ones"])
        m3s = ar.mark()
        wbf = [ar.alloc("wbfd%d" % i, [128, KC, 512], BF16) for i in range(2)]
        for blk in range(2):
            b = load_w(win_d, 2056 + blk * 512, 512)
            for cc in range(4):
                ch = blk * 4 + cc
                proj_fm(b, cc * 128, 128, lambda tb, ps, pk, ch=ch: op(
                    "act", lambda e: e.copy(out=dqk[:, ch, tb * 512:(tb + 1) * 512], in_=ps[:, :]),
                    reads=[pk], writes=["dqk%d" % ch]))
        b = load_w(win_d, 3080, 512)
        proj_tm(b, lambda i, ps, pk: op(
            "act", lambda e: e.copy(out=vd[:, i, :, 0:128], in_=ps[:, :].rearrange("p (h d) -> p h d", h=4)),
            reads=[pk], writes=["vd%d" % i]))
        fw.barrier()
        ar.release(m3s)

        Et = [ar.alloc("Et%d" % i, [128, 512], BF16) for i in range(4)]
        fsm = [ar.alloc("dfsm%d" % i, [128, 8], F32) for i in range(2)]
        o0 = [ar.alloc("o0_%d" % i, [128, 128], F32) for i in range(2)]
        odf = [ar.alloc("odf%d" % i, [128, 128], F32) for i in range(2)]
        junk128 = ar.alloc("junk128d", [128, 128], BF16)
        fin_n = [0]
        en = [0]
        for h in range(4):
            for tb in range(4):
                jmax = 4 * tb + 3
                convert_chunks(3)
                steps = [(j, p) for j in range(jmax + 1) for p in range(2)]

                def s_mm(idx, h=h, tb=tb):
                    j, p = steps[idx]
                    op("pe", lambda e: e.matmul(psF[idx % 2][:, :], lhsT=dqk[p * 64:(p + 1) * 64, 4 + h, j * 128:(j + 1) * 128],
                                                rhs=dqk[p * 64:(p + 1) * 64, h, tb * 512:(tb + 1) * 512], start=True, stop=True),
                       reads=["dqk%d" % h, "dqk%d" % (4 + h)], writes=["ps%d" % (idx % 2)])

                s_mm(0)
                for idx, (j, p) in enumerate(steps):
                    if idx + 1 < len(steps):
                        s_mm(idx + 1)
                    ek = en[0] % 4
                    en[0] += 1
                    E = Et[ek]
                    ekey = "Et%d" % ek
                    op("act", lambda e, E=E, idx=idx: e.activation(out=E[:], in_=psF[idx % 2][:, :], func=AF.Exp, scale=0.125),
                       reads=["ps%d" % (idx % 2)], writes=[ekey])
                    if j >= 4 * tb:
                        li = j - 4 * tb
                        op("dve", lambda e, E=E, li=li: e.tensor_tensor(out=E[:, li * 128:(li + 1) * 128], in0=E[:, li * 128:(li + 1) * 128],
                                                                      in1=tri_b[:], op=ALU.mult),
                           reads=[ekey, "tri_b"], writes=[ekey])
                    for li in range(max(j - 4 * tb, 0), 4):
                        i = 4 * tb + li
                        bank = 2 + 2 * p + li // 2
                        acc = psF[bank][:, (li % 2) * 256:(li % 2) * 256 + 129]
                        op("pe", lambda e, E=E, li=li, j=j, i=i, acc=acc: e.matmul(acc, lhsT=E[:, li * 128:(li + 1) * 128], rhs=vd[:, j, h, :],
                                                                                  start=(j == 0 and li % 2 == 0), stop=(j == i), skip_group_check=True),
                           reads=[ekey, "vd%d" % j, "vd_ones"], writes=["ps%d" % bank])
                for li in range(4):
                    i = 4 * tb + li
                    a0 = psF[2 + li // 2][:, (li % 2) * 256:(li % 2) * 256 + 129]
                    a1 = psF[4 + li // 2][:, (li % 2) * 256:(li % 2) * 256 + 129]
                    k = fin_n[0] % 2
                    fin_n[0] += 1
                    sm, o0k, odk = fsm[k], o0[k], odf[k]
                    smk, o0kk, odkk = "dfsm%d" % k, "o0_%d" % k, "odf%d" % k
                    op("dve", lambda e, sm=sm, a0=a0: e.reciprocal(out=sm[:, 0:1], in_=a0[:, 128:129]), reads=["ps%d" % (2 + li // 2)], writes=[smk])
                    op("dve", lambda e, sm=sm, a1=a1: e.reciprocal(out=sm[:, 1:2], in_=a1[:, 128:129]), reads=["ps%d" % (4 + li // 2)], writes=[smk])
                    op("dve", lambda e, sm=sm: e.tensor_tensor(out=sm[:, 2:3], in0=sm[:, 1:2], in1=nlam, op=ALU.mult), reads=[smk, "lsm"], writes=[smk])
                    op("dve", lambda e, sm=sm, a0=a0, o0k=o0k: e.tensor_scalar(out=o0k[:], in0=a0[:, 0:128], scalar1=sm[:, 0:1], scalar2=None, op0=ALU.mult),
                       reads=["ps%d" % (2 + li // 2), smk], writes=[o0kk])
                    op("dve", lambda e, sm=sm, a1=a1, o0k=o0k, odk=odk: e.scalar_tensor_tensor(
                        out=odk[:], in0=a1[:, 0:128], scalar=sm[:, 2:3], in1=o0k[:], op0=ALU.mult, op1=ALU.add),
                        reads=["ps%d" % (4 + li // 2), smk, o0kk], writes=[odkk])
                    op("act", lambda e, odk=odk, sm=sm: e.activation(out=junk128[:], in_=odk[:], func=AF.Square, accum_out=sm[:, 3:4]),
                       reads=[odkk], writes=["junk128d", smk])
                    rstd_of(sm[:, 3:4], sm[:, 4:5], 128, smk, smk)
                    op("dve", lambda e, odk=odk, sm=sm, i=i, h=h: e.scalar_tensor_tensor(
                        out=o_tok[:, i, 512 + h * 128:512 + (h + 1) * 128], in0=odk[:], scalar=sm[:, 4:5], in1=gdn_b[:], op0=ALU.mult, op1=ALU.mult),
                        reads=[odkk, smk, "gdn_b"], writes=["o_tok%d" % i])
        fw.barrier()
        ar.release(m3)
        if "od" in dbg:
            odd = ar.alloc("odd", [128, NT, 512], F32)
            op("dve", lambda e: e.tensor_copy(out=odd[:], in_=o_tok[:, :, 512:1024]), reads=["o_tok%d" % i for i in range(NT)], writes=["odd"])
            dbg_out("od", odd[:], [128, NT, 512], "odd")
        if stage <= 4:
            fw.wait_all("sync")
            return nc, dbg_d

        regA1 = ar.mark()
        x1 = ar.alloc("x1", [128, NT, D], F32)
        peer_top = ar.mark()
        m4 = ar.mark()
        GT1b = ar.alloc("GT1b", [128, D], F32)
        bcast(GT1b, "GT1b", mod_fm[:, 16:24], "mod_fm")
        wo = ar.alloc("wo", [128, KC, D], BF16)
        for hb in range(2):
            for kc in range(KC):
                dma("pool", lambda e, hb=hb, kc=kc: e.dma_start(out=wo[:, kc, hb * 512:(hb + 1) * 512],
                                                               in_=wout_d[kc * 128:(kc + 1) * 128, hb * 512:(hb + 1) * 512]),
                    writes=["wo%d" % hb])
        xb2 = [ar.alloc("xb2_%d" % i, [128, D], F32) for i in range(2)]
        ytmp = [ar.alloc("ytmp%d" % i, [128, 512], F32) for i in range(2)]
        for i in range(NT):
            pt = psT[i % 2]
            for kc in range(KC):
                op("pe", lambda e, kc=kc, i=i, pt=pt: e.transpose(pt[:, kc * 128:(kc + 1) * 128], o_tok[:, i, kc * 128:(kc + 1) * 128], ident_b[:]),
                   reads=["o_tok%d" % i, "ident_b"], writes=["psT%d" % (i % 2)])
            op("act", lambda e, i=i, pt=pt: e.copy(out=actT[:, :, i * 128:(i + 1) * 128], in_=pt[:].rearrange("p (kc t) -> p kc t", kc=KC)),
               reads=["psT%d" % (i % 2)], writes=["actT%d" % i])
        convert_chunks(8)
        for i in range(NT):
            b = i % 2
            convert_chunks(1)
            dma("sync", lambda e, i=i, b=b: e.dma_start(out=xb2[b][:], in_=x_d[i * 128:(i + 1) * 128, :]), writes=["xb2_%d" % b])
            for hb in range(2):
                pi = (2 * i + hb) % 4
                for kc in range(KC):
                    op("pe", lambda e, kc=kc, i=i, hb=hb, pi=pi: e.matmul(psF[pi][:, :], lhsT=actT[:, kc, i * 128:(i + 1) * 128],
                                                                         rhs=wo[:, kc, hb * 512:(hb + 1) * 512], start=(kc == 0), stop=(kc == KC - 1)),
                       reads=["actT%d" % i, "wo%d" % hb], writes=["ps%d" % pi], signal=(kc == KC - 1))
                yt = ytmp[hb]
                op("dve", lambda e, pi=pi, hb=hb, yt=yt: e.tensor_tensor(out=yt[:], in0=psF[pi][:, :], in1=GT1b[:, hb * 512:(hb + 1) * 512], op=ALU.mult),
                   reads=["ps%d" % pi, "GT1b"], writes=["ytmp%d" % hb])
                op("dve", lambda e, i=i, hb=hb, b=b, yt=yt: e.tensor_tensor(out=x1[:, i, hb * 512:(hb + 1) * 512], in0=yt[:],
                                                                           in1=xb2[b][:, hb * 512:(hb + 1) * 512], op=ALU.add),
                   reads=["ytmp%d" % hb, "xb2_%d" % b], writes=["x1_%d" % i])
        convert_chunks(128)
        if "x1" in dbg:
            dbg_out("x1", x1[:], [128, NT, D], "x1_15")
        fw.barrier()
        ar.release(m4)
        if stage <= 5:
            fw.wait_all("sync")
            return nc, dbg_d

        arA = Arena(nc, base=regA0, limit=regA1)
        arA.n = 5000
        wq = arA.alloc("wq", [128, KC, 2048], BF16)
        keysT = arA.alloc("keysT", [128, 16, 128], BF16)
        G2b = arA.alloc("G2b", [128, D], F32)
        SH2b = arA.alloc("SH2b", [128, D], F32)
        GT2b = arA.alloc("GT2b", [128, D], F32)
        FGb = arA.alloc("FGb", [128, D], F32)
        m_ssb = arA.mark()
        s_sb = arA.alloc("s_sb", [128, 16, 128], F32)
        for qb in range(4):
            for kc in range(KC):
                dma("pool", lambda e, qb=qb, kc=kc: e.dma_start(out=wq[:, kc, qb * 512:(qb + 1) * 512],
                                                               in_=wq_d[kc * 128:(kc + 1) * 128, qb * 512:(qb + 1) * 512]),
                    writes=["wq"])
        for c4 in range(4):
            dma("pool", lambda e, c4=c4: e.dma_start(out=keysT[:, c4 * 4:(c4 + 1) * 4, :], in_=keys_d[:, c4 * 4:(c4 + 1) * 4, :]), writes=["keysT"])
        dma("sync", lambda e: e.dma_start(out=FGb[:], in_=fg_d[0:1, :].partition_broadcast(128)), writes=["FGb"])
        bcast(G2b, "G2b", g2_fm, "g2_fm")
        bcast(SH2b, "SH2b", mod_fm[:, 24:32], "mod_fm")
        bcast(GT2b, "GT2b", mod_fm[:, 40:48], "mod_fm")
        iota16 = ar.alloc("iota16", [128, 16], F32)
        thr15 = ar.alloc("thr15", [128, 15], F32)
        dma("sync", lambda e: e.dma_start(out=iota16[:], in_=iota16_d[0:1, :].partition_broadcast(128)), writes=["iota16"])
        op("dve", lambda e: e.tensor_scalar(out=thr15[:], in0=iota16[:, 0:15], scalar1=16.0, scalar2=16.0, op0=ALU.mult, op1=ALU.add),
           reads=["iota16"], writes=["thr15"])
        h2bs = [ar.alloc("h2b0", [128, D], BF16), arA.alloc("h2b1", [128, D], BF16)]
        h2T = ar.alloc("h2T", [128, KC, 128], BF16)
        m_qTb = ar.mark()
        qTb = ar.alloc("qTb", [128, 16, 128], BF16)
        m_wk = ar.mark()
        wk = ar.alloc("wk", [128, 16, 128], F32)
        m_cand = ar.mark()
        cand = ar.alloc("cand", [128, 8, 256], F32)
        ar_pj = Arena(nc, base=m_cand, limit=m_cand + 4096)
        ar_pj.n = 7300
        PJK = ["cand"] + ["cand_%d" % h for h in range(8)]
        pjunk = ar_pj.alloc("pjunk", [128, D], BF16)
        sv = ar.alloc("sv", [128, 16, 16], F32)
        si = ar.alloc("si", [128, 16, 16], U32)
        sif = ar.alloc("sif", [128, 16, 16], F32)
        fv = ar.alloc("fv", [128, 8, 16], F32)
        fp = ar.alloc("fp", [128, 8, 16], U32)
        fpf = ar.alloc("fpf", [128, 128], F32)
        fa = ar.alloc("fa", [128, 128], F32)
        fb = ar.alloc("fb", [128, 128], F32)
        ia = ar.alloc("ia", [128, 128], F32)
        ib = ar.alloc("ib", [128, 128], F32)
        eidx = ar.alloc("eidx", [128, 128], I32)
        gt = ar.alloc("gt", [128, 8, 16], F32)
        zs = ar.alloc("zs", [128, 16], F32)
        psm = ar.alloc("psm", [128, 8], F32)
        pacc = arA.alloc("pacc", [128, D], F32)
        NG = 8
        NS = 3
        op("dve", lambda e: e.memset(eidx[:], 0), writes=["eidx"])
        NG = 10
        gbuf = [ar.alloc("gbuf%d" % i, [128, 2 * D], BF16) for i in range(NG - 1)]
        ar_wk = Arena(nc, base=m_wk, limit=m_wk + 8192)
        ar_wk.n = 7000
        gbuf.append(ar_wk.alloc("gbuf_wk", [128, 2 * D], BF16))
        gkey = ["gbuf%d" % i for i in range(NG - 1)] + ["wk"]
        for nm, mk, cnt in ():
            ar_al = Arena(nc, base=mk, limit=mk + 4096 * cnt)
            ar_al.n = 7100 + len(gbuf)
            for _ in range(cnt):
                gbuf.append(ar_al.alloc("gbuf_" + nm, [128, 2 * D], BF16))
                gkey.append(nm)
        NG = len(gbuf)
        xal = []
        for t in range(NT):
            ar_x = Arena(nc, base=regA1 + t * 4096, limit=regA1 + (t + 1) * 4096)
            ar_x.n = 7400 + t
            xal.append(ar_x.alloc("gx", [128, 2 * D], BF16))
        dgb = [arA.alloc("dgb%d" % i, [128, 128], BF16) for i in range(8)]
        ssm = [arA.alloc("ssm%d" % i, [128, 16], F32) for i in range(NS)]
        gn = [0]
        sv4 = sv[:].rearrange("p (h two) a -> p h two a", two=2)
        sif4 = sif[:].rearrange("p (h two) a -> p h two a", two=2)
        cand4 = cand[:].rearrange("p h (a b) -> p h a b", b=16)
        wk4 = wk[:].rearrange("p c n -> p (c n)").rearrange("p (h a b) -> p h a b", h=8, b=16)
        cmpT = wk[:].rearrange("p c n -> p (c n)")[:, 0:1920].rearrange("p (s m) -> p s m", m=15)
        fa3 = fa[:].rearrange("p (h k) -> p h k", k=16)
        fb3 = fb[:].rearrange("p (h k) -> p h k", k=16)
        ia3 = ia[:].rearrange("p (h k) -> p h k", k=16)
        ib3 = ib[:].rearrange("p (h k) -> p h k", k=16)

        def front(i):
            xi = x1[:, i, :]
            xk = "x1_%d" % i
            h2b = h2bs[i % 2]
            h2bk = "h2b%d" % (i % 2)
            op("act", lambda e, xi=xi: e.activation(out=pjunk[:], in_=xi, func=AF.Square, accum_out=psm[:, 0:1]),
               reads=[xk], writes=PJK + ["psmA"])
            rstd_of(psm[:, 0:1], psm[:, 1:2], D, "psmA", "psmA")
            op("dve", lambda e, xi=xi: e.scalar_tensor_tensor(out=pacc[:], in0=xi, scalar=psm[:, 1:2], in1=G2b[:], op0=ALU.mult, op1=ALU.mult),
               reads=[xk, "psmA", "G2b"], writes=["pacc"])
            op("dve", lambda e: e.tensor_tensor(out=h2b[:], in0=pacc[:], in1=SH2b[:], op=ALU.add), reads=["pacc", "SH2b"], writes=[h2bk])
            pt = psT[0]
            for kc in range(KC):
                op("pe", lambda e, kc=kc, pt=pt: e.transpose(pt[:, kc * 128:(kc + 1) * 128], h2b[:, kc * 128:(kc + 1) * 128], ident_b[:]),
                   reads=[h2bk, "ident_b"], writes=["psT0"])
            op("act", lambda e, pt=pt: e.copy(out=h2T[:], in_=pt[:].rearrange("p (kc t) -> p kc t", kc=KC)),
               reads=["psT0"], writes=["h2T"])
            for cg in range(4):
                bank = 2 + cg % 2
                for cc in range(4):
                    c = cg * 4 + cc
                    for kc in range(KC):
                        op("pe", lambda e, c=c, cc=cc, kc=kc, bank=bank: e.matmul(
                            psF[bank][:, cc * 128:(cc + 1) * 128], lhsT=wq[:, kc, c * 128:(c + 1) * 128], rhs=h2T[:, kc, :],
                            start=(kc == 0), stop=(kc == KC - 1)),
                            reads=["wq", "h2T"], writes=["ps%d" % bank], signal=(kc == KC - 1))
                op("act", lambda e, cg=cg, bank=bank: e.copy(out=qTb[:, cg * 4:(cg + 1) * 4, :],
                                                             in_=psF[bank][:, :].rearrange("p (c t) -> p c t", c=4)),
                   reads=["ps%d" % bank], writes=["qTb%d" % cg])
            for cg in range(4):
                for cc in range(4):
                    c = cg * 4 + cc
                    op("pe", lambda e, c=c, cc=cc, cg=cg: e.matmul(psF[cg % 2][:, cc * 128:(cc + 1) * 128], lhsT=qTb[:, c, :], rhs=keysT[:, c, :],
                                                                 start=True, stop=True),
                       reads=["qTb%d" % cg, "keysT"], writes=["ps%d" % (cg % 2)])
                op("act", lambda e, cg=cg: e.copy(out=s_sb[:, cg * 4:(cg + 1) * 4, :], in_=psF[cg % 2][:, :].rearrange("p (c n) -> p c n", c=4)),
                   reads=["ps%d" % (cg % 2)], writes=["s_sb%d" % cg])
        def topk(i):
            SK = lambda c: "s_sb%d" % (c // 4)
            for c in range(16):
                op("dve", lambda e, c=c: e.max(out=sv[:, c, 0:8], in_=s_sb[:, c, :]), reads=[SK(c)], writes=["sva%d" % c])
            for c in range(16):
                op("dve", lambda e, c=c: e.max_index(out=si[:, c, 0:8], in_max=sv[:, c, 0:8], in_values=s_sb[:, c, :]),
                   reads=[SK(c), "sva%d" % c], writes=["sia%d" % c])
            for c in range(16):
                op("dve", lambda e, c=c: e.match_replace(out=wk[:, c, :], in_to_replace=sv[:, c, 0:8], in_values=s_sb[:, c, :], imm_value=-1e30),
                   reads=[SK(c), "sva%d" % c], writes=["wk"] if c == 0 else ["wk_%d" % c])
            for c in range(16):
                op("dve", lambda e, c=c: e.max(out=sv[:, c, 8:16], in_=wk[:, c, :]), reads=["wk"] if c == 0 else ["wk_%d" % c], writes=["svb%d" % c])
            for c in range(16):
                op("dve", lambda e, c=c: e.max_index(out=si[:, c, 8:16], in_max=sv[:, c, 8:16], in_values=wk[:, c, :]),
                   reads=(["wk"] if c == 0 else ["wk_%d" % c]) + ["svb%d" % c], writes=["sib%d" % c])
            ALLSV = ["sva%d" % c for c in range(16)] + ["svb%d" % c for c in range(16)]
            ALLSI = ["sia%d" % c for c in range(16)] + ["sib%d" % c for c in range(16)]
            ALLWK = ["wk"] + ["wk_%d" % c for c in range(1, 16)]
            op("dve", lambda e: e.tensor_copy(out=sif[:], in_=si[:]), reads=ALLSI, writes=["sif"])
            op("dve", lambda e: e.tensor_tensor(out=cand4, in0=sv4[:, :, 0, :].unsqueeze(3).to_broadcast([128, 8, 16, 16]),
                                                in1=sv4[:, :, 1, :].unsqueeze(2).to_broadcast([128, 8, 16, 16]), op=ALU.add),
               reads=ALLSV, writes=["cand"] + ["cand_%d" % h for h in range(8)])
            for h in range(8):
                op("dve", lambda e, h=h: e.max(out=fv[:, h, 0:8], in_=cand[:, h, :]), reads=["cand"], writes=["fva%d" % h])
            for h in range(8):
                op("dve", lambda e, h=h: e.max_index(out=fp[:, h, 0:8], in_max=fv[:, h, 0:8], in_values=cand[:, h, :]),
                   reads=["cand", "fva%d" % h], writes=["fpa%d" % h])
            for h in range(8):
                op("dve", lambda e, h=h: e.match_replace(out=cand[:, h, :], in_to_replace=fv[:, h, 0:8], in_values=cand[:, h, :], imm_value=-1e30),
                   reads=["cand", "fva%d" % h, "fpa%d" % h], writes=["cand_%d" % h])
            for h in range(8):
                op("dve", lambda e, h=h: e.max(out=fv[:, h, 8:16], in_=cand[:, h, :]), reads=["cand_%d" % h], writes=["fvb%d" % h])
            for h in range(8):
                op("dve", lambda e, h=h: e.max_index(out=fp[:, h, 8:16], in_max=fv[:, h, 8:16], in_values=cand[:, h, :]),
                   reads=["cand_%d" % h, "fvb%d" % h], writes=["fpb%d" % h])
            ALLFV = ["fva%d" % h for h in range(8)] + ["fvb%d" % h for h in range(8)]
            ALLFP = ["fpa%d" % h for h in range(8)] + ["fpb%d" % h for h in range(8)]
            ALLCAND = ["cand"] + ["cand_%d" % h for h in range(8)]
            op("dve", lambda e: e.tensor_copy(out=fpf[:], in_=fp[:].rearrange("p h k -> p (h k)")), reads=ALLFP, writes=["fpf"])
            op("dve", lambda e: e.tensor_tensor(out=cmpT, in0=fpf[:].unsqueeze(2).to_broadcast([128, 128, 15]),
                                                in1=thr15[:].unsqueeze(1).to_broadcast([128, 128, 15]), op=ALU.is_ge),
               reads=["fpf", "thr15"], writes=ALLWK)
            op("dve", lambda e: e.tensor_reduce(out=fa[:], in_=cmpT, axis=AX.X, op=ALU.add), reads=ALLWK, writes=["fa"])
            op("dve", lambda e: e.scalar_tensor_tensor(out=fb[:], in0=fa[:], scalar=-16.0, in1=fpf[:], op0=ALU.mult, op1=ALU.add),
               reads=["fa", "fpf"], writes=["fb"])
            for (fx3, half, dst, dkey) in ((fa3, 0, ia3, "ia"), (fb3, 1, ib3, "ib")):
                op("dve", lambda e, fx3=fx3: e.tensor_tensor(out=wk4, in0=fx3.unsqueeze(3).to_broadcast([128, 8, 16, 16]),
                                                           in1=iota16[:].unsqueeze(1).unsqueeze(1).to_broadcast([128, 8, 16, 16]), op=ALU.is_equal),
                   reads=["fa", "fb", "iota16"], writes=ALLWK)
                op("dve", lambda e, half=half: e.tensor_tensor(out=cand4, in0=wk4,
                                                             in1=sif4[:, :, half, :].unsqueeze(2).to_broadcast([128, 8, 16, 16]), op=ALU.mult),
                   reads=ALLWK + ["sif"], writes=ALLCAND)
                op("dve", lambda e, dst=dst: e.tensor_reduce(out=dst, in_=cand4, axis=AX.X, op=ALU.add), reads=ALLCAND, writes=[dkey])
            op("dve", lambda e: e.scalar_tensor_tensor(out=ia[:], in0=ia[:], scalar=128.0, in1=ib[:], op0=ALU.mult, op1=ALU.add),
               reads=["ia", "ib"], writes=["ia"])
            if "ia_all" in dbg:
                if i == 0:
                    ia_d = nc.dram_tensor("dbg_ia_all", [NT, 128, 128], F32, kind="ExternalOutput").ap()
                    fp_d = nc.dram_tensor("dbg_fp_all", [NT, 128, 128], F32, kind="ExternalOutput").ap()
                    sif_d = nc.dram_tensor("dbg_sif_all", [NT, 128, 256], F32, kind="ExternalOutput").ap()
                dma("sync", lambda e, i=i: e.dma_start(out=ia_d[i], in_=ia[:]), reads=["ia"])
                dma("sync", lambda e, i=i: e.dma_start(out=fp_d[i], in_=fpf[:]), reads=["fpf"])
                dma("sync", lambda e, i=i: e.dma_start(out=sif_d[i], in_=sif[:].rearrange("p c a -> p (c a)")), reads=["sif"])
            op("dve", lambda e: e.tensor_scalar(out=ia[:], in0=ia[:], scalar1=0.0, scalar2=16383.0, op0=ALU.max, op1=ALU.min),
               reads=["ia"], writes=["ia"])
            op("dve", lambda e: e.tensor_copy(out=eidx[:], in_=ia[:]), reads=["ia"], writes=["eidx"])
            op("dve", lambda e: e.tensor_tensor(out=gt[:], in0=fv[:], in1=fv[:, :, 0:1].to_broadcast([128, 8, 16]), op=ALU.subtract),
               reads=ALLFV, writes=["gt"])
            op("act", lambda e: e.activation(out=gt[:], in_=gt[:], func=AF.Exp), reads=["gt"], writes=["gt"])
            op("dve", lambda e: e.tensor_reduce(out=zs[:, 0:8], in_=gt[:], axis=AX.X, op=ALU.add), reads=["gt"], writes=["zs"])
            op("dve", lambda e: e.reciprocal(out=zs[:, 8:16], in_=zs[:, 0:8]), reads=["zs"], writes=["zs"])
            op("dve", lambda e: e.tensor_tensor(out=gt[:], in0=gt[:], in1=zs[:, 8:16].unsqueeze(2).to_broadcast([128, 8, 16]), op=ALU.mult),
               reads=["gt", "zs"], writes=["gt"])
            if i == 0 and "eidx" in dbg:
                dbg_out("eidxf", ia[:], [128, 128], "ia")
                dbg_out("gates", gt[:].rearrange("p h k -> p (h k)"), [128, 128], "gt")
        def loop(i):
            h2b = h2bs[i % 2]
            h2bk = "h2b%d" % (i % 2)
            gflat = gt[:].rearrange("p h k -> p (h k)")

            def alias_keys(kname):
                if kname in ("s_sb", "qTb"):
                    return [kname] + ["%s%d" % (kname, c) for c in range(4)]
                if kname == "wk":
                    return ["wk"] + ["wk_%d" % c for c in range(1, 16)]
                return [kname]

            def stage_b(grp, gl):
                k = grp % NS
                sk2 = "ssm%d" % k
                op("dve", lambda e: e.tensor_tensor(out=ssm[k][:, 12:16], in0=ssm[k][:, 8:12], in1=gflat[:, grp * 4:grp * 4 + 4], op=ALU.mult),
                   reads=[sk2, "gt"], writes=[sk2])
                for j in range(4):
                    slot = grp * 4 + j
                    g = gl[j]
                    d = (grp % 2) * 4 + j
                    dk, gk = "dgb%d" % d, ring_k[g]
                    op("act", lambda e, d=d, j=j: e.activation(out=dgb[d][:], in_=ident_b[:], func=AF.Copy, scale=ssm[k][:, 12 + j:13 + j]),
                       reads=[sk2, "ident_b"], writes=[dk])
                    for hb in range(2):
                        op("pe", lambda e, g=g, d=d, hb=hb, slot=slot: e.matmul(psF[4 + hb][:, :], lhsT=dgb[d][:], rhs=ring_b[g][:, D + hb * 512:D + (hb + 1) * 512],
                                                                              start=(slot == 0), stop=(slot == 127)),
                           reads=[dk] + alias_keys(gk), writes=["ps%d" % (4 + hb)])

            ring_b = gbuf + xal[0:i]
            ring_k = gkey + ["x1_%d" % t for t in range(i)]
            nring = len(ring_b)
            gn[0] = 0

            def gathers(grp):
                gl = []
                for j in range(4):
                    slot = grp * 4 + j
                    g = gn[0] % nring
                    gn[0] += 1
                    gl.append(g)
                    dma("pool", lambda e, g=g, slot=slot: e.indirect_dma_start(
                        out=ring_b[g][:], out_offset=None, in_=uvb_d, in_offset=bass.IndirectOffsetOnAxis(ap=eidx[:, slot:slot + 1], axis=0)),
                        reads=["eidx", "gt"], writes=alias_keys(ring_k[g]))
                return gl

            gls = {}
            for grp in range(32):
                k = grp % NS
                sk2 = "ssm%d" % k
                gls[grp] = gathers(grp)
                if grp >= 1:
                    stage_b(grp - 1, gls[grp - 1])
                gl = gls[grp]
                for j in range(4):
                    g = gl[j]
                    op("dve", lambda e, g=g, k=k, j=j: e.scalar_tensor_tensor(out=pjunk[:], in0=ring_b[g][:, 0:D], scalar=1.0, in1=h2b[:], op0=ALU.mult, op1=ALU.mult,
                                                                            accum_out=ssm[k][:, j:j + 1]),
                       reads=alias_keys(ring_k[g]) + [h2bk], writes=PJK + [sk2])
                op("dve", lambda e, k=k: e.tensor_copy(out=ssm[k][:, 4:8], in_=ssm[k][:, 0:4]), reads=[sk2], writes=[sk2])
                op("act", lambda e, k=k: e.activation(out=ssm[k][:, 8:12], in_=ssm[k][:, 4:8], func=AF.Gelu), reads=[sk2], writes=[sk2])
            stage_b(31, gls[31])
        def epilogue(i):
            xi = x1[:, i, :]
            xk = "x1_%d" % i
            for hb in range(2):
                op("dve", lambda e, hb=hb: e.tensor_tensor(out=pacc[:, hb * 512:(hb + 1) * 512], in0=psF[4 + hb][:, :], in1=GT2b[:, hb * 512:(hb + 1) * 512], op=ALU.mult),
                   reads=["ps%d" % (4 + hb), "GT2b"], writes=["pacc"])
            op("dve", lambda e, xi=xi: e.tensor_tensor(out=pacc[:], in0=pacc[:], in1=xi, op=ALU.add), reads=["pacc", xk], writes=["pacc"])
            op("act", lambda e: e.activation(out=pjunk[:], in_=pacc[:], func=AF.Square, accum_out=psm[:, 2:3]),
               reads=["pacc"], writes=PJK + ["psmB"])
            rstd_of(psm[:, 2:3], psm[:, 3:4], D, "psmB", "psmB")
            op("dve", lambda e: e.scalar_tensor_tensor(out=pacc[:], in0=pacc[:], scalar=psm[:, 3:4], in1=FGb[:], op0=ALU.mult, op1=ALU.mult),
               reads=["pacc", "psmB", "FGb"], writes=["pacc"])
            dma("sync", lambda e, i=i: e.dma_start(out=out_d[i * 128:(i + 1) * 128, :], in_=pacc[:]), reads=["pacc"])

        ntile = 1 if stage == 6 else min(NT, peer_tiles)
        front(0)
        topk(0)
        for i in range(ntile):
            if i + 1 < ntile:
                front(i + 1)
            loop(i)
            epilogue(i)
            if i + 1 < ntile:
                topk(i + 1)
        fw.wait_all("sync")
    return nc, dbg_d


def _tag(a, b):
    pad = np.full((1,) + a.shape[1:], float(b), dtype=a.dtype)
    return np.ascontiguousarray(np.concatenate([a, pad], axis=0))


def _prep_inputs(inp, b):
    f = lambda a: np.ascontiguousarray(a, dtype=np.float32)
    sel4 = np.zeros((4, 512), np.float32)
    for h in range(4):
        sel4[h, h * 128:(h + 1) * 128] = 1.0
    m = {
        "x": f(inp["x"][b]),
        "cT": f(inp["c"][b].reshape(KC, 128).T),
        "ada_w": f(inp["ada_w"][0]),
        "ada_bT": f(inp["ada_b"][0].reshape(48, 128).T),
        "n1gT": f(inp["norm1_g"][0].reshape(KC, 128).T),
        "n2gT": f(inp["norm2_g"][0].reshape(KC, 128).T),
        "final_g": f(inp["final_g"].reshape(1, D)),
        "w_in": _tag(f(inp["w_in"][0]), b),
        "conv_wT": f(inp["conv_w"][0].reshape(4, 8, 128).transpose(2, 1, 0)),
        "conv_bT": f(inp["conv_b"][0].reshape(8, 128).T),
        "gate_b": f(inp["mlstm_gate_b"][0].reshape(2, 4).T),
        "mnorm_g": f(inp["mlstm_norm_g"][0].reshape(1, 512)),
        "lamv": f(np.concatenate([inp["lambda_q1"][0], inp["lambda_k1"][0], inp["lambda_q2"][0], inp["lambda_k2"][0]]).reshape(1, 256)),
        "dnorm_g": f(inp["diff_norm_g"][0].reshape(1, 128)),
        "w_out": _tag(f(inp["w_out"][0]), b),
        "w_query": _tag(f(inp["peer_w_query"][0]), b),
        "keysT": _tag(f(inp["peer_sub_keys"][0].reshape(16, 128, 128).transpose(2, 0, 1)), b),
        "peer_uv": _tag(np.concatenate([f(inp["peer_u"][0]), f(inp["peer_v"][0])], axis=1), b),
        "ident": np.eye(128, dtype=np.float32),
        "tri": np.triu(np.ones((128, 128), np.float32)),
        "sel4": sel4,
        "iota16": np.arange(16, dtype=np.float32).reshape(1, 16),
    }
    return m


def kernel(**inputs):
    nc, _ = build()
    shared = None
    in_maps = []
    for b in range(8):
        m = _prep_inputs(inputs, b)
        if shared is None:
            shared = m
        else:
            for k in m:
                if k not in ("x", "cT") and k not in TAGGED:
                    m[k] = shared[k]
        in_maps.append(m)
    res = run_bass_kernel_spmd(nc, in_maps, core_ids=list(range(8)))
    out = np.stack([np.asarray(r["out"], dtype=np.float32) for r in res.results], axis=0)
    return out.reshape(8, S, D)
```

```python
import math
from contextlib import ExitStack

import numpy as np
import concourse.bass as bass
import concourse.mybir as mybir
from concourse.bass_utils import run_bass_kernel_spmd

F32 = mybir.dt.float32
BF16 = mybir.dt.bfloat16
U32 = mybir.dt.uint32
I32 = mybir.dt.int32
AF = mybir.ActivationFunctionType
ALU = mybir.AluOpType
AX = mybir.AxisListType

S = 2048
D = 1024
NT = 16
KC = 8
N_IN = 3592
EPS = 1e-6
LAM_INIT = 0.8 - 0.6 * math.exp(0.0)
TAGGED = ("w_in", "w_out", "w_query", "keysT", "peer_uv")


class Fw:
    ENGS = ("sync", "act", "dve", "pool", "pe")

    def __init__(self, nc, stack, n_dma_sems=12, same_engine_sync=True):
        self.nc = nc
        self.e = {"sync": nc.sync, "act": nc.scalar, "dve": nc.vector,
                  "pool": nc.gpsimd, "pe": nc.tensor}
        self.sem = {k: stack.enter_context(nc.semaphore("s_" + k)) for k in self.ENGS}
        self.cnt = {k: 0 for k in self.ENGS}
        self.pending = {k: False for k in self.ENGS}
        self.known = {k: {} for k in self.ENGS}
        self.last_w = {}
        self.readers = {}
        self.same = same_engine_sync
        self.dma_pool = {}
        for q in ("sync", "pool"):
            sems = [stack.enter_context(nc.semaphore("d_%s_%d" % (q, i))) for i in range(n_dma_sems)]
            self.dma_pool[q] = {"sems": sems, "val": [0] * n_dma_sems, "next": 0}
        self.semobj = {}
        for k in self.ENGS:
            self.semobj[("e", k)] = self.sem[k]
        for q, d in self.dma_pool.items():
            for i, s in enumerate(d["sems"]):
                self.semobj[("d", q, i)] = s
        self.n_ins = {k: 0 for k in self.ENGS}

    def _deps(self, reads, writes):
        need = {}

        def add(ev):
            if ev is None:
                return
            k, v = ev
            if need.get(k, 0) < v:
                need[k] = v

        for r in reads:
            add(self.last_w.get(r))
        for w in writes:
            add(self.last_w.get(w))
            for ev in self.readers.get(w, ()):
                add(ev)
        return need

    def _wait(self, eng, need):
        for k, v in need.items():
            if k == ("e", eng) and (eng == "pe" or not self.same):
                continue
            if self.known[eng].get(k, 0) >= v:
                continue
            self.e[eng].wait_ge(self.semobj[k], v)
            self.known[eng][k] = v

    def _record(self, ev, reads, writes):
        for r in reads:
            lst = self.readers.setdefault(r, [])
            lst.append(ev)
            if len(lst) > 64:
                best = {}
                for k, v in lst:
                    if best.get(k, 0) < v:
                        best[k] = v
                self.readers[r] = list(best.items())
        for w in writes:
            self.last_w[w] = ev
            self.readers[w] = []

    def op(self, eng, fn, reads=(), writes=(), signal=True):
        ps_r = [k for k in reads if isinstance(k, str) and k.startswith("ps")]
        if ps_r:
            writes = list(writes) + ps_r
        need = self._deps(reads, writes)
        self._wait(eng, need)
        ins = fn(self.e[eng])
        self.n_ins[eng] += 1
        if signal:
            self.cnt[eng] += 1
            ins.then_inc(self.sem[eng], 1)
            ev = (("e", eng), self.cnt[eng])
            self.pending[eng] = False
        else:
            ev = (("e", eng), self.cnt[eng] + 1)
            self.pending[eng] = True
        self._record(ev, reads, writes)
        return ev

    def dma(self, q, fn, reads=(), writes=()):
        need = self._deps(reads, writes)
        pool = self.dma_pool[q]
        i = pool["next"]
        pool["next"] = (i + 1) % len(pool["sems"])
        key = ("d", q, i)
        if pool["val"][i] > 0 and need.get(key, 0) < pool["val"][i]:
            need[key] = pool["val"][i]
        self._wait(q, need)
        ins = fn(self.e[q])
        pool["val"][i] += 16
        ins.then_inc(pool["sems"][i], 16)
        ev = (key, pool["val"][i])
        self.n_ins[q] += 1
        self._record(ev, reads, writes)
        return ev

    def _all_events(self):
        need = {}
        for k in self.ENGS:
            assert not self.pending[k], "engine %s has an unsignalled tail" % k
            if self.cnt[k] > 0:
                need[("e", k)] = self.cnt[k]
        for q, d in self.dma_pool.items():
            for i, v in enumerate(d["val"]):
                if v > 0:
                    need[("d", q, i)] = v
        return need

    def wait_all(self, eng):
        need = self._all_events()
        for k, v in need.items():
            if k == ("e", eng) and (eng in ("pe", "sync") or not self.same):
                continue
            if self.known[eng].get(k, 0) >= v:
                continue
            self.e[eng].wait_ge(self.semobj[k], v)
            self.known[eng][k] = v

    def barrier(self):
        for eng in self.ENGS:
            self.wait_all(eng)
        self.last_w = {}
        self.readers = {}


class Arena:
    def __init__(self, nc, base=16640, limit=229376 - 256):
        self.nc = nc
        self.top = base
        self.limit = limit
        self.n = 0
        self.peak = 0

    def alloc(self, name, shape, dt):
        sz = {F32: 4, BF16: 2, U32: 4, I32: 4}[dt]
        nb = sz
        for s in shape[1:]:
            nb *= s
        nb = (nb + 63) // 64 * 64
        off = self.top
        self.top += nb
        self.peak = max(self.peak, self.top)
        assert self.top <= self.limit, "SBUF arena overflow at %s: %d" % (name, self.top)
        self.n += 1
        return self.nc.alloc_sbuf_tensor_at("%s_%d" % (name, self.n), list(shape), dt, offset=off)

    def mark(self):
        return self.top

    def release(self, m):
        self.top = m


def build(stage=99, dbg=(), peer_tiles=NT):
    nc = bass.Bass("TRN2", target_bir_lowering=False)
    dram = {}

    def din(name, shape, dt=F32):
        dram[name] = nc.dram_tensor(name, list(shape), dt, kind="ExternalInput").ap()
        return dram[name]

    x_d = din("x", [S, D])
    cT_d = din("cT", [128, KC])
    adaw_d = din("ada_w", [D, 6 * D])
    adab_d = din("ada_bT", [128, 48])
    n1g_d = din("n1gT", [128, KC])
    n2g_d = din("n2gT", [128, KC])
    fg_d = din("final_g", [1, D])
    win_d = din("w_in", [D + 1, N_IN])[0:D, :]
    cw_d = din("conv_wT", [128, 8, 4])
    cb_d = din("conv_bT", [128, 8])
    gb_d = din("gate_b", [4, 2])
    mng_d = din("mnorm_g", [1, 512])
    lamv_d = din("lamv", [1, 256])
    dng_d = din("dnorm_g", [1, 128])
    wout_d = din("w_out", [D + 1, D])[0:D, :]
    wq_d = din("w_query", [D + 1, 2048])[0:D, :]
    keys_d = din("keysT", [129, 16, 128])[0:128]
    uv_d = din("peer_uv", [16385, 2 * D])[0:16384, :]
    ident_d = din("ident", [128, 128])
    tri_d = din("tri", [128, 128])
    sel4_d = din("sel4", [4, 512])
    iota16_d = din("iota16", [1, 16])
    out_d = nc.dram_tensor("out", [S, D], F32, kind="ExternalOutput").ap()
    uvb_d = nc.dram_tensor("uvb", [16384, 2 * D], BF16, kind="Internal").ap()
    dbg_d = {}

    with ExitStack() as st:
        fw = Fw(nc, st)
        ar = Arena(nc)
        op, dma = fw.op, fw.dma

        psF = [st.enter_context(nc.psum_tensor("psF%d" % i, [128, 512], F32)) for i in range(6)]
        psT = [st.enter_context(nc.psum_tensor("psT%d" % i, [128, 1024], BF16)) for i in range(2)]

        def dbg_out(name, tensor_ap, shape, key):
            if name not in dbg:
                return
            d = nc.dram_tensor("dbg_" + name, list(shape), F32, kind="ExternalOutput").ap()
            dbg_d[name] = d
            dma("sync", lambda e: e.dma_start(out=d, in_=tensor_ap), reads=[key])

        ident_f = ar.alloc("ident_f", [128, 128], F32)
        ident_b = ar.alloc("ident_b", [128, 128], BF16)
        tri_f = ar.alloc("tri_f", [128, 128], F32)
        tri_b = ar.alloc("tri_b", [128, 128], BF16)
        ones_f = ar.alloc("ones_f", [128, 128], F32)
        sel4 = ar.alloc("sel4", [4, 512], F32)
        mod_fm = ar.alloc("mod_fm", [128, 48], F32)
        g1_fm = ar.alloc("g1_fm", [128, KC], F32)
        g2_fm = ar.alloc("g2_fm", [128, KC], F32)
        small = ar.alloc("small", [128, 64], F32)
        dma("sync", lambda e: e.dma_start(out=ident_f[:], in_=ident_d), writes=["ident_f"])
        dma("sync", lambda e: e.dma_start(out=tri_f[:], in_=tri_d), writes=["tri_f"])
        dma("sync", lambda e: e.dma_start(out=sel4[:], in_=sel4_d), writes=["sel4"])
        op("dve", lambda e: e.tensor_copy(out=ident_b[:], in_=ident_f[:]), reads=["ident_f"], writes=["ident_b"])
        op("dve", lambda e: e.tensor_copy(out=tri_b[:], in_=tri_f[:]), reads=["tri_f"], writes=["tri_b"])
        op("dve", lambda e: e.memset(ones_f[:], 1.0), writes=["ones_f"])

        m0 = ar.mark()
        cT = ar.alloc("cT", [128, KC], F32)
        sc2 = ar.alloc("silu_c2", [128, KC, 2], F32)
        adab = ar.alloc("adab", [128, 48], F32)
        n1g = ar.alloc("n1g", [128, KC], F32)
        n2g = ar.alloc("n2g", [128, KC], F32)
        awb = [ar.alloc("aw%d" % i, [128, KC, 512], F32) for i in range(2)]
        dma("sync", lambda e: e.dma_start(out=cT[:], in_=cT_d), writes=["cT"])
        dma("sync", lambda e: e.dma_start(out=adab[:], in_=adab_d), writes=["adab"])
        dma("sync", lambda e: e.dma_start(out=n1g[:], in_=n1g_d), writes=["n1g"])
        dma("sync", lambda e: e.dma_start(out=n2g[:], in_=n2g_d), writes=["n2g"])
        for j in range(2):
            op("act", lambda e, j=j: e.activation(out=sc2[:, :, j], in_=cT[:], func=AF.Silu),
               reads=["cT"], writes=["sc2"])
        ps_mod = psF[0]
        for blk in range(12):
            b = blk % 2
            for k4 in range(2):
                dma("sync", lambda e, blk=blk, b=b, k4=k4: e.dma_start(
                    out=awb[b][:, k4 * 4:(k4 + 1) * 4, :],
                    in_=adaw_d[k4 * 512:(k4 + 1) * 512, blk * 512:(blk + 1) * 512].rearrange("(kc p) n -> p kc n", p=128)),
                    writes=["aw%d" % b])
            for jj in range(4):
                j = blk * 4 + jj
                for kc in range(KC):
                    op("pe", lambda e, b=b, jj=jj, kc=kc, j=j: e.matmul(
                        ps_mod[:, 2 * j:2 * j + 2], lhsT=awb[b][:, kc, jj * 128:(jj + 1) * 128],
                        rhs=sc2[:, kc, :], start=(kc == 0), stop=(kc == KC - 1)),
                        reads=["aw%d" % b, "sc2"], writes=["ps0"], signal=(kc == KC - 1))
        op("dve", lambda e: e.tensor_tensor(
            out=mod_fm[:], in0=ps_mod[:, 0:96].rearrange("p (j two) -> p j two", two=2)[:, :, 0], in1=adab[:], op=ALU.add),
            reads=["ps0", "adab"], writes=["mod_fm"])
        op("dve", lambda e: e.scalar_tensor_tensor(out=g1_fm[:], in0=mod_fm[:, 8:16], scalar=1.0, in1=n1g[:],
                                                   op0=ALU.add, op1=ALU.mult),
           reads=["mod_fm", "n1g"], writes=["g1_fm"])
        op("dve", lambda e: e.scalar_tensor_tensor(out=g2_fm[:], in0=mod_fm[:, 32:40], scalar=1.0, in1=n2g[:],
                                                   op0=ALU.add, op1=ALU.mult),
           reads=["mod_fm", "n2g"], writes=["g2_fm"])
        dbg_out("mod_fm", mod_fm[:], [128, 48], "mod_fm")
        fw.barrier()
        ar.release(m0)

        bc_n = [0]
        dg = [ar.alloc("dg%d" % i, [128, 128], F32) for i in range(2)]

        def bcast(dst, dkey, src_ap, skey):
            for kc in range(KC):
                k = bc_n[0] % 2
                bc_n[0] += 1
                op("dve", lambda e, k=k, kc=kc: e.tensor_scalar(out=dg[k][:], in0=ident_f[:], scalar1=src_ap[:, kc:kc + 1],
                                                                scalar2=None, op0=ALU.mult),
                   reads=["ident_f", skey], writes=["dg%d" % k])
                bank = 4 + kc // 4
                op("pe", lambda e, k=k, kc=kc, bank=bank: e.matmul(
                    psF[bank][:, (kc % 4) * 128:(kc % 4 + 1) * 128], lhsT=ones_f[:], rhs=dg[k][:], start=True, stop=True),
                    reads=["ones_f", "dg%d" % k], writes=["ps%d" % bank])
            for hb in range(2):
                op("act", lambda e, hb=hb: e.copy(out=dst[:, hb * 512:(hb + 1) * 512], in_=psF[4 + hb][:]),
                   reads=["ps%d" % (4 + hb)], writes=[dkey])

        def rstd_of(ssq_ap, out_ap, n, key_in, key_out):
            op("act", lambda e: e.activation(out=out_ap, in_=ssq_ap, func=AF.Ln, scale=1.0 / n, bias=eps_col(out_ap)),
               reads=[key_in, "epsc"], writes=[key_out])
            op("act", lambda e: e.activation(out=out_ap, in_=out_ap, func=AF.Exp, scale=-0.5),
               reads=[key_out], writes=[key_out])

        epsc = ar.alloc("epsc", [128, 1], F32)
        op("dve", lambda e: e.memset(epsc[:], EPS), writes=["epsc"])

        def eps_col(out_ap):
            return epsc[0:out_ap.shape[0], 0:1]

        regA0 = ar.mark()
        actT = ar.alloc("actT", [128, KC, S], BF16)
        cvb = [ar.alloc("cvb%d" % i, [128, D], BF16) for i in range(2)]
        cv_n = [0]

        def convert_chunks(n):
            for _ in range(2 * n):
                c = cv_n[0]
                if c >= 256:
                    return
                cv_n[0] += 1
                k = c % 2
                r0, hf = (c // 2) * 128, (c % 2) * D
                dma("pool", lambda e, r0=r0, hf=hf, k=k: e.dma_start(out=cvb[k][:], in_=uv_d[r0:r0 + 128, hf:hf + D]), writes=["cvb%d" % k])
                dma("sync", lambda e, r0=r0, hf=hf, k=k: e.dma_start(out=uvb_d[r0:r0 + 128, hf:hf + D], in_=cvb[k][:]), reads=["cvb%d" % k])

        m1 = ar.mark()
        G1b = ar.alloc("G1b", [128, D], F32)
        SH1b = ar.alloc("SH1b", [128, D], F32)
        bcast(G1b, "G1b", g1_fm, "g1_fm")
        bcast(SH1b, "SH1b", mod_fm[:, 0:8], "mod_fm")
        dbg_out("G1b", G1b[:], [128, D], "G1b")
        dbg_out("SH1b", SH1b[:], [128, D], "SH1b")
        xb = [ar.alloc("xb%d" % i, [128, D], F32) for i in range(2)]
        junk = ar.alloc("junk", [128, D], BF16)
        t1 = ar.alloc("t1", [128, D], F32)
        hb16 = [ar.alloc("hb16_%d" % i, [128, D], BF16) for i in range(2)]
        ssq = ar.alloc("ssq", [128, NT], F32)
        rstd = ar.alloc("rstd", [128, NT], F32)
        for i in range(NT):
            b = i % 2
            dma("sync", lambda e, i=i, b=b: e.dma_start(out=xb[b][:], in_=x_d[i * 128:(i + 1) * 128, :]), writes=["xb%d" % b])
            op("act", lambda e, i=i, b=b: e.activation(out=junk[:], in_=xb[b][:], func=AF.Square, accum_out=ssq[:, i:i + 1]),
               reads=["xb%d" % b], writes=["junk", "ssq%d" % i])
            rstd_of(ssq[:, i:i + 1], rstd[:, i:i + 1], D, "ssq%d" % i, "rstd%d" % i)
            op("dve", lambda e, i=i, b=b: e.scalar_tensor_tensor(out=t1[:], in0=xb[b][:], scalar=rstd[:, i:i + 1], in1=G1b[:],
                                                                 op0=ALU.mult, op1=ALU.mult),
               reads=["xb%d" % b, "rstd%d" % i, "G1b"], writes=["t1"])
            op("pool", lambda e, b=b: e.tensor_tensor(out=hb16[b][:], in0=t1[:], in1=SH1b[:], op=ALU.add),
               reads=["t1", "SH1b"], writes=["hb16_%d" % b])
            pt = psT[i % 2]
            for kc in range(KC):
                op("pe", lambda e, kc=kc, b=b, pt=pt: e.transpose(pt[:, kc * 128:(kc + 1) * 128], hb16[b][:, kc * 128:(kc + 1) * 128], ident_b[:]),
                   reads=["hb16_%d" % b, "ident_b"], writes=["psT%d" % (i % 2)])
            op("act", lambda e, i=i, pt=pt: e.copy(out=actT[:, :, i * 128:(i + 1) * 128],
                                                   in_=pt[:].rearrange("p (kc t) -> p kc t", kc=KC)),
               reads=["psT%d" % (i % 2)], writes=["actT"])
        dbg_out("ssq", ssq[:], [128, NT], "ssq15")
        dbg_out("rstd", rstd[:], [128, NT], "rstd15")
        dbg_out("t1", t1[:], [128, D], "t1")
        if "hT" in dbg:
            hdbg = ar.alloc("hdbg", [128, KC, 128], F32)
            op("dve", lambda e: e.tensor_copy(out=hdbg[:], in_=actT[:, :, 128:256]), reads=["actT"], writes=["hdbg"])
            dbg_out("hT", hdbg[:], [128, KC, 128], "hdbg")
        fw.barrier()
        ar.release(m1)
        if stage <= 1:
            fw.wait_all("sync")
            return nc, dbg_d

        o_tok = ar.alloc("o_tok", [128, NT, D], BF16)
        gmn_b = ar.alloc("gmn_b", [128, 512], F32)
        dma("sync", lambda e: e.dma_start(out=gmn_b[:], in_=mng_d[0:1, :].partition_broadcast(128)), writes=["gmn_b"])
        m2 = ar.mark()
        qkT = ar.alloc("qkT", [128, 8, S], BF16)
        vm = ar.alloc("vm", [128, NT, 4, 129], BF16)
        og = ar.alloc("og", [128, NT, 512], BF16)
        gneg = ar.alloc("gneg", [4, S], F32)
        gi = ar.alloc("gi", [4, S], F32)
        gsp = ar.alloc("gsp", [4, S], F32)
        cw = ar.alloc("cw", [128, 8, 4], F32)
        cb = ar.alloc("cb", [128, 8], F32)
        gb = ar.alloc("gb", [4, 2], F32)
        ngbf = ar.alloc("ngbf", [4, 1], F32)
        dma("sync", lambda e: e.dma_start(out=cw[:], in_=cw_d), writes=["cw"])
        dma("sync", lambda e: e.dma_start(out=cb[:], in_=cb_d), writes=["cb"])
        dma("sync", lambda e: e.dma_start(out=gb[:], in_=gb_d), writes=["gb"])
        op("dve", lambda e: e.tensor_scalar(out=ngbf[:], in0=gb[:, 1:2], scalar1=-1.0, scalar2=None, op0=ALU.mult),
           reads=["gb"], writes=["ngbf"])
        op("pool", lambda e: e.memset(vm[:, :, :, 128:129], 1.0), writes=["vm_ones"])
        m2s = ar.mark()
        wbf = [ar.alloc("wbf%d" % i, [128, KC, 520], BF16) for i in range(2)]
        ub = [ar.alloc("ub%d" % i, [128, S + 3], F32) for i in range(2)]
        cacc = ar.alloc("cacc", [128, S], F32)
        for i in range(2):
            op("pool", lambda e, i=i: e.memset(ub[i][:, 0:3], 0.0), writes=["ub%d" % i])
        wn = [0]
        pcn = [0]

        def load_w(src_d, c0, n):
            b = wn[0] % 2
            wn[0] += 1
            for k4 in range(2):
                dma("pool", lambda e, k4=k4: e.dma_start(out=wbf[b][:, k4 * 4:(k4 + 1) * 4, 0:n],
                                                        in_=src_d[k4 * 512:(k4 + 1) * 512, c0:c0 + n].rearrange("(kc p) n -> p kc n", p=128)),
                    writes=["wbf%d" % b])
            return b

        def proj_fm(b, cc, m, evac, k0=0):
            convert_chunks(1)
            for tb in range(4):
                pi = pcn[0] % 4
                pcn[0] += 1
                for kc in range(KC):
                    op("pe", lambda e, kc=kc, tb=tb, pi=pi: e.matmul(
                        psF[pi][0:m, :], lhsT=wbf[b][:, kc, cc:cc + m], rhs=actT[:, kc, tb * 512:(tb + 1) * 512],
                        start=(kc == 0), stop=(kc == KC - 1)),
                        reads=["wbf%d" % b, "actT"], writes=["ps%d" % pi], signal=(kc == KC - 1))
                evac(tb, psF[pi], "ps%d" % pi)

        def proj_tm(b, evac):
            convert_chunks(2)
            for i in range(NT):
                pi = pcn[0] % 4
                pcn[0] += 1
                for kc in range(KC):
                    op("pe", lambda e, kc=kc, i=i, pi=pi: e.matmul(
                        psF[pi][:, :], lhsT=actT[:, kc, i * 128:(i + 1) * 128], rhs=wbf[b][:, kc, 0:512],
                        start=(kc == 0), stop=(kc == KC - 1)),
                        reads=["wbf%d" % b, "actT"], writes=["ps%d" % pi], signal=(kc == KC - 1))
                evac(i, psF[pi], "ps%d" % pi)

        for blk in range(2):
            b = load_w(win_d, blk * 512, 512)
            for cc in range(4):
                ch = blk * 4 + cc
                u = ub[ch % 2]
                ukey = "ub%d" % (ch % 2)
                proj_fm(b, cc * 128, 128, lambda tb, ps, pk, u=u, ukey=ukey: op(
                    "act", lambda e: e.copy(out=u[:, 3 + tb * 512:3 + (tb + 1) * 512], in_=ps[:, :]),
                    reads=[pk], writes=[ukey]))
                op("dve", lambda e, u=u, ch=ch: e.tensor_scalar(out=cacc[:], in0=u[:, 3:3 + S], scalar1=cw[:, ch, 3:4],
                                                               scalar2=None, op0=ALU.mult),
                   reads=[ukey, "cw"], writes=["cacc"])
                for j in (2, 1, 0):
                    op("dve", lambda e, u=u, ch=ch, j=j: e.scalar_tensor_tensor(
                        out=cacc[:], in0=u[:, j:j + S], scalar=cw[:, ch, j:j + 1], in1=cacc[:], op0=ALU.mult, op1=ALU.add),
                        reads=[ukey, "cw", "cacc"], writes=["cacc"])
                op("act", lambda e, ch=ch: e.activation(out=qkT[:, ch, :], in_=cacc[:], func=AF.Silu, bias=cb[:, ch:ch + 1]),
                   reads=["cacc", "cb"], writes=["qkT%d" % ch])
        b = load_w(win_d, 1024, 512)
        proj_tm(b, lambda i, ps, pk: op(
            "act", lambda e: e.copy(out=vm[:, i, :, 0:128], in_=ps[:, :].rearrange("p (h d) -> p h d", h=4)),
            reads=[pk], writes=["vm%d" % i]))
        b = load_w(win_d, 1536, 520)
        proj_tm(b, lambda i, ps, pk: op(
            "act", lambda e: e.activation(out=og[:, i, :], in_=ps[:, :], func=AF.Sigmoid),
            reads=[pk], writes=["og%d" % i]))
        proj_fm(b, 512, 4, lambda tb, ps, pk: op(
            "act", lambda e: e.activation(out=gi[:, tb * 512:(tb + 1) * 512], in_=ps[0:4, :], func=AF.Identity, bias=gb[:, 0:1]),
            reads=[pk, "gb"], writes=["gi"]))
        proj_fm(b, 516, 4, lambda tb, ps, pk: op(
            "act", lambda e: e.activation(out=gsp[:, tb * 512:(tb + 1) * 512], in_=ps[0:4, :], func=AF.Exp, scale=-1.0, bias=ngbf[:, 0:1]),
            reads=[pk, "ngbf"], writes=["gsp"]))
        op("act", lambda e: e.activation(out=gsp[:], in_=gsp[:], func=AF.Ln, scale=1.0, bias=1.0),
           reads=["gsp"], writes=["gsp"])
        if "gi" in dbg:
            dbg_out("gi", gi[:], [4, S], "gi")
            dbg_out("gsp", gsp[:], [4, S], "gsp")
        if "qkT" in dbg:
            qdbg = ar.alloc("qdbg", [128, 2, 512], F32)
            op("dve", lambda e: e.tensor_copy(out=qdbg[:, 0, :], in_=qkT[:, 1, 0:512]), reads=["qkT1"], writes=["qdbg"])
            op("dve", lambda e: e.tensor_copy(out=qdbg[:, 1, :], in_=qkT[:, 6, 1536:2048]), reads=["qkT6"], writes=["qdbg"])
            dbg_out("qkT", qdbg[:], [128, 2, 512], "qdbg")
        fw.barrier()
        ar.release(m2s)
        if stage <= 2:
            fw.wait_all("sync")
            return nc, dbg_d

        ones4 = ar.alloc("ones4", [4, S], F32)
        Bn = ar.alloc("Bn", [4, S], F32)
        a_colT = ar.alloc("a_colT", [128, NT, 4], F32)
        emtT = ar.alloc("emtT", [128, NT, 4], F32)
        negA_b = [ar.alloc("negA_b0", [128, S], F32)] * 2
        Dt = [ar.alloc("Dt%d" % i, [128, 512], F32) for i in range(2)]
        Pt = [ar.alloc("Pt%d" % i, [128, 512], BF16) for i in range(3)]
        fsm = [ar.alloc("fsm%d" % i, [128, 8], F32) for i in range(2)]
        hh = [ar.alloc("hh%d" % i, [128, 128], F32) for i in range(2)]
        o1 = [ar.alloc("o1_%d" % i, [128, 128], F32) for i in range(2)]
        junk128 = ar.alloc("junk128", [128, 128], BF16)
        op("dve", lambda e: e.memset(ones4[:], 1.0), writes=["ones4"])
        op("dve", lambda e: e.tensor_tensor_scan(out=Bn[:], data0=ones4[:], data1=gsp[:], initial=0.0, op0=ALU.mult, op1=ALU.add),
           reads=["ones4", "gsp"], writes=["Bn"])
        op("dve", lambda e: e.tensor_tensor(out=gi[:], in0=gi[:], in1=Bn[:], op=ALU.add), reads=["gi", "Bn"], writes=["gi"])
        op("dve", lambda e: e.tensor_tensor_scan(out=gsp[:], data0=ones4[:], data1=gi[:], initial=0.0, op0=ALU.mult, op1=ALU.max),
           reads=["ones4", "gi"], writes=["gsp"])
        op("dve", lambda e: e.tensor_scalar(out=gneg[:], in0=gsp[:], scalar1=-1.0, scalar2=None, op0=ALU.mult),
           reads=["gsp"], writes=["gneg"])
        op("dve", lambda e: e.tensor_tensor(out=Bn[:], in0=Bn[:], in1=gneg[:], op=ALU.add), reads=["Bn", "gneg"], writes=["Bn"])
        op("act", lambda e: e.activation(out=Bn[:], in_=Bn[:], func=AF.Exp), reads=["Bn"], writes=["Bn"])
        for j in range(NT):
            op("pe", lambda e, j=j: e.matmul(psF[4][:, j * 4:(j + 1) * 4], lhsT=gi[0:4, j * 128:(j + 1) * 128], rhs=ident_f[0:4, 0:4],
                                             start=True, stop=True), reads=["gi", "ident_f"], writes=["ps4"])
            op("pe", lambda e, j=j: e.matmul(psF[5][:, j * 4:(j + 1) * 4], lhsT=Bn[0:4, j * 128:(j + 1) * 128], rhs=ident_f[0:4, 0:4],
                                             start=True, stop=True), reads=["Bn", "ident_f"], writes=["ps5"])
        op("dve", lambda e: e.tensor_scalar(out=a_colT[:], in0=psF[4][:, 0:64].rearrange("p (j h) -> p j h", h=4),
                                            scalar1=float(math.log(128.0 ** -0.5)), scalar2=None, op0=ALU.add),
           reads=["ps4"], writes=["a_colT"])
        op("dve", lambda e: e.tensor_copy(out=emtT[:], in_=psF[5][:, 0:64].rearrange("p (j h) -> p j h", h=4)),
           reads=["ps5"], writes=["emtT"])
        dbg_out("a_colT", a_colT[:], [128, NT, 4], "a_colT")
        dbg_out("emtT", emtT[:], [128, NT, 4], "emtT")

        fin_n = [0]
        if "acc3" in dbg:
            acc3d = ar.alloc("acc3d", [128, 8, 129], F32)
        for h in range(4):
            nb = negA_b[h % 2]
            nbk = "negA_b0"
            for tb in range(4):
                op("pe", lambda e, h=h, tb=tb: e.matmul(psF[4 + tb % 2][:, :], lhsT=sel4[0:4, h * 128:(h + 1) * 128],
                                                        rhs=gneg[0:4, tb * 512:(tb + 1) * 512], start=True, stop=True),
                   reads=["sel4", "gneg"], writes=["ps%d" % (4 + tb % 2)])
                op("act", lambda e, tb=tb, nb=nb: e.copy(out=nb[:, tb * 512:(tb + 1) * 512], in_=psF[4 + tb % 2][:, :]),
                   reads=["ps%d" % (4 + tb % 2)], writes=[nbk])
            qh = qkT[:, h, :]
            kh = qkT[:, 4 + h, :]
            for tb in range(4):
                jmax = 4 * tb + 3
                convert_chunks(3)

                def s_mm(j, h=h, tb=tb, qh=qh, kh=kh):
                    op("pe", lambda e: e.matmul(psF[j % 2][:, :], lhsT=kh[:, j * 128:(j + 1) * 128], rhs=qh[:, tb * 512:(tb + 1) * 512],
                                                start=True, stop=True),
                       reads=["qkT%d" % h, "qkT%d" % (4 + h)], writes=["ps%d" % (j % 2)])

                s_mm(0)
                for j in range(jmax + 1):
                    if j + 1 <= jmax:
                        s_mm(j + 1)
                    d = Dt[j % 2]
                    p = Pt[j % 3]
                    pk = "Pt%d" % (j % 3)
                    op("act", lambda e, j=j, d=d, nb=nb: e.activation(out=d[:], in_=nb[:, tb * 512:(tb + 1) * 512], func=AF.Exp,
                                                                     bias=a_colT[:, j, h:h + 1]),
                       reads=[nbk, "a_colT"], writes=["Dt%d" % (j % 2)])
                    op("dve", lambda e, j=j, d=d, p=p: e.tensor_tensor(out=p[:], in0=psF[j % 2][:, :], in1=d[:], op=ALU.mult),
                       reads=["ps%d" % (j % 2), "Dt%d" % (j % 2)], writes=[pk])
                    if j >= 4 * tb:
                        li = j - 4 * tb
                        op("dve", lambda e, p=p, li=li: e.tensor_tensor(out=p[:, li * 128:(li + 1) * 128], in0=p[:, li * 128:(li + 1) * 128],
                                                                      in1=tri_b[:], op=ALU.mult),
                           reads=[pk, "tri_b"], writes=[pk])
                    for li in range(max(j - 4 * tb, 0), 4):
                        i = 4 * tb + li
                        acc = psF[2 + 2 * (tb % 2) + li // 2][:, (li % 2) * 256:(li % 2) * 256 + 129]
                        op("pe", lambda e, p=p, li=li, j=j, i=i, acc=acc: e.matmul(acc, lhsT=p[:, li * 128:(li + 1) * 128], rhs=vm[:, j, h, :],
                                                                                  start=(j == 0 and li % 2 == 0), stop=(j == i), skip_group_check=True),
                           reads=[pk, "vm%d" % j, "vm_ones"], writes=["ps%d" % (2 + 2 * (tb % 2) + li // 2)])
                for li in range(4):
                    i = 4 * tb + li
                    acc = psF[2 + 2 * (tb % 2) + li // 2][:, (li % 2) * 256:(li % 2) * 256 + 129]
                    k = fin_n[0] % 2
                    fin_n[0] += 1
                    sm, hk, ok_ = fsm[k], hh[k], o1[k]
                    smk, hkk, okk = "fsm%d" % k, "hh%d" % k, "o1_%d" % k
                    if "acc3" in dbg and h == 3 and i < 8:
                        op("dve", lambda e, acc=acc, i=i: e.tensor_copy(out=acc3d[:, i, :], in_=acc), reads=["ps%d" % (2 + 2 * (tb % 2) + li // 2)], writes=["acc3d"])
                    op("act", lambda e, acc=acc, sm=sm: e.activation(out=sm[:, 0:1], in_=acc[:, 128:129], func=AF.Abs),
                       reads=["ps%d" % (2 + 2 * (tb % 2) + li // 2)], writes=[smk])
                    op("dve", lambda e, sm=sm, i=i, h=h: e.tensor_tensor(out=sm[:, 1:2], in0=sm[:, 0:1], in1=emtT[:, i, h:h + 1], op=ALU.max),
                       reads=[smk, "emtT"], writes=[smk])
                    op("dve", lambda e, sm=sm: e.reciprocal(out=sm[:, 2:3], in_=sm[:, 1:2]), reads=[smk], writes=[smk])
                    op("dve", lambda e, sm=sm, acc=acc, hk=hk: e.tensor_scalar(out=hk[:], in0=acc[:, 0:128], scalar1=sm[:, 2:3], scalar2=None,
                                                                               op0=ALU.mult),
                       reads=["ps%d" % (2 + 2 * (tb % 2) + li // 2), smk], writes=[hkk])
                    op("act", lambda e, hk=hk, sm=sm: e.activation(out=junk128[:], in_=hk[:], func=AF.Square, accum_out=sm[:, 3:4]),
                       reads=[hkk], writes=["junk128", smk])
                    rstd_of(sm[:, 3:4], sm[:, 4:5], 128, smk, smk)
                    op("dve", lambda e, hk=hk, sm=sm, ok_=ok_, h=h: e.scalar_tensor_tensor(
                        out=ok_[:], in0=hk[:], scalar=sm[:, 4:5], in1=gmn_b[:, h * 128:(h + 1) * 128], op0=ALU.mult, op1=ALU.mult),
                        reads=[hkk, smk, "gmn_b"], writes=[okk])
                    op("dve", lambda e, ok_=ok_, i=i, h=h: e.tensor_tensor(out=o_tok[:, i, h * 128:(h + 1) * 128], in0=ok_[:],
                                                                          in1=og[:, i, h * 128:(h + 1) * 128], op=ALU.mult),
                       reads=[okk, "og%d" % i], writes=["o_tok%d" % i])
        if "acc3" in dbg:
            dbg_out("acc3", acc3d[:], [128, 8, 129], "acc3d")
        fw.barrier()
        ar.release(m2)
        if "hm" in dbg:
            hmd = ar.alloc("hmd", [128, NT, 512], F32)
            op("dve", lambda e: e.tensor_copy(out=hmd[:], in_=o_tok[:, :, 0:512]), reads=["o_tok%d" % i for i in range(NT)], writes=["hmd"])
            dbg_out("hm", hmd[:], [128, NT, 512], "hmd")
        if stage <= 3:
            fw.wait_all("sync")
            return nc, dbg_d

        m3 = ar.mark()
        dqk = ar.alloc("dqk", [128, 8, S], BF16)
        vd = ar.alloc("vd", [128, NT, 4, 129], BF16)
        gdn_b = ar.alloc("gdn_b", [128, 128], F32)
        lamv = ar.alloc("lamv", [128, 256], F32)
        lsm = ar.alloc("lsm", [128, 8], F32)
        ljunk = ar.alloc("ljunk", [128, 64], F32)
        dma("sync", lambda e: e.dma_start(out=gdn_b[:], in_=dng_d[0:1, :].partition_broadcast(128)), writes=["gdn_b"])
        dma("sync", lambda e: e.dma_start(out=lamv[:], in_=lamv_d[0:1, :].partition_broadcast(128)), writes=["lamv"])
        op("dve", lambda e: e.tensor_scalar(out=gdn_b[:], in0=gdn_b[:], scalar1=float(1.0 - LAM_INIT), scalar2=None, op0=ALU.mult),
           reads=["gdn_b"], writes=["gdn_b"])
        for t in range(2):
            op("dve", lambda e, t=t: e.scalar_tensor_tensor(out=ljunk[:], in0=lamv[:, t * 128:t * 128 + 64], scalar=1.0,
                                                          in1=lamv[:, t * 128 + 64:t * 128 + 128], op0=ALU.mult, op1=ALU.mult,
                                                          accum_out=lsm[:, t:t + 1]),
               reads=["lamv"], writes=["ljunk", "lsm"])
        op("dve", lambda e: e.tensor_copy(out=lsm[:, 6:8], in_=lsm[:, 0:2]), reads=["lsm"], writes=["lsm"])
        op("act", lambda e: e.activation(out=lsm[:, 2:4], in_=lsm[:, 6:8], func=AF.Exp), reads=["lsm"], writes=["lsm"])
        op("dve", lambda e: e.tensor_tensor(out=lsm[:, 4:5], in0=lsm[:, 3:4], in1=lsm[:, 2:3], op=ALU.subtract), reads=["lsm"], writes=["lsm"])
        op("dve", lambda e: e.tensor_scalar(out=lsm[:, 5:6], in0=lsm[:, 4:5], scalar1=float(-LAM_INIT), scalar2=None, op0=ALU.add),
           reads=["lsm"], writes=["lsm"])
        nlam = lsm[:, 5:6]
        op("pool", lambda e: e.memset(vd[:, :, :, 128:129], 1.0), writes=["vd_ones"])
        m3s = ar.mark()
        wbf = [ar.alloc("wbfd%d" % i, [128, KC, 512], BF16) for i in range(2)]
        for blk in range(2):
            b = load_w(win_d, 2056 + blk * 512, 512)
            for cc in range(4):
                ch = blk * 4 + cc
                proj_fm(b, cc * 128, 128, lambda tb, ps, pk, ch=ch: op(
                    "act", lambda e: e.copy(out=dqk[:, ch, tb * 512:(tb + 1) * 512], in_=ps[:, :]),
                    reads=[pk], writes=["dqk%d" % ch]))
        b = load_w(win_d, 3080, 512)
        proj_tm(b, lambda i, ps, pk: op(
            "act", lambda e: e.copy(out=vd[:, i, :, 0:128], in_=ps[:, :].rearrange("p (h d) -> p h d", h=4)),
            reads=[pk], writes=["vd%d" % i]))
        fw.barrier()
        ar.release(m3s)

        Et = [ar.alloc("Et%d" % i, [128, 512], BF16) for i in range(4)]
        fsm = [ar.alloc("dfsm%d" % i, [128, 8], F32) for i in range(2)]
        o0 = [ar.alloc("o0_%d" % i, [128, 128], F32) for i in range(2)]
        odf = [ar.alloc("odf%d" % i, [128, 128], F32) for i in range(2)]
        junk128 = ar.alloc("junk128d", [128, 128], BF16)
        fin_n = [0]
        en = [0]
        for h in range(4):
            for tb in range(4):
                jmax = 4 * tb + 3
                convert_chunks(3)
                steps = [(j, p) for j in range(jmax + 1) for p in range(2)]

                def s_mm(idx, h=h, tb=tb):
                    j, p = steps[idx]
                    op("pe", lambda e: e.matmul(psF[idx % 2][:, :], lhsT=dqk[p * 64:(p + 1) * 64, 4 + h, j * 128:(j + 1) * 128],
                                                rhs=dqk[p * 64:(p + 1) * 64, h, tb * 512:(tb + 1) * 512], start=True, stop=True),
                       reads=["dqk%d" % h, "dqk%d" % (4 + h)], writes=["ps%d" % (idx % 2)])

                s_mm(0)
                for idx, (j, p) in enumerate(steps):
                    if idx + 1 < len(steps):
                        s_mm(idx + 1)
                    ek = en[0] % 4
                    en[0] += 1
                    E = Et[ek]
                    ekey = "Et%d" % ek
                    op("act", lambda e, E=E, idx=idx: e.activation(out=E[:], in_=psF[idx % 2][:, :], func=AF.Exp, scale=0.125),
                       reads=["ps%d" % (idx % 2)], writes=[ekey])
                    if j >= 4 * tb:
                        li = j - 4 * tb
                        op("dve", lambda e, E=E, li=li: e.tensor_tensor(out=E[:, li * 128:(li + 1) * 128], in0=E[:, li * 128:(li + 1) * 128],
                                                                      in1=tri_b[:], op=ALU.mult),
                           reads=[ekey, "tri_b"], writes=[ekey])
                    for li in range(max(j - 4 * tb, 0), 4):
                        i = 4 * tb + li
                        bank = 2 + 2 * p + li // 2
                        acc = psF[bank][:, (li % 2) * 256:(li % 2) * 256 + 129]
                        op("pe", lambda e, E=E, li=li, j=j, i=i, acc=acc: e.matmul(acc, lhsT=E[:, li * 128:(li + 1) * 128], rhs=vd[:, j, h, :],
                                                                                  start=(j == 0 and li % 2 == 0), stop=(j == i), skip_group_check=True),
                           reads=[ekey, "vd%d" % j, "vd_ones"], writes=["ps%d" % bank])
                for li in range(4):
                    i = 4 * tb + li
                    a0 = psF[2 + li // 2][:, (li % 2) * 256:(li % 2) * 256 + 129]
                    a1 = psF[4 + li // 2][:, (li % 2) * 256:(li % 2) * 256 + 129]
                    k = fin_n[0] % 2
                    fin_n[0] += 1
                    sm, o0k, odk = fsm[k], o0[k], odf[k]
                    smk, o0kk, odkk = "dfsm%d" % k, "o0_%d" % k, "odf%d" % k
                    op("dve", lambda e, sm=sm, a0=a0: e.reciprocal(out=sm[:, 0:1], in_=a0[:, 128:129]), reads=["ps%d" % (2 + li // 2)], writes=[smk])
                    op("dve", lambda e, sm=sm, a1=a1: e.reciprocal(out=sm[:, 1:2], in_=a1[:, 128:129]), reads=["ps%d" % (4 + li // 2)], writes=[smk])
                    op("dve", lambda e, sm=sm: e.tensor_tensor(out=sm[:, 2:3], in0=sm[:, 1:2], in1=nlam, op=ALU.mult), reads=[smk, "lsm"], writes=[smk])
                    op("dve", lambda e, sm=sm, a0=a0, o0k=o0k: e.tensor_scalar(out=o0k[:], in0=a0[:, 0:128], scalar1=sm[:, 0:1], scalar2=None, op0=ALU.mult),
                       reads=["ps%d" % (2 + li // 2), smk], writes=[o0kk])
                    op("dve", lambda e, sm=sm, a1=a1, o0k=o0k, odk=odk: e.scalar_tensor_tensor(
                        out=odk[:], in0=a1[:, 0:128], scalar=sm[:, 2:3], in1=o0k[:], op0=ALU.mult, op1=ALU.add),
                        reads=["ps%d" % (4 + li // 2), smk, o0kk], writes=[odkk])
                    op("act", lambda e, odk=odk, sm=sm: e.activation(out=junk128[:], in_=odk[:], func=AF.Square, accum_out=sm[:, 3:4]),
                       reads=[odkk], writes=["junk128d", smk])
                    rstd_of(sm[:, 3:4], sm[:, 4:5], 128, smk, smk)
                    op("dve", lambda e, odk=odk, sm=sm, i=i, h=h: e.scalar_tensor_tensor(
                        out=o_tok[:, i, 512 + h * 128:512 + (h + 1) * 128], in0=odk[:], scalar=sm[:, 4:5], in1=gdn_b[:], op0=ALU.mult, op1=ALU.mult),
                        reads=[odkk, smk, "gdn_b"], writes=["o_tok%d" % i])
        fw.barrier()
        ar.release(m3)
        if "od" in dbg:
            odd = ar.alloc("odd", [128, NT, 512], F32)
            op("dve", lambda e: e.tensor_copy(out=odd[:], in_=o_tok[:, :, 512:1024]), reads=["o_tok%d" % i for i in range(NT)], writes=["odd"])
            dbg_out("od", odd[:], [128, NT, 512], "odd")
        if stage <= 4:
            fw.wait_all("sync")
            return nc, dbg_d

        regA1 = ar.mark()
        x1 = ar.alloc("x1", [128, NT, D], F32)
        peer_top = ar.mark()
        m4 = ar.mark()
        GT1b = ar.alloc("GT1b", [128, D], F32)
        bcast(GT1b, "GT1b", mod_fm[:, 16:24], "mod_fm")
        wo = ar.alloc("wo", [128, KC, D], BF16)
        for hb in range(2):
            for k4 in range(2):
                dma("pool", lambda e, hb=hb, k4=k4: e.dma_start(out=wo[:, k4 * 4:(k4 + 1) * 4, hb * 512:(hb + 1) * 512],
                                                               in_=wout_d[k4 * 512:(k4 + 1) * 512, hb * 512:(hb + 1) * 512].rearrange("(kc p) n -> p kc n", p=128)),
                    writes=["wo%d" % hb])
        xb2 = [ar.alloc("xb2_%d" % i, [128, D], F32) for i in range(2)]
        ytmp = [ar.alloc("ytmp%d" % i, [128, 512], F32) for i in range(2)]
        for i in range(NT):
            pt = psT[i % 2]
            for kc in range(KC):
                op("pe", lambda e, kc=kc, i=i, pt=pt: e.transpose(pt[:, kc * 128:(kc + 1) * 128], o_tok[:, i, kc * 128:(kc + 1) * 128], ident_b[:]),
                   reads=["o_tok%d" % i, "ident_b"], writes=["psT%d" % (i % 2)])
            op("act", lambda e, i=i, pt=pt: e.copy(out=actT[:, :, i * 128:(i + 1) * 128], in_=pt[:].rearrange("p (kc t) -> p kc t", kc=KC)),
               reads=["psT%d" % (i % 2)], writes=["actT%d" % i])
        convert_chunks(8)
        for i in range(NT):
            b = i % 2
            convert_chunks(1)
            dma("sync", lambda e, i=i, b=b: e.dma_start(out=xb2[b][:], in_=x_d[i * 128:(i + 1) * 128, :]), writes=["xb2_%d" % b])
            for hb in range(2):
                pi = (2 * i + hb) % 4
                for kc in range(KC):
                    op("pe", lambda e, kc=kc, i=i, hb=hb, pi=pi: e.matmul(psF[pi][:, :], lhsT=actT[:, kc, i * 128:(i + 1) * 128],
                                                                         rhs=wo[:, kc, hb * 512:(hb + 1) * 512], start=(kc == 0), stop=(kc == KC - 1)),
                       reads=["actT%d" % i, "wo%d" % hb], writes=["ps%d" % pi], signal=(kc == KC - 1))
                yt = ytmp[hb]
                op("dve", lambda e, pi=pi, hb=hb, yt=yt: e.tensor_tensor(out=yt[:], in0=psF[pi][:, :], in1=GT1b[:, hb * 512:(hb + 1) * 512], op=ALU.mult),
                   reads=["ps%d" % pi, "GT1b"], writes=["ytmp%d" % hb])
                op("dve", lambda e, i=i, hb=hb, b=b, yt=yt: e.tensor_tensor(out=x1[:, i, hb * 512:(hb + 1) * 512], in0=yt[:],
                                                                           in1=xb2[b][:, hb * 512:(hb + 1) * 512], op=ALU.add),
                   reads=["ytmp%d" % hb, "xb2_%d" % b], writes=["x1_%d" % i])
        convert_chunks(128)
        if "x1" in dbg:
            dbg_out("x1", x1[:], [128, NT, D], "x1_15")
        fw.barrier()
        ar.release(m4)
        if stage <= 5:
            fw.wait_all("sync")
            return nc, dbg_d

        arA = Arena(nc, base=regA0, limit=regA1)
        arA.n = 5000
        wq = arA.alloc("wq", [128, KC, 2048], BF16)
        keysT = arA.alloc("keysT", [128, 16, 128], BF16)
        G2b = arA.alloc("G2b", [128, D], F32)
        SH2b = arA.alloc("SH2b", [128, D], F32)
        GT2b = arA.alloc("GT2b", [128, D], F32)
        FGb = arA.alloc("FGb", [128, D], F32)
        m_ssb = arA.mark()
        s_sb = arA.alloc("s_sb", [128, 16, 128], F32)
        for qb in range(4):
            for k4 in range(2):
                dma("pool", lambda e, qb=qb, k4=k4: e.dma_start(out=wq[:, k4 * 4:(k4 + 1) * 4, qb * 512:(qb + 1) * 512],
                                                               in_=wq_d[k4 * 512:(k4 + 1) * 512, qb * 512:(qb + 1) * 512].rearrange("(kc p) n -> p kc n", p=128)),
                    writes=["wq"])
        for c4 in range(4):
            dma("pool", lambda e, c4=c4: e.dma_start(out=keysT[:, c4 * 4:(c4 + 1) * 4, :], in_=keys_d[:, c4 * 4:(c4 + 1) * 4, :]), writes=["keysT"])
        dma("sync", lambda e: e.dma_start(out=FGb[:], in_=fg_d[0:1, :].partition_broadcast(128)), writes=["FGb"])
        bcast(G2b, "G2b", g2_fm, "g2_fm")
        bcast(SH2b, "SH2b", mod_fm[:, 24:32], "mod_fm")
        bcast(GT2b, "GT2b", mod_fm[:, 40:48], "mod_fm")
        iota16 = ar.alloc("iota16", [128, 16], F32)
        thr15 = ar.alloc("thr15", [128, 15], F32)
        dma("sync", lambda e: e.dma_start(out=iota16[:], in_=iota16_d[0:1, :].partition_broadcast(128)), writes=["iota16"])
        op("dve", lambda e: e.tensor_scalar(out=thr15[:], in0=iota16[:, 0:15], scalar1=16.0, scalar2=16.0, op0=ALU.mult, op1=ALU.add),
           reads=["iota16"], writes=["thr15"])
        h2bs = [ar.alloc("h2b0", [128, D], BF16), arA.alloc("h2b1", [128, D], BF16)]
        h2T = ar.alloc("h2T", [128, KC, 128], BF16)
        m_qTb = ar.mark()
        qTb = ar.alloc("qTb", [128, 16, 128], BF16)
        m_wk = ar.mark()
        wk = ar.alloc("wk", [128, 16, 128], F32)
        m_cand = ar.mark()
        cand = ar.alloc("cand", [128, 8, 256], F32)
        ar_pj = Arena(nc, base=m_cand, limit=m_cand + 4096)
        ar_pj.n = 7300
        PJK = ["cand"] + ["cand_%d" % h for h in range(8)]
        pjunk = ar_pj.alloc("pjunk", [128, D], BF16)
        sv = ar.alloc("sv", [128, 16, 16], F32)
        si = ar.alloc("si", [128, 16, 16], U32)
        sif = ar.alloc("sif", [128, 16, 16], F32)
        fv = ar.alloc("fv", [128, 8, 16], F32)
        fp = ar.alloc("fp", [128, 8, 16], U32)
        fpf = ar.alloc("fpf", [128, 128], F32)
        fa = ar.alloc("fa", [128, 128], F32)
        fb = ar.alloc("fb", [128, 128], F32)
        ia = ar.alloc("ia", [128, 128], F32)
        ib = ar.alloc("ib", [128, 128], F32)
        eidx = ar.alloc("eidx", [128, 128], I32)
        gt = ar.alloc("gt", [128, 8, 16], F32)
        zs = ar.alloc("zs", [128, 16], F32)
        psm = ar.alloc("psm", [128, 8], F32)
        pacc = arA.alloc("pacc", [128, D], F32)
        NG = 8
        NS = 3
        op("dve", lambda e: e.memset(eidx[:], 0), writes=["eidx"])
        NG = 10
        gbuf = [ar.alloc("gbuf%d" % i, [128, 2 * D], BF16) for i in range(NG - 1)]
        ar_wk = Arena(nc, base=m_wk, limit=m_wk + 8192)
        ar_wk.n = 7000
        gbuf.append(ar_wk.alloc("gbuf_wk", [128, 2 * D], BF16))
        gkey = ["gbuf%d" % i for i in range(NG - 1)] + ["wk"]
        for nm, mk, cnt in ():
            ar_al = Arena(nc, base=mk, limit=mk + 4096 * cnt)
            ar_al.n = 7100 + len(gbuf)
            for _ in range(cnt):
                gbuf.append(ar_al.alloc("gbuf_" + nm, [128, 2 * D], BF16))
                gkey.append(nm)
        NG = len(gbuf)
        xal = []
        for t in range(NT):
            ar_x = Arena(nc, base=regA1 + t * 4096, limit=regA1 + (t + 1) * 4096)
            ar_x.n = 7400 + t
            xal.append(ar_x.alloc("gx", [128, 2 * D], BF16))
        dgb = [arA.alloc("dgb%d" % i, [128, 128], BF16) for i in range(8)]
        ssm = [arA.alloc("ssm%d" % i, [128, 16], F32) for i in range(NS)]
        gn = [0]
        sv4 = sv[:].rearrange("p (h two) a -> p h two a", two=2)
        sif4 = sif[:].rearrange("p (h two) a -> p h two a", two=2)
        cand4 = cand[:].rearrange("p h (a b) -> p h a b", b=16)
        wk4 = wk[:].rearrange("p c n -> p (c n)").rearrange("p (h a b) -> p h a b", h=8, b=16)
        cmpT = wk[:].rearrange("p c n -> p (c n)")[:, 0:1920].rearrange("p (s m) -> p s m", m=15)
        fa3 = fa[:].rearrange("p (h k) -> p h k", k=16)
        fb3 = fb[:].rearrange("p (h k) -> p h k", k=16)
        ia3 = ia[:].rearrange("p (h k) -> p h k", k=16)
        ib3 = ib[:].rearrange("p (h k) -> p h k", k=16)

        def front(i):
            xi = x1[:, i, :]
            xk = "x1_%d" % i
            h2b = h2bs[i % 2]
            h2bk = "h2b%d" % (i % 2)
            op("act", lambda e, xi=xi: e.activation(out=pjunk[:], in_=xi, func=AF.Square, accum_out=psm[:, 0:1]),
               reads=[xk], writes=PJK + ["psmA"])
            rstd_of(psm[:, 0:1], psm[:, 1:2], D, "psmA", "psmA")
            op("dve", lambda e, xi=xi: e.scalar_tensor_tensor(out=pacc[:], in0=xi, scalar=psm[:, 1:2], in1=G2b[:], op0=ALU.mult, op1=ALU.mult),
               reads=[xk, "psmA", "G2b"], writes=["pacc"])
            op("dve", lambda e: e.tensor_tensor(out=h2b[:], in0=pacc[:], in1=SH2b[:], op=ALU.add), reads=["pacc", "SH2b"], writes=[h2bk])
            pt = psT[0]
            for kc in range(KC):
                op("pe", lambda e, kc=kc, pt=pt: e.transpose(pt[:, kc * 128:(kc + 1) * 128], h2b[:, kc * 128:(kc + 1) * 128], ident_b[:]),
                   reads=[h2bk, "ident_b"], writes=["psT0"])
            op("act", lambda e, pt=pt: e.copy(out=h2T[:], in_=pt[:].rearrange("p (kc t) -> p kc t", kc=KC)),
               reads=["psT0"], writes=["h2T"])
            for cg in range(4):
                bank = 2 + cg % 2
                for cc in range(4):
                    c = cg * 4 + cc
                    for kc in range(KC):
                        op("pe", lambda e, c=c, cc=cc, kc=kc, bank=bank: e.matmul(
                            psF[bank][:, cc * 128:(cc + 1) * 128], lhsT=wq[:, kc, c * 128:(c + 1) * 128], rhs=h2T[:, kc, :],
                            start=(kc == 0), stop=(kc == KC - 1)),
                            reads=["wq", "h2T"], writes=["ps%d" % bank], signal=(kc == KC - 1))
                op("act", lambda e, cg=cg, bank=bank: e.copy(out=qTb[:, cg * 4:(cg + 1) * 4, :],
                                                             in_=psF[bank][:, :].rearrange("p (c t) -> p c t", c=4)),
                   reads=["ps%d" % bank], writes=["qTb%d" % cg])
            for cg in range(4):
                for cc in range(4):
                    c = cg * 4 + cc
                    op("pe", lambda e, c=c, cc=cc, cg=cg: e.matmul(psF[cg % 2][:, cc * 128:(cc + 1) * 128], lhsT=qTb[:, c, :], rhs=keysT[:, c, :],
                                                                 start=True, stop=True),
                       reads=["qTb%d" % cg, "keysT"], writes=["ps%d" % (cg % 2)])
                op("act", lambda e, cg=cg: e.copy(out=s_sb[:, cg * 4:(cg + 1) * 4, :], in_=psF[cg % 2][:, :].rearrange("p (c n) -> p c n", c=4)),
                   reads=["ps%d" % (cg % 2)], writes=["s_sb%d" % cg])
        def topk(i):
            SK = lambda c: "s_sb%d" % (c // 4)
            for c in range(16):
                op("dve", lambda e, c=c: e.max(out=sv[:, c, 0:8], in_=s_sb[:, c, :]), reads=[SK(c)], writes=["sva%d" % c])
            for c in range(16):
                op("dve", lambda e, c=c: e.max_index(out=si[:, c, 0:8], in_max=sv[:, c, 0:8], in_values=s_sb[:, c, :]),
                   reads=[SK(c), "sva%d" % c], writes=["sia%d" % c])
            for c in range(16):
                op("dve", lambda e, c=c: e.match_replace(out=wk[:, c, :], in_to_replace=sv[:, c, 0:8], in_values=s_sb[:, c, :], imm_value=-1e30),
                   reads=[SK(c), "sva%d" % c], writes=["wk"] if c == 0 else ["wk_%d" % c])
            for c in range(16):
                op("dve", lambda e, c=c: e.max(out=sv[:, c, 8:16], in_=wk[:, c, :]), reads=["wk"] if c == 0 else ["wk_%d" % c], writes=["svb%d" % c])
            for c in range(16):
                op("dve", lambda e, c=c: e.max_index(out=si[:, c, 8:16], in_max=sv[:, c, 8:16], in_values=wk[:, c, :]),
                   reads=(["wk"] if c == 0 else ["wk_%d" % c]) + ["svb%d" % c], writes=["sib%d" % c])
            ALLSV = ["sva%d" % c for c in range(16)] + ["svb%d" % c for c in range(16)]
            ALLSI = ["sia%d" % c for c in range(16)] + ["sib%d" % c for c in range(16)]
            ALLWK = ["wk"] + ["wk_%d" % c for c in range(1, 16)]
            op("dve", lambda e: e.tensor_copy(out=sif[:], in_=si[:]), reads=ALLSI, writes=["sif"])
            op("dve", lambda e: e.tensor_tensor(out=cand4, in0=sv4[:, :, 0, :].unsqueeze(3).to_broadcast([128, 8, 16, 16]),
                                                in1=sv4[:, :, 1, :].unsqueeze(2).to_broadcast([128, 8, 16, 16]), op=ALU.add),
               reads=ALLSV, writes=["cand"] + ["cand_%d" % h for h in range(8)])
            for h in range(8):
                op("dve", lambda e, h=h: e.max(out=fv[:, h, 0:8], in_=cand[:, h, :]), reads=["cand"], writes=["fva%d" % h])
            for h in range(8):
                op("dve", lambda e, h=h: e.max_index(out=fp[:, h, 0:8], in_max=fv[:, h, 0:8], in_values=cand[:, h, :]),
                   reads=["cand", "fva%d" % h], writes=["fpa%d" % h])
            for h in range(8):
                op("dve", lambda e, h=h: e.match_replace(out=cand[:, h, :], in_to_replace=fv[:, h, 0:8], in_values=cand[:, h, :], imm_value=-1e30),
                   reads=["cand", "fva%d" % h, "fpa%d" % h], writes=["cand_%d" % h])
            for h in range(8):
                op("dve", lambda e, h=h: e.max(out=fv[:, h, 8:16], in_=cand[:, h, :]), reads=["cand_%d" % h], writes=["fvb%d" % h])
            for h in range(8):
                op("dve", lambda e, h=h: e.max_index(out=fp[:, h, 8:16], in_max=fv[:, h, 8:16], in_values=cand[:, h, :]),
                   reads=["cand_%d" % h, "fvb%d" % h], writes=["fpb%d" % h])
            ALLFV = ["fva%d" % h for h in range(8)] + ["fvb%d" % h for h in range(8)]
            ALLFP = ["fpa%d" % h for h in range(8)] + ["fpb%d" % h for h in range(8)]
            ALLCAND = ["cand"] + ["cand_%d" % h for h in range(8)]
            op("dve", lambda e: e.tensor_copy(out=fpf[:], in_=fp[:].rearrange("p h k -> p (h k)")), reads=ALLFP, writes=["fpf"])
            op("dve", lambda e: e.tensor_tensor(out=cmpT, in0=fpf[:].unsqueeze(2).to_broadcast([128, 128, 15]),
                                                in1=thr15[:].unsqueeze(1).to_broadcast([128, 128, 15]), op=ALU.is_ge),
               reads=["fpf", "thr15"], writes=ALLWK)
            op("dve", lambda e: e.tensor_reduce(out=fa[:], in_=cmpT, axis=AX.X, op=ALU.add), reads=ALLWK, writes=["fa"])
            op("dve", lambda e: e.scalar_tensor_tensor(out=fb[:], in0=fa[:], scalar=-16.0, in1=fpf[:], op0=ALU.mult, op1=ALU.add),
               reads=["fa", "fpf"], writes=["fb"])
            for (fx3, half, dst, dkey) in ((fa3, 0, ia3, "ia"), (fb3, 1, ib3, "ib")):
                op("dve", lambda e, fx3=fx3: e.tensor_tensor(out=wk4, in0=fx3.unsqueeze(3).to_broadcast([128, 8, 16, 16]),
                                                           in1=iota16[:].unsqueeze(1).unsqueeze(1).to_broadcast([128, 8, 16, 16]), op=ALU.is_equal),
                   reads=["fa", "fb", "iota16"], writes=ALLWK)
                op("dve", lambda e, half=half: e.tensor_tensor(out=cand4, in0=wk4,
                                                             in1=sif4[:, :, half, :].unsqueeze(2).to_broadcast([128, 8, 16, 16]), op=ALU.mult),
                   reads=ALLWK + ["sif"], writes=ALLCAND)
                op("dve", lambda e, dst=dst: e.tensor_reduce(out=dst, in_=cand4, axis=AX.X, op=ALU.add), reads=ALLCAND, writes=[dkey])
            op("dve", lambda e: e.scalar_tensor_tensor(out=ia[:], in0=ia[:], scalar=128.0, in1=ib[:], op0=ALU.mult, op1=ALU.add),
               reads=["ia", "ib"], writes=["ia"])
            if "ia_all" in dbg:
                if i == 0:
                    ia_d = nc.dram_tensor("dbg_ia_all", [NT, 128, 128], F32, kind="ExternalOutput").ap()
                    fp_d = nc.dram_tensor("dbg_fp_all", [NT, 128, 128], F32, kind="ExternalOutput").ap()
                    sif_d = nc.dram_tensor("dbg_sif_all", [NT, 128, 256], F32, kind="ExternalOutput").ap()
                dma("sync", lambda e, i=i: e.dma_start(out=ia_d[i], in_=ia[:]), reads=["ia"])
                dma("sync", lambda e, i=i: e.dma_start(out=fp_d[i], in_=fpf[:]), reads=["fpf"])
                dma("sync", lambda e, i=i: e.dma_start(out=sif_d[i], in_=sif[:].rearrange("p c a -> p (c a)")), reads=["sif"])
            op("dve", lambda e: e.tensor_scalar(out=ia[:], in0=ia[:], scalar1=0.0, scalar2=16383.0, op0=ALU.max, op1=ALU.min),
               reads=["ia"], writes=["ia"])
            op("dve", lambda e: e.tensor_copy(out=eidx[:], in_=ia[:]), reads=["ia"], writes=["eidx"])
            op("dve", lambda e: e.tensor_tensor(out=gt[:], in0=fv[:], in1=fv[:, :, 0:1].to_broadcast([128, 8, 16]), op=ALU.subtract),
               reads=ALLFV, writes=["gt"])
            op("act", lambda e: e.activation(out=gt[:], in_=gt[:], func=AF.Exp), reads=["gt"], writes=["gt"])
            op("dve", lambda e: e.tensor_reduce(out=zs[:, 0:8], in_=gt[:], axis=AX.X, op=ALU.add), reads=["gt"], writes=["zs"])
            op("dve", lambda e: e.reciprocal(out=zs[:, 8:16], in_=zs[:, 0:8]), reads=["zs"], writes=["zs"])
            op("dve", lambda e: e.tensor_tensor(out=gt[:], in0=gt[:], in1=zs[:, 8:16].unsqueeze(2).to_broadcast([128, 8, 16]), op=ALU.mult),
               reads=["gt", "zs"], writes=["gt"])
            if i == 0 and "eidx" in dbg:
                dbg_out("eidxf", ia[:], [128, 128], "ia")
                dbg_out("gates", gt[:].rearrange("p h k -> p (h k)"), [128, 128], "gt")
        def loop(i):
            h2b = h2bs[i % 2]
            h2bk = "h2b%d" % (i % 2)
            gflat = gt[:].rearrange("p h k -> p (h k)")

            def alias_keys(kname):
                if kname in ("s_sb", "qTb"):
                    return [kname] + ["%s%d" % (kname, c) for c in range(4)]
                if kname == "wk":
                    return ["wk"] + ["wk_%d" % c for c in range(1, 16)]
                return [kname]

            def stage_b(grp, gl):
                k = grp % NS
                sk2 = "ssm%d" % k
                op("dve", lambda e: e.tensor_tensor(out=ssm[k][:, 12:16], in0=ssm[k][:, 8:12], in1=gflat[:, grp * 4:grp * 4 + 4], op=ALU.mult),
                   reads=[sk2, "gt"], writes=[sk2])
                for j in range(4):
                    slot = grp * 4 + j
                    g = gl[j]
                    d = (grp % 2) * 4 + j
                    dk, gk = "dgb%d" % d, ring_k[g]
                    op("act", lambda e, d=d, j=j: e.activation(out=dgb[d][:], in_=ident_b[:], func=AF.Copy, scale=ssm[k][:, 12 + j:13 + j]),
                       reads=[sk2, "ident_b"], writes=[dk])
                    for hb in range(2):
                        op("pe", lambda e, g=g, d=d, hb=hb, slot=slot: e.matmul(psF[4 + hb][:, :], lhsT=dgb[d][:], rhs=ring_b[g][:, D + hb * 512:D + (hb + 1) * 512],
                                                                              start=(slot == 0), stop=(slot == 127)),
                           reads=[dk] + alias_keys(gk), writes=["ps%d" % (4 + hb)])

            ring_b = gbuf + xal[0:i]
            ring_k = gkey + ["x1_%d" % t for t in range(i)]
            nring = len(ring_b)
            gn[0] = 0

            def gathers(grp):
                gl = []
                for j in range(4):
                    slot = grp * 4 + j
                    g = gn[0] % nring
                    gn[0] += 1
                    gl.append(g)
                    dma("pool", lambda e, g=g, slot=slot: e.indirect_dma_start(
                        out=ring_b[g][:], out_offset=None, in_=uvb_d, in_offset=bass.IndirectOffsetOnAxis(ap=eidx[:, slot:slot + 1], axis=0)),
                        reads=["eidx", "gt"], writes=alias_keys(ring_k[g]))
                return gl

            gls = {}
            for grp in range(32):
                k = grp % NS
                sk2 = "ssm%d" % k
                gls[grp] = gathers(grp)
                if grp >= 1:
                    stage_b(grp - 1, gls[grp - 1])
                gl = gls[grp]
                for j in range(4):
                    g = gl[j]
                    op("dve", lambda e, g=g, k=k, j=j: e.scalar_tensor_tensor(out=pjunk[:], in0=ring_b[g][:, 0:D], scalar=1.0, in1=h2b[:], op0=ALU.mult, op1=ALU.mult,
                                                                            accum_out=ssm[k][:, j:j + 1]),
                       reads=alias_keys(ring_k[g]) + [h2bk], writes=PJK + [sk2])
                op("dve", lambda e, k=k: e.tensor_copy(out=ssm[k][:, 4:8], in_=ssm[k][:, 0:4]), reads=[sk2], writes=[sk2])
                op("act", lambda e, k=k: e.activation(out=ssm[k][:, 8:12], in_=ssm[k][:, 4:8], func=AF.Gelu), reads=[sk2], writes=[sk2])
            stage_b(31, gls[31])
        def epilogue(i):
            xi = x1[:, i, :]
            xk = "x1_%d" % i
            for hb in range(2):
                op("dve", lambda e, hb=hb: e.tensor_tensor(out=pacc[:, hb * 512:(hb + 1) * 512], in0=psF[4 + hb][:, :], in1=GT2b[:, hb * 512:(hb + 1) * 512], op=ALU.mult),
                   reads=["ps%d" % (4 + hb), "GT2b"], writes=["pacc"])
            op("dve", lambda e, xi=xi: e.tensor_tensor(out=pacc[:], in0=pacc[:], in1=xi, op=ALU.add), reads=["pacc", xk], writes=["pacc"])
            op("act", lambda e: e.activation(out=pjunk[:], in_=pacc[:], func=AF.Square, accum_out=psm[:, 2:3]),
               reads=["pacc"], writes=PJK + ["psmB"])
            rstd_of(psm[:, 2:3], psm[:, 3:4], D, "psmB", "psmB")
            op("dve", lambda e: e.scalar_tensor_tensor(out=pacc[:], in0=pacc[:], scalar=psm[:, 3:4], in1=FGb[:], op0=ALU.mult, op1=ALU.mult),
               reads=["pacc", "psmB", "FGb"], writes=["pacc"])
            dma("sync", lambda e, i=i: e.dma_start(out=out_d[i * 128:(i + 1) * 128, :], in_=pacc[:]), reads=["pacc"])

        ntile = 1 if stage == 6 else min(NT, peer_tiles)
        front(0)
        topk(0)
        for i in range(ntile):
            if i + 1 < ntile:
                front(i + 1)
            loop(i)
            epilogue(i)
            if i + 1 < ntile:
                topk(i + 1)
        fw.wait_all("sync")
    return nc, dbg_d


def _tag(a, b):
    pad = np.full((1,) + a.shape[1:], float(b), dtype=a.dtype)
    return np.ascontiguousarray(np.concatenate([a, pad], axis=0))


def _prep_inputs(inp, b):
    f = lambda a: np.ascontiguousarray(a, dtype=np.float32)
    sel4 = np.zeros((4, 512), np.float32)
    for h in range(4):
        sel4[h, h * 128:(h + 1) * 128] = 1.0
    m = {
        "x": f(inp["x"][b]),
        "cT": f(inp["c"][b].reshape(KC, 128).T),
        "ada_w": f(inp["ada_w"][0]),
        "ada_bT": f(inp["ada_b"][0].reshape(48, 128).T),
        "n1gT": f(inp["norm1_g"][0].reshape(KC, 128).T),
        "n2gT": f(inp["norm2_g"][0].reshape(KC, 128).T),
        "final_g": f(inp["final_g"].reshape(1, D)),
        "w_in": _tag(f(inp["w_in"][0]), b),
        "conv_wT": f(inp["conv_w"][0].reshape(4, 8, 128).transpose(2, 1, 0)),
        "conv_bT": f(inp["conv_b"][0].reshape(8, 128).T),
        "gate_b": f(inp["mlstm_gate_b"][0].reshape(2, 4).T),
        "mnorm_g": f(inp["mlstm_norm_g"][0].reshape(1, 512)),
        "lamv": f(np.concatenate([inp["lambda_q1"][0], inp["lambda_k1"][0], inp["lambda_q2"][0], inp["lambda_k2"][0]]).reshape(1, 256)),
        "dnorm_g": f(inp["diff_norm_g"][0].reshape(1, 128)),
        "w_out": _tag(f(inp["w_out"][0]), b),
        "w_query": _tag(f(inp["peer_w_query"][0]), b),
        "keysT": _tag(f(inp["peer_sub_keys"][0].reshape(16, 128, 128).transpose(2, 0, 1)), b),
        "peer_uv": _tag(np.concatenate([f(inp["peer_u"][0]), f(inp["peer_v"][0])], axis=1), b),
        "ident": np.eye(128, dtype=np.float32),
        "tri": np.triu(np.ones((128, 128), np.float32)),
        "sel4": sel4,
        "iota16": np.arange(16, dtype=np.float32).reshape(1, 16),
    }
    return m


def kernel(**inputs):
    nc, _ = build()
    shared = None
    in_maps = []
    for b in range(8):
        m = _prep_inputs(inputs, b)
        if shared is None:
            shared = m
        else:
            for k in m:
                if k not in ("x", "cT") and k not in TAGGED:
                    m[k] = shared[k]
        in_maps.append(m)
    res = run_bass_kernel_spmd(nc, in_maps, core_ids=list(range(8)))
    out = np.stack([np.asarray(r["out"], dtype=np.float32) for r in res.results], axis=0)
    return out.reshape(8, S, D)
```
